# Optimizing a Trainium2 kernel written in Bass

```python
import math
import jax, jax.numpy as jnp
from jax import lax
import numpy as np

D_MODEL = 1024
BATCH = 16
SEQ = 2048
DEPTH = 1

MIX_WIDTH = D_MODEL
POOL_WIDTH = MIX_WIDTH // 2
POOL_GROUPS = 4
POOL_GROUP_DIM = POOL_WIDTH // POOL_GROUPS
POOL_WINDOWS = (2, 4, 8, 16)
ATTN_WIDTH = MIX_WIDTH - POOL_WIDTH
HEAD_DIM = 64
N_HEADS = ATTN_WIDTH // HEAD_DIM
D_FF = 2816
Q_BLOCK = 128
IN_COLS = POOL_WIDTH + 3 * ATTN_WIDTH + N_HEADS
EPS = 1e-6

kernel_name = "hymba_pool_fox_macaron_block"


def rmsnorm(x, g):
    xf = x.astype(jnp.float32)
    y = xf * lax.rsqrt(jnp.mean(xf * xf, axis=-1, keepdims=True) + EPS)
    return (y * g.astype(jnp.float32)).astype(x.dtype)


def swiglu(h, w_gate, w_up, w_down):
    return (jax.nn.silu(h @ w_gate) * (h @ w_up)) @ w_down


def causal_window_mean(v, w):
    B, S, C = v.shape
    vf = v.astype(jnp.float32)
    cs = jnp.cumsum(vf, axis=1)
    shifted = jnp.concatenate([jnp.zeros((B, w, C), jnp.float32), cs[:, : S - w]], axis=1)
    count = jnp.minimum(jnp.arange(1, S + 1, dtype=jnp.float32), float(w))
    return ((cs - shifted) / count[None, :, None]).astype(v.dtype)


def pool_mixer(pv, pool_w, pool_scale):
    B, S, _ = pv.shape
    groups = pv.reshape(B, S, POOL_GROUPS, POOL_GROUP_DIM)
    pooled = jnp.stack(
        [causal_window_mean(groups[:, :, g], POOL_WINDOWS[g]) for g in range(POOL_GROUPS)], axis=2
    ) - groups
    mixed = jnp.einsum("bsgc,gcd->bsgd", pooled, pool_w)
    return mixed.reshape(B, S, POOL_WIDTH) * pool_scale


def forgetting_attention(q, k, v, f_logit, b_forget, q_norm, k_norm):
    B, S, H, Dh = q.shape
    q = rmsnorm(q, q_norm).transpose(0, 2, 1, 3)
    k = rmsnorm(k, k_norm).transpose(0, 2, 1, 3)
    v = v.transpose(0, 2, 1, 3)
    log_f = jax.nn.log_sigmoid((f_logit + b_forget).astype(jnp.float32))
    F = jnp.cumsum(log_f, axis=1).transpose(0, 2, 1)
    scale = 1.0 / math.sqrt(Dh)
    outs = []
    for i in range(S // Q_BLOCK):
        q0, end = i * Q_BLOCK, (i + 1) * Q_BLOCK
        qb = q[:, :, q0:end]
        kb, vb = k[:, :, :end], v[:, :, :end]
        logits = jnp.einsum("bhqd,bhkd->bhqk", qb, kb).astype(jnp.float32) * scale
        logits = logits + F[:, :, q0:end, None] - F[:, :, None, :end]
        q_pos = jnp.arange(q0, end)[:, None]
        k_pos = jnp.arange(end)[None, :]
        logits = jnp.where(q_pos >= k_pos, logits, -jnp.inf)
        p = jax.nn.softmax(logits, axis=-1).astype(vb.dtype)
        outs.append(jnp.einsum("bhqk,bhkd->bhqd", p, vb))
    o = jnp.concatenate(outs, axis=2)
    return o.transpose(0, 2, 1, 3).reshape(B, S, H * Dh)


def setup_inputs(seed: int = 0) -> dict:
    key = jax.random.key(seed)
    ks = jax.random.split(key, 24)
    f32 = jnp.float32

    def nrm(k, shape, fan_in):
        return jax.random.normal(k, shape, f32) * fan_in ** -0.5

    def gain(k, shape):
        return 1.0 + 0.02 * jax.random.normal(k, shape, f32)

    return {
        "x": jax.random.normal(ks[0], (BATCH, SEQ, D_MODEL), f32),
        "ffn1_norm": gain(ks[1], (D_MODEL,)),
        "ffn1_w_gate": nrm(ks[2], (D_MODEL, D_FF), D_MODEL),
        "ffn1_w_up": nrm(ks[3], (D_MODEL, D_FF), D_MODEL),
        "ffn1_w_down": nrm(ks[4], (D_FF, D_MODEL), D_FF),
        "mix_norm": gain(ks[5], (D_MODEL,)),
        "w_in": nrm(ks[6], (D_MODEL, IN_COLS), D_MODEL),
        "b_forget": jax.random.uniform(ks[7], (N_HEADS,), f32, 1.0, 4.0),
        "pool_w": nrm(ks[8], (POOL_GROUPS, POOL_GROUP_DIM, POOL_GROUP_DIM), POOL_GROUP_DIM),
        "pool_scale": gain(ks[9], (POOL_WIDTH,)),
        "q_norm": gain(ks[10], (HEAD_DIM,)),
        "k_norm": gain(ks[11], (HEAD_DIM,)),
        "out_norm_pool": gain(ks[12], (POOL_WIDTH,)),
        "out_norm_attn": gain(ks[13], (ATTN_WIDTH,)),
        "w_out": nrm(ks[14], (MIX_WIDTH, D_MODEL), MIX_WIDTH),
        "ffn2_norm": gain(ks[15], (D_MODEL,)),
        "ffn2_w_gate": nrm(ks[16], (D_MODEL, D_FF), D_MODEL),
        "ffn2_w_up": nrm(ks[17], (D_MODEL, D_FF), D_MODEL),
        "ffn2_w_down": nrm(ks[18], (D_FF, D_MODEL), D_FF),
    }


def reference(x, ffn1_norm, ffn1_w_gate, ffn1_w_up, ffn1_w_down, mix_norm, w_in, b_forget,
              pool_w, pool_scale, q_norm, k_norm, out_norm_pool, out_norm_attn, w_out,
              ffn2_norm, ffn2_w_gate, ffn2_w_up, ffn2_w_down):
    B, S, _ = x.shape
    for _layer in range(DEPTH):
        x = x + 0.5 * swiglu(rmsnorm(x, ffn1_norm), ffn1_w_gate, ffn1_w_up, ffn1_w_down)

        h = rmsnorm(x, mix_norm) @ w_in
        c0 = POOL_WIDTH
        pv = h[..., :c0]
        q = h[..., c0:c0 + ATTN_WIDTH].reshape(B, S, N_HEADS, HEAD_DIM)
        k = h[..., c0 + ATTN_WIDTH:c0 + 2 * ATTN_WIDTH].reshape(B, S, N_HEADS, HEAD_DIM)
        v = h[..., c0 + 2 * ATTN_WIDTH:c0 + 3 * ATTN_WIDTH].reshape(B, S, N_HEADS, HEAD_DIM)
        f_logit = h[..., c0 + 3 * ATTN_WIDTH:]

        y_pool = rmsnorm(pool_mixer(pv, pool_w, pool_scale), out_norm_pool)
        y_attn = rmsnorm(forgetting_attention(q, k, v, f_logit, b_forget, q_norm, k_norm), out_norm_attn)
        x = x + jnp.concatenate([y_pool, y_attn], axis=-1) @ w_out

        x = x + 0.5 * swiglu(rmsnorm(x, ffn2_norm), ffn2_w_gate, ffn2_w_up, ffn2_w_down)
    return x
```

```python
import contextlib
import numpy as np
import concourse.bass as bass
import concourse.mybir as mybir
from concourse.bass_utils import run_bass_kernel_spmd

F32 = mybir.dt.float32
BF16 = mybir.dt.bfloat16
ALU = mybir.AluOpType
AF = mybir.ActivationFunctionType

NCORES = 8
D = 1024
S = 2048
DFF = 2816
NKC = 8
TT = 512
NTT = 4
FPH = 11
EPS = 1e-6
NSEQ = 2


class Tracker:
    def __init__(self, nc, es):
        self.nc = nc
        self.eng = {"pe": nc.tensor, "act": nc.scalar, "dve": nc.vector,
                    "pool": nc.gpsimd, "sp": nc.sync}
        self.semobj = {}
        self.val = {}
        for k in self.eng:
            self.semobj["e_" + k] = es.enter_context(nc.semaphore("s_" + k))
            self.val["e_" + k] = 0
        self.es = es
        self.known = {}
        self.lastw = {}
        self.readers = {}

    def add_dma_sem(self, name):
        self.semobj[name] = self.es.enter_context(self.nc.semaphore("d_" + name))
        self.val[name] = 0

    def _deps(self, reads, writes):
        d = {}

        def add(tok):
            if tok is None:
                return
            s, v = tok
            if d.get(s, 0) < v:
                d[s] = v
        for r in reads:
            add(self.lastw.get(r))
        for w in writes:
            add(self.lastw.get(w))
            for s, v in self.readers.get(w, {}).items():
                add((s, v))
        return d

    def _wait(self, e, d):
        for s, v in d.items():
            if e == "pe" and s == "e_pe":
                continue
            if self.known.get((e, s), 0) >= v:
                continue
            self.eng[e].wait_ge(self.semobj[s], v)
            self.known[(e, s)] = v

    def _commit(self, tok, reads, writes):
        s, v = tok
        for r in reads:
            rd = self.readers.setdefault(r, {})
            if rd.get(s, 0) < v:
                rd[s] = v
        for w in writes:
            self.lastw[w] = tok
            self.readers[w] = {}

    def op(self, e, reads, writes, emit):
        self._wait(e, self._deps(reads, writes))
        ins = emit(self.eng[e])
        s = "e_" + e
        self.val[s] += 1
        ins.then_inc(self.semobj[s], 1)
        self._commit((s, self.val[s]), reads, writes)

    def dma(self, e, semname, reads, writes, emit):
        self._wait(e, self._deps(reads, writes))
        ins = emit(self.eng[e])
        self.val[semname] += 16
        ins.then_inc(self.semobj[semname], 16)
        self._commit((semname, self.val[semname]), reads, writes)

    def barrier(self, engines=None, all_sems=False):
        d = {s: v for s, v in self.val.items()
             if v > 0 and (all_sems or s.startswith("e_") or s in ("winv", "cstp", "cst0", "cst2"))}
        for e in (engines or self.eng):
            self._wait(e, d)


class Ring:
    def __init__(self, tr, name, tile, nslots, slotw, items):
        self.tr, self.name, self.tile = tr, name, tile
        self.nslots, self.slotw, self.items = nslots, slotw, items
        self.n_issued = self.n_acq = self.n_rel = 0
        for s in range(nslots):
            tr.add_dma_sem(f"{name}{s}")

    def prefetch(self):
        while self.n_issued < len(self.items) and self.n_issued < self.n_rel + self.nslots:
            i = self.n_issued
            slot = i % self.nslots
            src = self.items[i]
            w = src.shape[-1]
            dst = self.tile[:, slot * self.slotw: slot * self.slotw + w]
            self.tr.dma("pool", f"{self.name}{slot}", [], [(self.name, slot)],
                        lambda g, dst=dst, src=src: g.dma_start(out=dst, in_=src))
            self.n_issued += 1

    def acquire(self):
        self.prefetch()
        i = self.n_acq
        assert self.n_issued > i, (self.name, i, self.n_issued, self.n_rel)
        self.n_acq += 1
        slot = i % self.nslots
        return self.tile[:, slot * self.slotw:(slot + 1) * self.slotw], (self.name, slot)

    def release(self):
        self.n_rel += 1
        self.prefetch()


def build_nc():
    nc = bass.Bass("TRN2", target_bir_lowering=False)

    def din(name, shape):
        return nc.dram_tensor(name, shape, F32, kind="ExternalInput").ap()

    xT = din("xT", [NSEQ, NKC, 128, S])
    wgu = [din("wgu1", [22, 128, 2048]), din("wgu2", [22, 128, 2048])]
    wd = [din("wd1", [16, 128, FPH * 128]), din("wd2", [16, 128, FPH * 128])]
    win = din("win", [6, 128, 2048])
    winv_d = din("winv", [128, NKC, 520])
    wout = din("wout", [4, 128, 2048])
    poolw_d = din("poolw", [128, 512])
    pvec_d = din("pvec", [128, 40])
    bfb_d = din("bfb", [128, 128])
    yT = nc.dram_tensor("yT", [NSEQ, NKC, 128, S], F32, kind="ExternalOutput").ap()

    es = contextlib.ExitStack()
    with es:
        def sb(name, shape, dt):
            return es.enter_context(nc.sbuf_tensor("sb_" + name, shape, dt))
        uid = [0]

        tr = Tracker(nc, es)
        for nm in [f"xld{k}" for k in range(8)] + [f"yst{k}" for k in range(8)] + ["cst0", "cst2", "cstp", "winv"]:
            tr.add_dma_sem(nm)

        x_sb = sb("x_sb", [128, NKC * S], F32)
        h_sb = sb("h_sb", [128, NKC * S], BF16)
        rA_t = sb("rA", [128, 2 * 2048], BF16)
        rB_t = sb("rB", [128, 2 * FPH * 128], BF16)
        sq_t = sb("sq", [128, 4 * TT], BF16)
        rstd_t = sb("rstd", [128, 2 * TT], F32)
        pvec = sb("pvec", [128, 40], F32)
        ones_bf = sb("ones_bf", [128, 128], BF16)
        bd_bf = sb("bd_bf", [128, 128], BF16)
        ident_bf = sb("ident_bf", [128, 128], BF16)
        mneg_bf = sb("mneg_bf", [128, 128], BF16)
        zero_bf = sb("zero_bf", [128, 128], BF16)
        U_f = sb("U_f", [128, 128], F32)
        ones_f = sb("ones_f", [128, 128], F32)
        H_f = sb("H_f", [128, 128], F32)
        bfb = sb("bfb", [128, 128], F32)
        poolw = sb("poolw", [128, 512], BF16)
        invc = sb("invc", [128, 16], F32)
        gq8 = sb("gq8", [128, 1], F32)

        ps = [es.enter_context(nc.psum_tensor(f"ps{b}", [128, 512], F32)) for b in range(8)]

        def xs(kc, tt):
            return x_sb[:, kc * S + tt * TT: kc * S + (tt + 1) * TT]

        def hs(kc, tt):
            return h_sb[:, kc * S + tt * TT: kc * S + (tt + 1) * TT]

        def hs_tok(kc, t0, n):
            return h_sb[:, kc * S + t0: kc * S + t0 + n]

        V = nc.vector
        G = nc.gpsimd
        tr.dma("sp", "cst0", [], ["c0"], lambda e: e.dma_start(out=pvec[:], in_=pvec_d))
        tr.dma("sp", "cst2", [], ["c2"], lambda e: e.dma_start(out=bfb[:], in_=bfb_d))
        tr.dma("pool", "cstp", [], ["c3"], lambda e: e.dma_start(out=poolw[:], in_=poolw_d))
        tr.op("dve", [], ["k0"], lambda e: e.memset(ones_bf[:], 1.0))
        tr.op("dve", [], ["k1"], lambda e: e.memset(ones_f[:], 1.0))
        tr.op("dve", [], ["k2"], lambda e: e.memset(bd_bf[:], 0.0))
        tr.op("dve", ["k2"], ["k2"], lambda e: e.memset(bd_bf[0:64, 0:64], 1.0))
        tr.op("dve", ["k2"], ["k2"], lambda e: e.memset(bd_bf[64:128, 64:128], 1.0))
        tr.op("dve", [], ["k3"], lambda e: e.memset(H_f[:], 0.0))
        tr.op("dve", ["k3"], ["k3"], lambda e: e.memset(H_f[0:64, :], 1.0))
        tr.op("dve", [], ["k11"], lambda e: e.memset(zero_bf[:], 0.0))
        tr.op("pool", ["k0"], ["k4"], lambda e: e.affine_select(
            out=ident_bf[:], in_=ones_bf[:], pattern=[[-1, 128]], compare_op=ALU.is_equal,
            fill=0.0, base=0, channel_multiplier=1))
        tr.op("pool", ["k11"], ["k5"], lambda e: e.affine_select(
            out=mneg_bf[:], in_=zero_bf[:], pattern=[[1, 128]], compare_op=ALU.is_ge,
            fill=-30000.0, base=0, channel_multiplier=-1))
        tr.op("pool", ["k1"], ["k6"], lambda e: e.affine_select(
            out=U_f[:], in_=ones_f[:], pattern=[[1, 128]], compare_op=ALU.is_ge,
            fill=0.0, base=0, channel_multiplier=-1))
        tr.op("pool", [], ["k7"], lambda e: e.iota(
            invc[:], pattern=[[1, 16]], base=1, channel_multiplier=0,
            allow_small_or_imprecise_dtypes=True))
        tr.op("dve", ["k7"], ["k7"], lambda e: e.reciprocal(out=invc[:], in_=invc[:]))
        tr.op("dve", ["c0"], ["k8"], lambda e: e.tensor_scalar(
            out=gq8[:], in0=pvec[:, 32:33], scalar1=0.125, scalar2=None, op0=ALU.mult))
        tr.barrier()

        itemsA, itemsB = [], []
        for s in range(NSEQ):
            for f in range(2):
                if f == 1:
                    for i in range(6):
                        itemsA.append(win[i])
                    for i in range(4):
                        itemsA.append(wout[i])
                for ffc in range(22):
                    itemsA.append(wgu[f][ffc])
                for i in range(16):
                    itemsB.append(wd[f][i])
        ringA = Ring(tr, "rA", rA_t, 2, 2048, itemsA)
        ringB = Ring(tr, "rB", rB_t, 2, FPH * 128, itemsB)

        cnt = {"sq": 0, "rstd": 0, "ab": 0, "bb": 0, "sg": 0, "pb": 0, "mb": 0, "yb": 0, "pl": 0}

        def nxt(k, mod):
            v = cnt[k] % mod
            cnt[k] += 1
            return v

        def rmsnorm_x(gcol, tiles=range(NTT)):
            for tt in tiles:
                for kc in range(NKC):
                    sl = nxt("sq", 4)
                    sqs = sq_t[:, sl * TT:(sl + 1) * TT]
                    tr.op("act", [("x", kc, tt)], [("sq", sl)],
                          lambda e, sqs=sqs, kc=kc, tt=tt: e.activation(
                              out=sqs, in_=xs(kc, tt), func=AF.Square))
                    tr.op("pe", [("sq", sl)], [("ps", 6)],
                          lambda e, sqs=sqs, kc=kc: e.matmul(
                              ps[6][:], lhsT=ones_bf[:], rhs=sqs,
                              start=(kc == 0), stop=(kc == NKC - 1)))
                r = nxt("rstd", 2)
                rs_ = rstd_t[:, r * TT:(r + 1) * TT]
                tr.op("act", [("ps", 6)], [("rstd", r)],
                      lambda e, rs_=rs_: e.activation(out=rs_, in_=ps[6][:], func=AF.Ln,
                                                      scale=1.0 / D, bias=eps_t[:, 0:1]))
                tr.op("act", [("rstd", r)], [("rstd", r)],
                      lambda e, rs_=rs_: e.activation(out=rs_, in_=rs_, func=AF.Exp, scale=-0.5))
                for kc in range(NKC):
                    tr.op("dve", [("x", kc, tt), ("rstd", r)], [("h", kc, tt)],
                          lambda e, kc=kc, tt=tt, rs_=rs_: e.scalar_tensor_tensor(
                              out=hs(kc, tt), in0=xs(kc, tt),
                              scalar=pvec[:, gcol + kc: gcol + kc + 1], in1=rs_,
                              op0=ALU.mult, op1=ALU.mult))

        eps_t = sb("eps_t", [128, 1], F32)
        tr.op("dve", [], ["k9"], lambda e: e.memset(eps_t[:], EPS))
        one_t = sb("one_t", [128, 1], F32)
        tr.op("dve", [], ["k10"], lambda e: e.memset(one_t[:], 1.0))
        tr.barrier()

        def ffn(gcol, after_chunk=None, prenormed=False):
            uid[0] += 1
            with nc.sbuf_tensor(f"act_t{uid[0]}", [128, FPH * S], BF16) as act_t, \
                    nc.sbuf_tensor(f"sg_t{uid[0]}", [128, 2 * TT], F32) as sg_t:
                def acts(fl, tt):
                    return act_t[:, fl * S + tt * TT: fl * S + (tt + 1) * TT]

                ringB.prefetch()
                def stage_a(fl, tt, wap, wkey):
                    pb = nxt("ab", 2) * 2
                    hreads = [("h", kc, tt) for kc in range(NKC)]

                    def emit_mm(e, off, bank):
                        ins = None
                        for kc in range(NKC):
                            ins = e.matmul(
                                ps[bank][:], lhsT=wap[:, off + kc * 128: off + (kc + 1) * 128],
                                rhs=hs(kc, tt), start=(kc == 0), stop=(kc == NKC - 1))
                        return ins
                    tr.op("pe", [wkey] + hreads, [("ps", pb)], lambda e: emit_mm(e, 0, pb))
                    tr.op("pe", [wkey] + hreads, [("ps", pb + 1)], lambda e: emit_mm(e, 1024, pb + 1))
                    s_ = nxt("sg", 2)
                    sgs = sg_t[:, s_ * TT:(s_ + 1) * TT]
                    tr.op("act", [("ps", pb)], [("sg", s_)],
                          lambda e: e.activation(out=sgs, in_=ps[pb][:], func=AF.Silu))
                    tr.op("dve", [("sg", s_), ("ps", pb + 1)], [("act", fl, tt)],
                          lambda e: e.tensor_tensor(
                              out=acts(fl, tt), in0=sgs, in1=ps[pb + 1][:], op=ALU.mult))

                if not prenormed:
                    rmsnorm_x(gcol)
                for half in range(2):
                    for fl in range(FPH):
                        wap, wkey = ringA.acquire()
                        for tt in range(NTT):
                            stage_a(fl, tt, wap, wkey)
                        ringA.release()
                    for dc in range(NKC):
                        wdap, wdkey = ringB.acquire()
                        for tt in range(NTT):
                            pb = 4 + nxt("bb", 2)

                            def emit_d(e, pb=pb, tt=tt, dc=dc, wdap=wdap):
                                ins = None
                                for fl in range(FPH):
                                    ins = e.matmul(
                                        ps[pb][:], lhsT=wdap[:, fl * 128:(fl + 1) * 128],
                                        rhs=acts(fl, tt), start=(fl == 0), stop=(fl == FPH - 1))
                                return ins
                            tr.op("pe", [wdkey] + [("act", fl, tt) for fl in range(FPH)],
                                  [("ps", pb)], emit_d)
                            tr.op("dve", [("ps", pb), ("x", dc, tt)], [("x", dc, tt)],
                                  lambda e, pb=pb, dc=dc, tt=tt: e.scalar_tensor_tensor(
                                      out=xs(dc, tt), in0=ps[pb][:], scalar=0.5, in1=xs(dc, tt),
                                      op0=ALU.mult, op1=ALU.add))
                        ringB.release()
                        if half == 1 and after_chunk is not None:
                            after_chunk(dc)
                tr.barrier()

        def mixer():
            mes = contextlib.ExitStack()
            uid[0] += 1
            with mes:
                def msb(name, shape, dt):
                    return mes.enter_context(nc.sbuf_tensor(f"m{uid[0]}_{name}", shape, dt))
                qn = msb("qn", [128, 4 * S], BF16)
                kn = msb("kn", [128, 4 * S], BF16)
                Vt = msb("Vt", [128, 16 * 768], BF16)
                ovl = msb("ovl", [128, 2080], F32)
                ovl2 = msb("ovl2", [128, 2112], F32)
                BT = msb("BT", [128, 2 * 128], F32)
                PT = msb("PT", [128, 3 * TT], BF16)
                pooled = PT
                fzp = msb("fzp", [128, 2 * 128], F32)
                rc = msb("rc", [128, 2 * TT], F32)
                halo = msb("halo", [128, 64], F32)
                qz = msb("qz", [128, 2 * 1024], BF16)
                fix = msb("fix", [128, 16], F32)

                fence_t = msb("fence_t", [128, 2], F32)

                def alias_fence(old_keys, new_keys):
                    tr.op("dve", list(old_keys) + ["fence"], list(old_keys) + list(new_keys) + ["fence"],
                          lambda e: e.memset(fence_t[:, 0:1], 0.0))

                winv = ovl[:, :].bitcast(BF16)
                mixed = ovl
                pvg = ovl2
                V5 = Vt[:, :].rearrange("p (j c s e) -> p j c s e", j=16, c=4, s=3, e=64)

                def og(il):
                    return ovl2[:, il * 512:(il + 1) * 512]

                for kc in range(NKC):
                    tr.dma("pool", "winv", [], ["winv"],
                           lambda g, kc=kc: g.dma_start(out=winv[:, kc * 520:(kc + 1) * 520],
                                                        in_=winv_d[:, kc, :]))
                tr.op("dve", [], ["Vones"], lambda e: e.memset(V5[:, :, :, 1, :], 1.0))

                rmsnorm_x(8)

                def qk_stage_b(pb, sl, dst_t, dkey, gap, c, tt):
                    sqs = sq_t[:, sl * TT:(sl + 1) * TT]
                    mb = 4 + nxt("mb", 2)
                    tr.op("pe", [("sq", sl)], [("ps", mb)],
                          lambda e: e.matmul(ps[mb][:], lhsT=bd_bf[:], rhs=sqs, start=True, stop=True))
                    r = nxt("rstd", 2)
                    rs_ = rstd_t[:, r * TT:(r + 1) * TT]
                    tr.op("act", [("ps", mb)], [("rstd", r)],
                          lambda e: e.activation(out=rs_, in_=ps[mb][:], func=AF.Ln, scale=1.0 / 64,
                                                 bias=eps_t[:, 0:1]))
                    tr.op("act", [("rstd", r)], [("rstd", r)],
                          lambda e: e.activation(out=rs_, in_=rs_, func=AF.Exp, scale=-0.5))
                    tr.op("dve", [("ps", pb), ("rstd", r)], [(dkey, c, tt)],
                          lambda e: e.scalar_tensor_tensor(
                              out=dst_t[:, c * S + tt * TT: c * S + (tt + 1) * TT],
                              in0=ps[pb][:], scalar=gap, in1=rs_, op0=ALU.mult, op1=ALU.mult))

                pend = None
                for it in range(4):
                    wap, wkey = ringA.acquire()
                    for c2 in range(2):
                        c = (it % 2) * 2 + c2
                        isq = it < 2
                        dst_t = qn if isq else kn
                        dkey = "qn" if isq else "kn"
                        gap = gq8[:, 0:1] if isq else pvec[:, 33:34]
                        for tt in range(NTT):
                            pb = nxt("pb", 4)

                            def emit_p(e, pb=pb, tt=tt, c2=c2, wap=wap):
                                ins = None
                                for kc in range(NKC):
                                    ins = e.matmul(
                                        ps[pb][:],
                                        lhsT=wap[:, c2 * 1024 + kc * 128: c2 * 1024 + (kc + 1) * 128],
                                        rhs=hs(kc, tt), start=(kc == 0), stop=(kc == NKC - 1))
                                return ins
                            tr.op("pe", [wkey] + [("h", kc, tt) for kc in range(NKC)],
                                  [("ps", pb)], emit_p)
                            sl = nxt("sq", 4)
                            sqs = sq_t[:, sl * TT:(sl + 1) * TT]
                            tr.op("act", [("ps", pb)], [("sq", sl)],
                                  lambda e, sqs=sqs, pb=pb: e.activation(
                                      out=sqs, in_=ps[pb][:], func=AF.Square))
                            if pend is not None:
                                qk_stage_b(*pend)
                            pend = (pb, sl, dst_t, dkey, gap, c, tt)
                    ringA.release()
                qk_stage_b(*pend)

                for j in range(16):
                    pb = nxt("pb", 4)

                    def emit_v(e, pb=pb, j=j):
                        ins = None
                        for kc in range(NKC):
                            ins = e.matmul(ps[pb][:], lhsT=hs_tok(kc, j * 128, 128),
                                           rhs=winv[:, kc * 520: kc * 520 + 512],
                                           start=(kc == 0), stop=(kc == NKC - 1))
                        return ins
                    hr = [("h", kc, j // 4) for kc in range(NKC)]
                    tr.op("pe", ["winv"] + hr, [("ps", pb)], emit_v)

                    def emit_f(e, j=j):
                        ins = None
                        for kc in range(NKC):
                            ins = e.matmul(ps[6][:, j * 8:(j + 1) * 8],
                                           lhsT=hs_tok(kc, j * 128, 128),
                                           rhs=winv[:, kc * 520 + 512: kc * 520 + 520],
                                           start=(kc == 0), stop=(kc == NKC - 1))
                        return ins
                    tr.op("pe", ["winv"] + hr, [("ps", 6)], emit_f)
                    for s_ in range(2):
                        tr.op("dve", [("ps", pb), "Vones"], [("V", j)],
                              lambda e, pb=pb, j=j, s_=s_: e.tensor_copy(
                                  out=V5[:, j, :, 2 * s_, :],
                                  in_=ps[pb][:].rearrange("p (c s e) -> p c s e", c=4, s=2)[:, :, s_, :]))

                def fzs(k):
                    if k in (1, 2):
                        return fzp[:, (k - 1) * 128: k * 128]
                    kk = {0: 0, 3: 1, 4: 2, 5: 3, 6: 4}[k]
                    return rc[:, kk * 128:(kk + 1) * 128]
                tr.op("dve", [("ps", 6)], [("fz", 0)], lambda e: e.tensor_tensor(
                    out=fzs(0), in0=ps[6][:, 0:128], in1=bfb[:], op=ALU.add))
                tr.op("dve", [("fz", 0)], [("fz", 1)], lambda e: e.tensor_scalar(
                    out=fzs(1), in0=fzs(0), scalar1=-1.0, scalar2=None, op0=ALU.mult))
                tr.op("dve", [("fz", 0), ("fz", 1)], [("fz", 1)], lambda e: e.tensor_tensor(
                    out=fzs(1), in0=fzs(1), in1=fzs(0), op=ALU.max))
                tr.op("act", [("fz", 1)], [("fz", 2)], lambda e: e.activation(
                    out=fzs(2), in_=fzs(1), func=AF.Exp, scale=-1.0))
                tr.op("act", [("fz", 2)], [("fz", 2)], lambda e: e.activation(
                    out=fzs(2), in_=fzs(2), func=AF.Ln, bias=one_t[:, 0:1], scale=1.0))
                tr.op("dve", [("fz", 0)], [("fz", 3)], lambda e: e.tensor_scalar_min(
                    out=fzs(3), in0=fzs(0), scalar1=0.0))
                tr.op("dve", [("fz", 3), ("fz", 2)], [("fz", 3)], lambda e: e.tensor_sub(
                    out=fzs(3), in0=fzs(3), in1=fzs(2)))
                tr.op("pe", [("fz", 3)], [("ps", 6)], lambda e: e.matmul(
                    ps[6][:, 0:128], lhsT=U_f[:], rhs=fzs(3), start=True, stop=True))
                tr.op("pe", [("fz", 3)], [("ps", 6)], lambda e: e.matmul(
                    ps[6][:, 128:256], lhsT=ones_f[:], rhs=fzs(3), start=True, stop=True))
                tr.op("pe", [("fz", 3)], [("ps", 6)], lambda e: e.matmul(
                    ps[6][:, 256:384], lhsT=H_f[:], rhs=fzs(3), start=True, stop=True))
                tr.op("dve", [("ps", 6)], [("fz", 4), ("fz", 5), ("fz", 6)],
                      lambda e: e.tensor_copy(out=rc[:, 256:640], in_=ps[6][:, 0:384]))
                tr.op("dve", [("fz", 0)], [("fz", 0)], lambda e: e.memset(rc[:, 0:8], 0.0))
                for j in range(1, 16):
                    tr.op("dve", [("fz", 0), ("fz", 5)], [("fz", 0)],
                          lambda e, j=j: e.tensor_tensor(
                              out=rc[:, j * 8:(j + 1) * 8], in0=rc[:, (j - 1) * 8: j * 8],
                              in1=rc[:, 384 + (j - 1) * 8: 384 + j * 8], op=ALU.add))
                tr.op("dve", [("fz", 4), ("fz", 0)], [("fz", 1)],
                      lambda e: e.scalar_tensor_tensor(
                          out=fzs(1), in0=fzs(4), scalar=-1.0, in1=fzs(0),
                          op0=ALU.mult, op1=ALU.subtract))
                tr.op("dve", [("fz", 0)], [("fz", 2)], lambda e: e.tensor_copy(
                    out=fzs(2), in_=fzs(0)))
                negF3 = fzs(1).rearrange("p (j h) -> p j h", h=8)
                Fref3 = fzs(2).rearrange("p (j h) -> p j h", h=8)

                wp0, wk0 = ringA.acquire()
                wp1, wk1 = ringA.acquire()
                alias_fence([("fz", k) for k in (0, 3, 4, 5, 6)], ["tmpP0"])
                tr.op("dve", ["winv", "fence"], ["winv", "fence"] + [("mixed", g) for g in range(4)],
                      lambda e: e.memset(fence_t[:, 0:1], 0.0))
                SCAN_ENG = {0: "dve", 1: "dve", 2: "pool", 3: "pool"}

                def pool_stage_a(n):
                    tt, g = divmod(n, 4)
                    w = 2 << g
                    wap, wkey = (wp0, wk0) if g < 2 else (wp1, wk1)
                    off = (g % 2) * 1024
                    pb = nxt("pb", 4)

                    def emit_pv(e):
                        ins = None
                        for kc in range(NKC):
                            ins = e.matmul(
                                ps[pb][:], lhsT=wap[:, off + kc * 128: off + (kc + 1) * 128],
                                rhs=hs(kc, tt), start=(kc == 0), stop=(kc == NKC - 1))
                        return ins
                    tr.op("pe", [wkey] + [("h", kc, tt) for kc in range(NKC)], [("ps", pb)], emit_pv)
                    sl = n % 2
                    pv = pvg[:, sl * 528:(sl + 1) * 528]
                    tr.op("act", [("ps", pb)], [("pvg", sl)],
                          lambda e: e.copy(out=pv[:, 16:528], in_=ps[pb][:]))

                qzf = qz[:, :].bitcast(F32)

                def pool_stage_a2(n):
                    tt, g = divmod(n, 4)
                    w = 2 << g
                    sl = n % 2
                    pv = pvg[:, sl * 528:(sl + 1) * 528]
                    en = "pool" if g == 3 else "dve"
                    if g == 3:
                        tbuf = [rc[:, 0:528], qzf[:, 0:528]]
                        tkey = ["tmpP0", "tmpP1"]
                    else:
                        tbuf = [ovl2[:, 1056:1584], ovl2[:, 1584:2112]]
                        tkey = [("tmp", 0), ("tmp", 1)]
                    if tt == 0:
                        tr.op(en, [], [("pvgh", sl)], lambda e: e.memset(pv[:, 0:16], 0.0))
                    else:
                        tr.op(en, [("halo", g)], [("pvgh", sl)],
                              lambda e: e.tensor_copy(out=pv[:, 0:16], in_=halo[:, g * 16:(g + 1) * 16]))
                    tr.op(en, [("pvg", sl)], [("halo", g)],
                          lambda e: e.tensor_copy(out=halo[:, g * 16:(g + 1) * 16], in_=pv[:, 512:528]))
                    src, skeys = pv, [("pvg", sl), ("pvgh", sl)]
                    lo, k, ti = 0, 1, 0
                    while k < w:
                        dst = tbuf[ti]
                        tr.op(en, skeys, [tkey[ti]],
                              lambda e, dst=dst, src=src, lo=lo, k=k: e.tensor_tensor(
                                  out=dst[:, lo + k:528], in0=src[:, lo + k:528],
                                  in1=src[:, lo:528 - k], op=ALU.add))
                        src, skeys = dst, [tkey[ti]]
                        lo += k
                        k *= 2
                        ti ^= 1
                    psl = n % 2
                    pl = pooled[:, psl * TT:(psl + 1) * TT]
                    if g == 3:
                        oth, okey = tbuf[ti], tkey[ti]
                        tr.op(en, skeys, [okey], lambda e: e.tensor_scalar(
                            out=oth[:, 16:528], in0=src[:, 16:528], scalar1=1.0 / w, scalar2=0.0,
                            op0=ALU.mult, op1=ALU.add))
                        tr.op(en, [okey, ("pvg", sl)], [("pooled", psl)], lambda e: e.tensor_tensor(
                            out=pl, in0=oth[:, 16:528], in1=pv[:, 16:528], op=ALU.subtract))
                        if tt == 0:
                            tr.op(en, skeys, skeys, lambda e: e.tensor_tensor(
                                out=src[:, 0:w - 1], in0=src[:, 16:16 + w - 1],
                                in1=invc[:, 0:w - 1], op=ALU.mult))
                            tr.op(en, skeys + [("pvg", sl), ("pooled", psl)], [("pooled", psl)],
                                  lambda e: e.tensor_tensor(
                                      out=pl[:, 0:w - 1], in0=src[:, 0:w - 1],
                                      in1=pv[:, 16:16 + w - 1], op=ALU.subtract))
                        return
                    tr.op("dve", skeys + [("pvg", sl)], [("pooled", psl)],
                          lambda e: e.scalar_tensor_tensor(
                              out=pl, in0=src[:, 16:528], scalar=1.0 / w, in1=pv[:, 16:528],
                              op0=ALU.mult, op1=ALU.subtract))
                    if tt == 0:
                        tr.op("dve", skeys, ["fix"],
                              lambda e: e.tensor_tensor(
                                  out=fix[:, 0:w - 1], in0=src[:, 16:16 + w - 1],
                                  in1=invc[:, 0:w - 1], op=ALU.mult))
                        tr.op("dve", ["fix", ("pvg", sl), ("pooled", psl)], [("pooled", psl)],
                              lambda e: e.tensor_tensor(
                                  out=pl[:, 0:w - 1], in0=fix[:, 0:w - 1],
                                  in1=pv[:, 16:16 + w - 1], op=ALU.subtract))

                sqslot = {}

                def pool_stage_b(n):
                    tt, g = divmod(n, 4)
                    psl = n % 2
                    pl = pooled[:, psl * TT:(psl + 1) * TT]
                    mb = 4 + nxt("mb", 2)
                    tr.op("pe", [("pooled", psl)], [("ps", mb)],
                          lambda e: e.matmul(ps[mb][:], lhsT=poolw[:, g * 128:(g + 1) * 128], rhs=pl,
                                             start=True, stop=True))
                    mx = mixed[:, g * TT:(g + 1) * TT]
                    tr.op("act", [("ps", mb)], [("mixed", g)],
                          lambda e: e.activation(out=mx, in_=ps[mb][:], func=AF.Copy,
                                                 scale=pvec[:, 24 + g: 25 + g]))
                    sl2 = nxt("sq", 4)
                    sqslot[n] = sl2
                    sqs = sq_t[:, sl2 * TT:(sl2 + 1) * TT]
                    tr.op("act", [("ps", mb)], [("sq", sl2)],
                          lambda e: e.activation(out=sqs, in_=ps[mb][:], func=AF.Square,
                                                 scale=pvec[:, 24 + g: 25 + g]))

                def pool_stage_b2(n):
                    tt, g = divmod(n, 4)
                    sl2 = sqslot[n]
                    sqs = sq_t[:, sl2 * TT:(sl2 + 1) * TT]
                    tr.op("pe", [("sq", sl2)], [("ps", 6)],
                          lambda e: e.matmul(ps[6][:], lhsT=ones_bf[:], rhs=sqs,
                                             start=(g == 0), stop=(g == 3)))
                    if g != 3:
                        return
                    r = nxt("rstd", 2)
                    rs_ = rstd_t[:, r * TT:(r + 1) * TT]
                    tr.op("act", [("ps", 6)], [("rstd", r)],
                          lambda e: e.activation(out=rs_, in_=ps[6][:], func=AF.Ln, scale=1.0 / 512,
                                                 bias=eps_t[:, 0:1]))
                    tr.op("act", [("rstd", r)], [("rstd", r)],
                          lambda e: e.activation(out=rs_, in_=rs_, func=AF.Exp, scale=-0.5))
                    for g_ in range(4):
                        tr.op("dve", [("mixed", g_), ("rstd", r)], [("h", g_, tt)],
                              lambda e, g_=g_: e.scalar_tensor_tensor(
                                  out=hs(g_, tt), in0=mixed[:, g_ * TT:(g_ + 1) * TT],
                                  scalar=pvec[:, 28 + g_: 29 + g_], in1=rs_,
                                  op0=ALU.mult, op1=ALU.mult))

                for n in range(19):
                    if n < 16:
                        pool_stage_a(n)
                    if 0 <= n - 1 < 16:
                        pool_stage_a2(n - 1)
                    if 0 <= n - 3 < 16:
                        pool_stage_b2(n - 3)
                    if 0 <= n - 2 < 16:
                        pool_stage_b(n - 2)
                ringA.release()
                ringA.release()

                alias_fence([("pvg", 0), ("pvg", 1), ("pvgh", 0), ("pvgh", 1), ("tmp", 0), ("tmp", 1)],
                            [("on", c_, s_) for c_ in range(4) for s_ in range(2)])

                alias_fence([("pooled", 0), ("pooled", 1)],
                            [("PT", p_, s_) for p_ in range(3) for s_ in range(2)])
                alias_fence([("fz", k) for k in (0, 3, 4, 5, 6)] + ["tmpP0"],
                            [("rc", r_, s_) for r_ in range(2) for s_ in range(2)])
                tr.op("dve", ["tmpP1"], ["tmpP1", ("qz", 0, 0), ("qz", 0, 1), ("qz", 1, 0), ("qz", 1, 1)],
                      lambda e: e.memset(qz[:], 0.0))
                steps = []
                gcc = 0
                for g in range(4):
                    for c in range(4):
                        par = gcc % 2
                        gcc += 1
                        first_step = len(steps)
                        for j in range(4 * g + 4):
                            for hq in range(2):
                                i0 = 4 * g + 2 * hq
                                i1 = i0 + 1
                                if j > i1:
                                    continue
                                lo = max(i0, j)
                                steps.append(dict(g=g, c=c, par=par, j=j, i1=i1, lo=lo, hq=hq,
                                                  nb=i1 - lo + 1, qoff=(lo - 4 * g) * 128,
                                                  pre=False, post=False, gpost=False, first=False))
                        steps[first_step]["pre"] = True
                        steps[first_step]["first"] = True
                        steps[-1]["post"] = True
                        steps[-1]["gpost"] = (c == 3)

                def emit_pre(st):
                    g, c, par = st["g"], st["c"], st["par"]
                    for s_ in range(2):
                        p0 = s_ * 64
                        tr.op("pool", [("qn", c, g)], [("qz", par, s_)],
                              lambda e, p0=p0, s_=s_: e.tensor_copy(
                                  out=qz[p0:p0 + 64, par * 1024 + s_ * 512: par * 1024 + (s_ + 1) * 512],
                                  in_=qn[p0:p0 + 64, c * S + g * TT: c * S + (g + 1) * TT]))
                    for s_ in range(2):
                        h = 2 * c + s_
                        for hq in range(2):
                            i1 = 4 * g + 2 * hq + 1
                            bo = par * 64 + s_ * 32 + hq * 16
                            tr.op("pool", [("fz", 1), ("fz", 2)], [("BT", par, s_, hq)],
                                  lambda e, i1=i1, h=h, bo=bo: e.tensor_scalar(
                                      out=BT[:, bo: bo + i1 + 1],
                                      in0=negF3[:, 0:i1 + 1, h], scalar1=1.0,
                                      scalar2=Fref3[:, i1, h:h + 1],
                                      op0=ALU.mult, op1=ALU.add))

                def emit_qk(n, st):
                    c, par, j, qoff, ncol = st["c"], st["par"], st["j"], st["qoff"], st["nb"] * 128
                    sbk = n % 3
                    qz3 = qz[:, par * 1024:(par + 1) * 1024].rearrange("p (s t) -> p s t", s=2)
                    diag = (st["lo"] == j)

                    def emit_s(e):
                        ins = e.matmul(
                            ps[sbk][:, 0:2 * ncol],
                            lhsT=kn[:, c * S + j * 128: c * S + (j + 1) * 128],
                            rhs=qz3[:, :, qoff:qoff + ncol], start=True, stop=not diag,
                            skip_group_check=True)
                        if diag:
                            for s_ in range(2):
                                ins = e.matmul(
                                    ps[sbk][:, s_ * ncol: s_ * ncol + 128], lhsT=ident_bf[:],
                                    rhs=mneg_bf[:], start=False, stop=(s_ == 1),
                                    skip_group_check=True)
                        return ins
                    tr.op("pe", [("kn", c, j // 4), ("qz", par, 0), ("qz", par, 1)], [("ps", sbk)],
                          emit_s)

                def emit_exp(n, st):
                    g, par, j, lo, nb, hq = st["g"], st["par"], st["j"], st["lo"], st["nb"], st["hq"]
                    ncol = nb * 128
                    sbk = psl = n % 3
                    for s_ in range(2):
                        co = s_ * ncol
                        pts = PT[:, psl * TT + co: psl * TT + co + ncol]
                        bo = par * 64 + s_ * 32 + hq * 16 + j
                        tr.op("act", [("ps", sbk), ("BT", par, s_, hq)], [("PT", psl, s_)],
                              lambda e, pts=pts, co=co, bo=bo: e.activation(
                                  out=pts, in_=ps[sbk][:, co:co + ncol], func=AF.Exp,
                                  bias=BT[:, bo:bo + 1], scale=1.0))

                def emit_pv(n, st):
                    c, par, j, qoff, nb = st["c"], st["par"], st["j"], st["qoff"], st["nb"]
                    ncol = nb * 128
                    psl = n % 3
                    TA = 3 + 2 * par
                    for s_ in range(2):
                        Tb = TA + s_
                        tr.op("pe", [("PT", psl, s_), ("V", j)],
                              [("ps", Tb)],
                              lambda e, Tb=Tb, s_=s_: e.matmul(
                                  ps[Tb][:, qoff:qoff + ncol],
                                  lhsT=V5[:, j, c, s_:s_ + 2, :],
                                  rhs=PT[:, psl * TT + s_ * ncol: psl * TT + (s_ + 1) * ncol],
                                  start=st["first"], stop=(j == st["i1"]), skip_group_check=True))

                def emit_post(st):
                    g, c, par = st["g"], st["c"], st["par"]
                    assert not gpend, (g, c, gpend)
                    TA = 3 + 2 * par
                    TB = TA + 1
                    rcA = rc[0:64, par * TT:(par + 1) * TT]
                    rcB = rc[64:128, par * TT:(par + 1) * TT]
                    onc = ovl2[:, c * TT:(c + 1) * TT]
                    tr.op("dve", [("ps", TA)], [("rc", par, 0)],
                          lambda e: e.reciprocal(out=rcA, in_=ps[TA][64:128, :]))
                    tr.op("dve", [("ps", TB)], [("rc", par, 1)],
                          lambda e: e.reciprocal(out=rcB, in_=ps[TB][0:64, :]))
                    tr.op("dve", [("ps", TA), ("rc", par, 0)], [("on", c, 0)],
                          lambda e: e.tensor_tensor(
                              out=onc[0:64, :], in0=ps[TA][0:64, :], in1=rcA, op=ALU.mult))
                    tr.op("dve", [("ps", TB), ("rc", par, 1)], [("on", c, 1)],
                          lambda e: e.tensor_tensor(
                              out=onc[64:128, :], in0=ps[TB][64:128, :], in1=rcB, op=ALU.mult))
                    if st["gpost"]:
                        gpend.append(g)

                def emit_gpost(g):
                    for c_ in range(4):
                        onc_ = ovl2[:, c_ * TT:(c_ + 1) * TT]
                        sl = nxt("sq", 4)
                        sqs = sq_t[:, sl * TT:(sl + 1) * TT]
                        tr.op("pool", [("on", c_, 0), ("on", c_, 1)], [("sq", sl)],
                              lambda e, sqs=sqs, onc_=onc_: e.tensor_tensor(
                                  out=sqs, in0=onc_, in1=onc_, op=ALU.mult))
                        tr.op("pe", [("sq", sl)], [("ps", 7)],
                              lambda e, sqs=sqs, c_=c_: e.matmul(
                                  ps[7][:], lhsT=ones_bf[:], rhs=sqs, start=(c_ == 0), stop=(c_ == 3)))
                    r = nxt("rstd", 2)
                    rs_ = rstd_t[:, r * TT:(r + 1) * TT]
                    tr.op("act", [("ps", 7)], [("rstd", r)],
                          lambda e: e.activation(
                              out=rs_, in_=ps[7][:], func=AF.Ln, scale=1.0 / 512, bias=eps_t[:, 0:1]))
                    tr.op("act", [("rstd", r)], [("rstd", r)],
                          lambda e: e.activation(out=rs_, in_=rs_, func=AF.Exp, scale=-0.5))
                    for c_ in range(4):
                        onc_ = ovl2[:, c_ * TT:(c_ + 1) * TT]
                        if g < 2:
                            tr.op("pool", [("on", c_, 0), ("on", c_, 1)], [("on", c_, 0), ("on", c_, 1)],
                                  lambda e, onc_=onc_, c_=c_: e.tensor_scalar(
                                      out=onc_, in0=onc_, scalar1=pvec[:, 34 + c_: 35 + c_],
                                      scalar2=0.0, op0=ALU.mult, op1=ALU.add))
                            tr.op("pool", [("on", c_, 0), ("on", c_, 1), ("rstd", r)], [("h", 4 + c_, g)],
                                  lambda e, onc_=onc_, c_=c_: e.tensor_tensor(
                                      out=hs(4 + c_, g), in0=onc_, in1=rs_, op=ALU.mult))
                            continue
                        tr.op("dve", [("on", c_, 0), ("on", c_, 1), ("rstd", r)], [("h", 4 + c_, g)],
                              lambda e, onc_=onc_, c_=c_: e.scalar_tensor_tensor(
                                  out=hs(4 + c_, g), in0=onc_, scalar=pvec[:, 34 + c_: 35 + c_],
                                  in1=rs_, op0=ALU.mult, op1=ALU.mult))

                NS = len(steps)
                gpend = []
                gdue = {}
                GDEFER = 12
                pres = [n for n in range(NS) if steps[n]["pre"]]
                emit_pre(steps[pres[0]])
                npre = 1
                for n in range(NS + 2):
                    if n < NS:
                        emit_qk(n, steps[n])
                    if 0 <= n - 1 < NS:
                        emit_exp(n - 1, steps[n - 1])
                    if 0 <= n - 2 < NS:
                        emit_pv(n - 2, steps[n - 2])
                        if steps[n - 2]["post"]:
                            emit_post(steps[n - 2])
                            for g_ in gpend:
                                gdue.setdefault(g_, n + GDEFER)
                    for g_ in list(gpend):
                        if n >= gdue[g_]:
                            emit_gpost(g_)
                            gpend.remove(g_)
                    if n < NS and steps[n]["pre"] and npre < len(pres):
                        emit_pre(steps[pres[npre]])
                        npre += 1

                for g_ in list(gpend):
                    emit_gpost(g_)
                def wo_step(dc, d2, tt, wap, wkey):
                    pb = nxt("pb", 3)

                    def emit_o(e):
                        ins = None
                        for kc in range(NKC):
                            ins = e.matmul(
                                ps[pb][:],
                                lhsT=wap[:, d2 * 1024 + kc * 128: d2 * 1024 + (kc + 1) * 128],
                                rhs=hs(kc, tt), start=(kc == 0), stop=(kc == NKC - 1))
                        return ins
                    tr.op("pe", [wkey] + [("h", kc, tt) for kc in range(NKC)], [("ps", pb)], emit_o)
                    tr.op("dve", [("ps", pb), ("x", dc, tt)], [("x", dc, tt)],
                          lambda e: e.tensor_tensor(
                              out=xs(dc, tt), in0=ps[pb][:], in1=xs(dc, tt), op=ALU.add))

                for it in range(3):
                    wap, wkey = ringA.acquire()
                    for d2 in range(2):
                        for tt in range(NTT):
                            wo_step(it * 2 + d2, d2, tt, wap, wkey)
                    ringA.release()
                wap, wkey = ringA.acquire()
                for tt in range(NTT):
                    for d2 in range(2):
                        wo_step(6 + d2, d2, tt, wap, wkey)
                    if tt >= 1:
                        rmsnorm_x(16, [tt - 1])
                ringA.release()
                rmsnorm_x(16, [NTT - 1])
                tr.barrier()

        def load_x(s, kc):
            tr.dma("sp", f"xld{kc}", [], [("x", kc, tt) for tt in range(NTT)],
                   lambda e: e.dma_start(out=x_sb[:, kc * S:(kc + 1) * S], in_=xT[s, kc]))

        for tt in range(NTT):
            for kc in range(NKC):
                tr.add_dma_sem(f"xi{kc}_{tt}")
                tr.dma("sp", f"xi{kc}_{tt}", [], [("x", kc, tt)],
                       lambda e, kc=kc, tt=tt: e.dma_start(
                           out=xs(kc, tt), in_=xT[0, kc][:, tt * TT:(tt + 1) * TT]))
        for s in range(NSEQ):
            def after_chunk(kc, s=s):
                tr.dma("sp", f"yst{kc}", [("x", kc, tt) for tt in range(NTT)], [("y", s, kc)],
                       lambda e: e.dma_start(out=yT[s, kc], in_=x_sb[:, kc * S:(kc + 1) * S]))
                if s + 1 < NSEQ:
                    load_x(s + 1, kc)
            ffn(0)
            mixer()
            ffn(16, after_chunk, prenormed=True)
        tr.barrier(["sp"], all_sems=True)
    return nc


def _prep_weights(inp):
    f = lambda a: np.ascontiguousarray(np.asarray(a, dtype=np.float32))

    def lay_gu(Wg, Wu):
        g = f(Wg).reshape(8, 128, 22, 128).transpose(2, 1, 0, 3)
        u = f(Wu).reshape(8, 128, 22, 128).transpose(2, 1, 0, 3)
        return f(np.stack([g, u], axis=2).reshape(22, 128, 2048))

    def lay_d(Wd):
        return f(f(Wd).reshape(2, FPH, 128, 8, 128).transpose(0, 3, 2, 1, 4).reshape(16, 128, FPH * 128))

    w_in = f(inp["w_in"])
    wi = w_in[:, 0:1536].reshape(8, 128, 12, 128).transpose(2, 1, 0, 3)
    order = [4, 5, 6, 7, 8, 9, 10, 11, 0, 1, 2, 3]
    wi = wi[order].reshape(6, 2, 128, 8, 128).transpose(0, 2, 1, 3, 4).reshape(6, 128, 2048)
    winv = w_in[:, 1536:2056].reshape(8, 128, 520).transpose(1, 0, 2)
    wo = f(inp["w_out"]).reshape(8, 128, 8, 128).transpose(2, 1, 0, 3)
    wo = wo.reshape(4, 2, 128, 8, 128).transpose(0, 2, 1, 3, 4).reshape(4, 128, 2048)
    poolw = f(inp["pool_w"]).transpose(1, 0, 2).reshape(128, 512)
    pvec = np.zeros((128, 40), np.float32)
    pvec[:, 0:8] = f(inp["ffn1_norm"]).reshape(8, 128).T
    pvec[:, 8:16] = f(inp["mix_norm"]).reshape(8, 128).T
    pvec[:, 16:24] = f(inp["ffn2_norm"]).reshape(8, 128).T
    pvec[:, 24:28] = f(inp["pool_scale"]).reshape(4, 128).T
    pvec[:, 28:32] = f(inp["out_norm_pool"]).reshape(4, 128).T
    pvec[:, 32] = np.tile(f(inp["q_norm"]), 2)
    pvec[:, 33] = np.tile(f(inp["k_norm"]), 2)
    pvec[:, 34:38] = f(inp["out_norm_attn"]).reshape(4, 128).T
    bfb = np.broadcast_to(np.tile(f(inp["b_forget"]), 16)[None, :], (128, 128))
    return {
        "wgu1": lay_gu(inp["ffn1_w_gate"], inp["ffn1_w_up"]),
        "wgu2": lay_gu(inp["ffn2_w_gate"], inp["ffn2_w_up"]),
        "wd1": lay_d(inp["ffn1_w_down"]), "wd2": lay_d(inp["ffn2_w_down"]),
        "win": f(wi), "winv": f(winv), "wout": f(wo), "poolw": f(poolw),
        "pvec": pvec, "bfb": f(bfb),
    }


_NC_CACHE = {}


def kernel(**inputs):
    x = np.asarray(inputs["x"], dtype=np.float32)
    w = _prep_weights(inputs)
    if "nc" not in _NC_CACHE:
        _NC_CACHE["nc"] = build_nc()
    nc = _NC_CACHE["nc"]
    in_maps = []
    for c in range(NCORES):
        xc = x[c * NSEQ:(c + 1) * NSEQ].transpose(0, 2, 1).reshape(NSEQ, NKC, 128, S)
        m = {"xT": np.ascontiguousarray(xc)}
        m.update(w)
        in_maps.append(m)
    res = run_bass_kernel_spmd(nc, in_maps, core_ids=list(range(NCORES)))
    out = np.empty((NCORES * NSEQ, S, D), np.float32)
    for c in range(NCORES):
        y = np.asarray(res.results[c]["yT"]).reshape(NSEQ, D, S)
        out[c * NSEQ:(c + 1) * NSEQ] = y.transpose(0, 2, 1)
    return out
```

```python
import contextlib
import numpy as np
import concourse.bass as bass
import concourse.mybir as mybir
from concourse.bass_utils import run_bass_kernel_spmd

F32 = mybir.dt.float32
BF16 = mybir.dt.bfloat16
ALU = mybir.AluOpType
AF = mybir.ActivationFunctionType

NCORES = 8
D = 1024
S = 2048
DFF = 2816
NKC = 8
TT = 512
NTT = 4
FPH = 11
EPS = 1e-6
NSEQ = 2


class Tracker:
    def __init__(self, nc, es):
        self.nc = nc
        self.eng = {"pe": nc.tensor, "act": nc.scalar, "dve": nc.vector,
                    "pool": nc.gpsimd, "sp": nc.sync}
        self.semobj = {}
        self.val = {}
        for k in self.eng:
            self.semobj["e_" + k] = es.enter_context(nc.semaphore("s_" + k))
            self.val["e_" + k] = 0
        self.es = es
        self.known = {}
        self.lastw = {}
        self.readers = {}

    def add_dma_sem(self, name):
        self.semobj[name] = self.es.enter_context(self.nc.semaphore("d_" + name))
        self.val[name] = 0

    def _deps(self, reads, writes):
        d = {}

        def add(tok):
            if tok is None:
                return
            s, v = tok
            if d.get(s, 0) < v:
                d[s] = v
        for r in reads:
            add(self.lastw.get(r))
        for w in writes:
            add(self.lastw.get(w))
            for s, v in self.readers.get(w, {}).items():
                add((s, v))
        return d

    def _wait(self, e, d):
        for s, v in d.items():
            if e == "pe" and s == "e_pe":
                continue
            if self.known.get((e, s), 0) >= v:
                continue
            self.eng[e].wait_ge(self.semobj[s], v)
            self.known[(e, s)] = v

    def _commit(self, tok, reads, writes):
        s, v = tok
        for r in reads:
            rd = self.readers.setdefault(r, {})
            if rd.get(s, 0) < v:
                rd[s] = v
        for w in writes:
            self.lastw[w] = tok
            self.readers[w] = {}

    def op(self, e, reads, writes, emit):
        self._wait(e, self._deps(reads, writes))
        ins = emit(self.eng[e])
        s = "e_" + e
        self.val[s] += 1
        ins.then_inc(self.semobj[s], 1)
        self._commit((s, self.val[s]), reads, writes)

    def dma(self, e, semname, reads, writes, emit):
        self._wait(e, self._deps(reads, writes))
        ins = emit(self.eng[e])
        self.val[semname] += 16
        ins.then_inc(self.semobj[semname], 16)
        self._commit((semname, self.val[semname]), reads, writes)

    def barrier(self, engines=None, all_sems=False):
        d = {s: v for s, v in self.val.items()
             if v > 0 and (all_sems or s.startswith("e_") or s in ("winv", "cstp", "cst0", "cst2"))}
        for e in (engines or self.eng):
            self._wait(e, d)


class Ring:
    def __init__(self, tr, name, tile, nslots, slotw, items):
        self.tr, self.name, self.tile = tr, name, tile
        self.nslots, self.slotw, self.items = nslots, slotw, items
        self.n_issued = self.n_acq = self.n_rel = 0
        for s in range(nslots):
            tr.add_dma_sem(f"{name}{s}")

    def prefetch(self):
        while self.n_issued < len(self.items) and self.n_issued < self.n_rel + self.nslots:
            i = self.n_issued
            slot = i % self.nslots
            src = self.items[i]
            w = src.shape[-1]
            dst = self.tile[:, slot * self.slotw: slot * self.slotw + w]
            self.tr.dma("pool", f"{self.name}{slot}", [], [(self.name, slot)],
                        lambda g, dst=dst, src=src: g.dma_start(out=dst, in_=src))
            self.n_issued += 1

    def acquire(self):
        self.prefetch()
        i = self.n_acq
        assert self.n_issued > i, (self.name, i, self.n_issued, self.n_rel)
        self.n_acq += 1
        slot = i % self.nslots
        return self.tile[:, slot * self.slotw:(slot + 1) * self.slotw], (self.name, slot)

    def release(self):
        self.n_rel += 1
        self.prefetch()


def build_nc():
    nc = bass.Bass("TRN2", target_bir_lowering=False)

    def din(name, shape):
        return nc.dram_tensor(name, shape, F32, kind="ExternalInput").ap()

    xT = din("xT", [NSEQ, NKC, 128, S])
    wgu = [din("wgu1", [22, 128, 2048]), din("wgu2", [22, 128, 2048])]
    wd = [din("wd1", [16, 128, FPH * 128]), din("wd2", [16, 128, FPH * 128])]
    win = din("win", [6, 128, 2048])
    winv_d = din("winv", [128, NKC, 520])
    wout = din("wout", [4, 128, 2048])
    poolw_d = din("poolw", [128, 512])
    pvec_d = din("pvec", [128, 40])
    bfb_d = din("bfb", [128, 128])
    yT = nc.dram_tensor("yT", [NSEQ, NKC, 128, S], F32, kind="ExternalOutput").ap()

    es = contextlib.ExitStack()
    with es:
        def sb(name, shape, dt):
            return es.enter_context(nc.sbuf_tensor("sb_" + name, shape, dt))
        uid = [0]

        tr = Tracker(nc, es)
        for nm in [f"xld{k}" for k in range(8)] + [f"yst{k}" for k in range(8)] + ["cst0", "cst2", "cstp", "winv"]:
            tr.add_dma_sem(nm)

        x_sb = sb("x_sb", [128, NKC * S], F32)
        h_sb = sb("h_sb", [128, NKC * S], BF16)
        rA_t = sb("rA", [128, 2 * 2048], BF16)
        rB_t = sb("rB", [128, 2 * FPH * 128], BF16)
        sq_t = sb("sq", [128, 4 * TT], BF16)
        rstd_t = sb("rstd", [128, 2 * TT], F32)
        pvec = sb("pvec", [128, 40], F32)
        ones_bf = sb("ones_bf", [128, 128], BF16)
        bd_bf = sb("bd_bf", [128, 128], BF16)
        ident_bf = sb("ident_bf", [128, 128], BF16)
        mneg_bf = sb("mneg_bf", [128, 128], BF16)
        zero_bf = sb("zero_bf", [128, 128], BF16)
        U_f = sb("U_f", [128, 128], F32)
        ones_f = sb("ones_f", [128, 128], F32)
        H_f = sb("H_f", [128, 128], F32)
        bfb = sb("bfb", [128, 128], F32)
        poolw = sb("poolw", [128, 512], BF16)
        invc = sb("invc", [128, 16], F32)
        gq8 = sb("gq8", [128, 1], F32)

        ps = [es.enter_context(nc.psum_tensor(f"ps{b}", [128, 512], F32)) for b in range(8)]

        def xs(kc, tt):
            return x_sb[:, kc * S + tt * TT: kc * S + (tt + 1) * TT]

        def hs(kc, tt):
            return h_sb[:, kc * S + tt * TT: kc * S + (tt + 1) * TT]

        def hs_tok(kc, t0, n):
            return h_sb[:, kc * S + t0: kc * S + t0 + n]

        V = nc.vector
        G = nc.gpsimd
        tr.dma("sp", "cst0", [], ["c0"], lambda e: e.dma_start(out=pvec[:], in_=pvec_d))
        tr.dma("sp", "cst2", [], ["c2"], lambda e: e.dma_start(out=bfb[:], in_=bfb_d))
        tr.dma("pool", "cstp", [], ["c3"], lambda e: e.dma_start(out=poolw[:], in_=poolw_d))
        tr.op("dve", [], ["k0"], lambda e: e.memset(ones_bf[:], 1.0))
        tr.op("dve", [], ["k1"], lambda e: e.memset(ones_f[:], 1.0))
        tr.op("dve", [], ["k2"], lambda e: e.memset(bd_bf[:], 0.0))
        tr.op("dve", ["k2"], ["k2"], lambda e: e.memset(bd_bf[0:64, 0:64], 1.0))
        tr.op("dve", ["k2"], ["k2"], lambda e: e.memset(bd_bf[64:128, 64:128], 1.0))
        tr.op("dve", [], ["k3"], lambda e: e.memset(H_f[:], 0.0))
        tr.op("dve", ["k3"], ["k3"], lambda e: e.memset(H_f[0:64, :], 1.0))
        tr.op("dve", [], ["k11"], lambda e: e.memset(zero_bf[:], 0.0))
        tr.op("pool", ["k0"], ["k4"], lambda e: e.affine_select(
            out=ident_bf[:], in_=ones_bf[:], pattern=[[-1, 128]], compare_op=ALU.is_equal,
            fill=0.0, base=0, channel_multiplier=1))
        tr.op("pool", ["k11"], ["k5"], lambda e: e.affine_select(
            out=mneg_bf[:], in_=zero_bf[:], pattern=[[1, 128]], compare_op=ALU.is_ge,
            fill=-30000.0, base=0, channel_multiplier=-1))
        tr.op("pool", ["k1"], ["k6"], lambda e: e.affine_select(
            out=U_f[:], in_=ones_f[:], pattern=[[1, 128]], compare_op=ALU.is_ge,
            fill=0.0, base=0, channel_multiplier=-1))
        tr.op("pool", [], ["k7"], lambda e: e.iota(
            invc[:], pattern=[[1, 16]], base=1, channel_multiplier=0,
            allow_small_or_imprecise_dtypes=True))
        tr.op("dve", ["k7"], ["k7"], lambda e: e.reciprocal(out=invc[:], in_=invc[:]))
        tr.op("dve", ["c0"], ["k8"], lambda e: e.tensor_scalar(
            out=gq8[:], in0=pvec[:, 32:33], scalar1=0.125, scalar2=None, op0=ALU.mult))
        tr.barrier()

        itemsA, itemsB = [], []
        for s in range(NSEQ):
            for f in range(2):
                if f == 1:
                    for i in range(6):
                        itemsA.append(win[i])
                    for i in range(4):
                        itemsA.append(wout[i])
                for ffc in range(22):
                    itemsA.append(wgu[f][ffc])
                for i in range(16):
                    itemsB.append(wd[f][i])
        ringA = Ring(tr, "rA", rA_t, 2, 2048, itemsA)
        ringB = Ring(tr, "rB", rB_t, 2, FPH * 128, itemsB)

        cnt = {"sq": 0, "rstd": 0, "ab": 0, "bb": 0, "sg": 0, "pb": 0, "mb": 0, "yb": 0, "pl": 0}

        def nxt(k, mod):
            v = cnt[k] % mod
            cnt[k] += 1
            return v

        def rmsnorm_x(gcol, tiles=range(NTT)):
            for tt in tiles:
                for kc in range(NKC):
                    sl = nxt("sq", 4)
                    sqs = sq_t[:, sl * TT:(sl + 1) * TT]
                    tr.op("act", [("x", kc, tt)], [("sq", sl)],
                          lambda e, sqs=sqs, kc=kc, tt=tt: e.activation(
                              out=sqs, in_=xs(kc, tt), func=AF.Square))
                    tr.op("pe", [("sq", sl)], [("ps", 6)],
                          lambda e, sqs=sqs, kc=kc: e.matmul(
                              ps[6][:], lhsT=ones_bf[:], rhs=sqs,
                              start=(kc == 0), stop=(kc == NKC - 1)))
                r = nxt("rstd", 2)
                rs_ = rstd_t[:, r * TT:(r + 1) * TT]
                tr.op("act", [("ps", 6)], [("rstd", r)],
                      lambda e, rs_=rs_: e.activation(out=rs_, in_=ps[6][:], func=AF.Ln,
                                                      scale=1.0 / D, bias=eps_t[:, 0:1]))
                tr.op("act", [("rstd", r)], [("rstd", r)],
                      lambda e, rs_=rs_: e.activation(out=rs_, in_=rs_, func=AF.Exp, scale=-0.5))
                for kc in range(NKC):
                    tr.op("dve", [("x", kc, tt), ("rstd", r)], [("h", kc, tt)],
                          lambda e, kc=kc, tt=tt, rs_=rs_: e.scalar_tensor_tensor(
                              out=hs(kc, tt), in0=xs(kc, tt),
                              scalar=pvec[:, gcol + kc: gcol + kc + 1], in1=rs_,
                              op0=ALU.mult, op1=ALU.mult))

        eps_t = sb("eps_t", [128, 1], F32)
        tr.op("dve", [], ["k9"], lambda e: e.memset(eps_t[:], EPS))
        one_t = sb("one_t", [128, 1], F32)
        tr.op("dve", [], ["k10"], lambda e: e.memset(one_t[:], 1.0))
        tr.barrier()

        def ffn(gcol, after_chunk=None, prenormed=False):
            uid[0] += 1
            with nc.sbuf_tensor(f"act_t{uid[0]}", [128, FPH * S], BF16) as act_t, \
                    nc.sbuf_tensor(f"sg_t{uid[0]}", [128, 2 * TT], F32) as sg_t:
                def acts(fl, tt):
                    return act_t[:, fl * S + tt * TT: fl * S + (tt + 1) * TT]

                ringB.prefetch()
                def stage_a(fl, tt, wap, wkey):
                    pb = nxt("ab", 2) * 2
                    hreads = [("h", kc, tt) for kc in range(NKC)]

                    def emit_mm(e, off, bank):
                        ins = None
                        for kc in range(NKC):
                            ins = e.matmul(
                                ps[bank][:], lhsT=wap[:, off + kc * 128: off + (kc + 1) * 128],
                                rhs=hs(kc, tt), start=(kc == 0), stop=(kc == NKC - 1))
                        return ins
                    tr.op("pe", [wkey] + hreads, [("ps", pb)], lambda e: emit_mm(e, 0, pb))
                    tr.op("pe", [wkey] + hreads, [("ps", pb + 1)], lambda e: emit_mm(e, 1024, pb + 1))
                    s_ = nxt("sg", 2)
                    sgs = sg_t[:, s_ * TT:(s_ + 1) * TT]
                    tr.op("act", [("ps", pb)], [("sg", s_)],
                          lambda e: e.activation(out=sgs, in_=ps[pb][:], func=AF.Silu))
                    tr.op("dve", [("sg", s_), ("ps", pb + 1)], [("act", fl, tt)],
                          lambda e: e.tensor_tensor(
                              out=acts(fl, tt), in0=sgs, in1=ps[pb + 1][:], op=ALU.mult))

                if not prenormed:
                    rmsnorm_x(gcol)
                for half in range(2):
                    for fl in range(FPH):
                        wap, wkey = ringA.acquire()
                        for tt in range(NTT):
                            stage_a(fl, tt, wap, wkey)
                        ringA.release()
                    for dc in range(NKC):
                        wdap, wdkey = ringB.acquire()
                        for tt in range(NTT):
                            pb = 4 + nxt("bb", 2)

                            def emit_d(e, pb=pb, tt=tt, dc=dc, wdap=wdap):
                                ins = None
                                for fl in range(FPH):
                                    ins = e.matmul(
                                        ps[pb][:], lhsT=wdap[:, fl * 128:(fl + 1) * 128],
                                        rhs=acts(fl, tt), start=(fl == 0), stop=(fl == FPH - 1))
                                return ins
                            tr.op("pe", [wdkey] + [("act", fl, tt) for fl in range(FPH)],
                                  [("ps", pb)], emit_d)
                            tr.op("dve", [("ps", pb), ("x", dc, tt)], [("x", dc, tt)],
                                  lambda e, pb=pb, dc=dc, tt=tt: e.scalar_tensor_tensor(
                                      out=xs(dc, tt), in0=ps[pb][:], scalar=0.5, in1=xs(dc, tt),
                                      op0=ALU.mult, op1=ALU.add))
                        ringB.release()
                        if half == 1 and after_chunk is not None:
                            after_chunk(dc)
                tr.barrier()

        def mixer():
            mes = contextlib.ExitStack()
            uid[0] += 1
            with mes:
                def msb(name, shape, dt):
                    return mes.enter_context(nc.sbuf_tensor(f"m{uid[0]}_{name}", shape, dt))
                qn = msb("qn", [128, 4 * S], BF16)
                kn = msb("kn", [128, 4 * S], BF16)
                Vt = msb("Vt", [128, 16 * 768], BF16)
                ovl = msb("ovl", [128, 2080], F32)
                ovl2 = msb("ovl2", [128, 2112], F32)
                BT = msb("BT", [128, 2 * 128], F32)
                PT = msb("PT", [128, 3 * TT], BF16)
                pooled = PT
                fzp = msb("fzp", [128, 2 * 128], F32)
                rc = msb("rc", [128, 2 * TT], F32)
                halo = msb("halo", [128, 64], F32)
                qz = msb("qz", [128, 2 * 1024], BF16)
                fix = msb("fix", [128, 16], F32)

                fence_t = msb("fence_t", [128, 2], F32)

                def alias_fence(old_keys, new_keys):
                    tr.op("dve", list(old_keys) + ["fence"], list(old_keys) + list(new_keys) + ["fence"],
                          lambda e: e.memset(fence_t[:, 0:1], 0.0))

                winv = ovl[:, :].bitcast(BF16)
                mixed = ovl
                pvg = ovl2
                V5 = Vt[:, :].rearrange("p (j c s e) -> p j c s e", j=16, c=4, s=3, e=64)

                def og(il):
                    return ovl2[:, il * 512:(il + 1) * 512]

                for kc in range(NKC):
                    tr.dma("pool", "winv", [], ["winv"],
                           lambda g, kc=kc: g.dma_start(out=winv[:, kc * 520:(kc + 1) * 520],
                                                        in_=winv_d[:, kc, :]))
                tr.op("dve", [], ["Vones"], lambda e: e.memset(V5[:, :, :, 1, :], 1.0))

                rmsnorm_x(8)

                def qk_stage_b(pb, sl, dst_t, dkey, gap, c, tt):
                    sqs = sq_t[:, sl * TT:(sl + 1) * TT]
                    mb = 4 + nxt("mb", 2)
                    tr.op("pe", [("sq", sl)], [("ps", mb)],
                          lambda e: e.matmul(ps[mb][:], lhsT=bd_bf[:], rhs=sqs, start=True, stop=True))
                    r = nxt("rstd", 2)
                    rs_ = rstd_t[:, r * TT:(r + 1) * TT]
                    tr.op("act", [("ps", mb)], [("rstd", r)],
                          lambda e: e.activation(out=rs_, in_=ps[mb][:], func=AF.Ln, scale=1.0 / 64,
                                                 bias=eps_t[:, 0:1]))
                    tr.op("act", [("rstd", r)], [("rstd", r)],
                          lambda e: e.activation(out=rs_, in_=rs_, func=AF.Exp, scale=-0.5))
                    tr.op("dve", [("ps", pb), ("rstd", r)], [(dkey, c, tt)],
                          lambda e: e.scalar_tensor_tensor(
                              out=dst_t[:, c * S + tt * TT: c * S + (tt + 1) * TT],
                              in0=ps[pb][:], scalar=gap, in1=rs_, op0=ALU.mult, op1=ALU.mult))

                pend = None
                for it in range(4):
                    wap, wkey = ringA.acquire()
                    for c2 in range(2):
                        c = (it % 2) * 2 + c2
                        isq = it < 2
                        dst_t = qn if isq else kn
                        dkey = "qn" if isq else "kn"
                        gap = gq8[:, 0:1] if isq else pvec[:, 33:34]
                        for tt in range(NTT):
                            pb = nxt("pb", 4)

                            def emit_p(e, pb=pb, tt=tt, c2=c2, wap=wap):
                                ins = None
                                for kc in range(NKC):
                                    ins = e.matmul(
                                        ps[pb][:],
                                        lhsT=wap[:, c2 * 1024 + kc * 128: c2 * 1024 + (kc + 1) * 128],
                                        rhs=hs(kc, tt), start=(kc == 0), stop=(kc == NKC - 1))
                                return ins
                            tr.op("pe", [wkey] + [("h", kc, tt) for kc in range(NKC)],
                                  [("ps", pb)], emit_p)
                            sl = nxt("sq", 4)
                            sqs = sq_t[:, sl * TT:(sl + 1) * TT]
                            tr.op("act", [("ps", pb)], [("sq", sl)],
                                  lambda e, sqs=sqs, pb=pb: e.activation(
                                      out=sqs, in_=ps[pb][:], func=AF.Square))
                            if pend is not None:
                                qk_stage_b(*pend)
                            pend = (pb, sl, dst_t, dkey, gap, c, tt)
                    ringA.release()
                qk_stage_b(*pend)

                for j in range(16):
                    pb = nxt("pb", 4)

                    def emit_v(e, pb=pb, j=j):
                        ins = None
                        for kc in range(NKC):
                            ins = e.matmul(ps[pb][:], lhsT=hs_tok(kc, j * 128, 128),
                                           rhs=winv[:, kc * 520: kc * 520 + 512],
                                           start=(kc == 0), stop=(kc == NKC - 1))
                        return ins
                    hr = [("h", kc, j // 4) for kc in range(NKC)]
                    tr.op("pe", ["winv"] + hr, [("ps", pb)], emit_v)

                    def emit_f(e, j=j):
                        ins = None
                        for kc in range(NKC):
                            ins = e.matmul(ps[6][:, j * 8:(j + 1) * 8],
                                           lhsT=hs_tok(kc, j * 128, 128),
                                           rhs=winv[:, kc * 520 + 512: kc * 520 + 520],
                                           start=(kc == 0), stop=(kc == NKC - 1))
                        return ins
                    tr.op("pe", ["winv"] + hr, [("ps", 6)], emit_f)
                    for s_ in range(2):
                        tr.op("dve", [("ps", pb), "Vones"], [("V", j)],
                              lambda e, pb=pb, j=j, s_=s_: e.tensor_copy(
                                  out=V5[:, j, :, 2 * s_, :],
                                  in_=ps[pb][:].rearrange("p (c s e) -> p c s e", c=4, s=2)[:, :, s_, :]))

                def fzs(k):
                    if k in (1, 2):
                        return fzp[:, (k - 1) * 128: k * 128]
                    kk = {0: 0, 3: 1, 4: 2, 5: 3, 6: 4}[k]
                    return rc[:, kk * 128:(kk + 1) * 128]
                tr.op("dve", [("ps", 6)], [("fz", 0)], lambda e: e.tensor_tensor(
                    out=fzs(0), in0=ps[6][:, 0:128], in1=bfb[:], op=ALU.add))
                tr.op("dve", [("fz", 0)], [("fz", 1)], lambda e: e.tensor_scalar(
                    out=fzs(1), in0=fzs(0), scalar1=-1.0, scalar2=None, op0=ALU.mult))
                tr.op("dve", [("fz", 0), ("fz", 1)], [("fz", 1)], lambda e: e.tensor_tensor(
                    out=fzs(1), in0=fzs(1), in1=fzs(0), op=ALU.max))
                tr.op("act", [("fz", 1)], [("fz", 2)], lambda e: e.activation(
                    out=fzs(2), in_=fzs(1), func=AF.Exp, scale=-1.0))
                tr.op("act", [("fz", 2)], [("fz", 2)], lambda e: e.activation(
                    out=fzs(2), in_=fzs(2), func=AF.Ln, bias=one_t[:, 0:1], scale=1.0))
                tr.op("dve", [("fz", 0)], [("fz", 3)], lambda e: e.tensor_scalar_min(
                    out=fzs(3), in0=fzs(0), scalar1=0.0))
                tr.op("dve", [("fz", 3), ("fz", 2)], [("fz", 3)], lambda e: e.tensor_sub(
                    out=fzs(3), in0=fzs(3), in1=fzs(2)))
                tr.op("pe", [("fz", 3)], [("ps", 6)], lambda e: e.matmul(
                    ps[6][:, 0:128], lhsT=U_f[:], rhs=fzs(3), start=True, stop=True))
                tr.op("pe", [("fz", 3)], [("ps", 6)], lambda e: e.matmul(
                    ps[6][:, 128:256], lhsT=ones_f[:], rhs=fzs(3), start=True, stop=True))
                tr.op("pe", [("fz", 3)], [("ps", 6)], lambda e: e.matmul(
                    ps[6][:, 256:384], lhsT=H_f[:], rhs=fzs(3), start=True, stop=True))
                tr.op("dve", [("ps", 6)], [("fz", 4), ("fz", 5), ("fz", 6)],
                      lambda e: e.tensor_copy(out=rc[:, 256:640], in_=ps[6][:, 0:384]))
                tr.op("dve", [("fz", 0)], [("fz", 0)], lambda e: e.memset(rc[:, 0:8], 0.0))
                for j in range(1, 16):
                    tr.op("dve", [("fz", 0), ("fz", 5)], [("fz", 0)],
                          lambda e, j=j: e.tensor_tensor(
                              out=rc[:, j * 8:(j + 1) * 8], in0=rc[:, (j - 1) * 8: j * 8],
                              in1=rc[:, 384 + (j - 1) * 8: 384 + j * 8], op=ALU.add))
                tr.op("dve", [("fz", 4), ("fz", 0)], [("fz", 1)],
                      lambda e: e.scalar_tensor_tensor(
                          out=fzs(1), in0=fzs(4), scalar=-1.0, in1=fzs(0),
                          op0=ALU.mult, op1=ALU.subtract))
                tr.op("dve", [("fz", 0)], [("fz", 2)], lambda e: e.tensor_copy(
                    out=fzs(2), in_=fzs(0)))
                negF3 = fzs(1).rearrange("p (j h) -> p j h", h=8)
                Fref3 = fzs(2).rearrange("p (j h) -> p j h", h=8)

                wp0, wk0 = ringA.acquire()
                wp1, wk1 = ringA.acquire()
                alias_fence([("fz", k) for k in (0, 3, 4, 5, 6)], ["tmpP0"])
                tr.op("dve", ["winv", "fence"], ["winv", "fence"] + [("mixed", g) for g in range(4)],
                      lambda e: e.memset(fence_t[:, 0:1], 0.0))
                SCAN_ENG = {0: "dve", 1: "dve", 2: "pool", 3: "pool"}

                def pool_stage_a(n):
                    tt, g = divmod(n, 4)
                    w = 2 << g
                    wap, wkey = (wp0, wk0) if g < 2 else (wp1, wk1)
                    off = (g % 2) * 1024
                    pb = nxt("pb", 4)

                    def emit_pv(e):
                        ins = None
                        for kc in range(NKC):
                            ins = e.matmul(
                                ps[pb][:], lhsT=wap[:, off + kc * 128: off + (kc + 1) * 128],
                                rhs=hs(kc, tt), start=(kc == 0), stop=(kc == NKC - 1))
                        return ins
                    tr.op("pe", [wkey] + [("h", kc, tt) for kc in range(NKC)], [("ps", pb)], emit_pv)
                    sl = n % 2
                    pv = pvg[:, sl * 528:(sl + 1) * 528]
                    tr.op("act", [("ps", pb)], [("pvg", sl)],
                          lambda e: e.copy(out=pv[:, 16:528], in_=ps[pb][:]))

                qzf = qz[:, :].bitcast(F32)

                def pool_stage_a2(n):
                    tt, g = divmod(n, 4)
                    w = 2 << g
                    sl = n % 2
                    pv = pvg[:, sl * 528:(sl + 1) * 528]
                    en = "pool" if g == 3 else "dve"
                    if g == 3:
                        tbuf = [rc[:, 0:528], qzf[:, 0:528]]
                        tkey = ["tmpP0", "tmpP1"]
                    else:
                        tbuf = [ovl2[:, 1056:1584], ovl2[:, 1584:2112]]
                        tkey = [("tmp", 0), ("tmp", 1)]
                    if tt == 0:
                        tr.op(en, [], [("pvgh", sl)], lambda e: e.memset(pv[:, 0:16], 0.0))
                    else:
                        tr.op(en, [("halo", g)], [("pvgh", sl)],
                              lambda e: e.tensor_copy(out=pv[:, 0:16], in_=halo[:, g * 16:(g + 1) * 16]))
                    tr.op(en, [("pvg", sl)], [("halo", g)],
                          lambda e: e.tensor_copy(out=halo[:, g * 16:(g + 1) * 16], in_=pv[:, 512:528]))
                    src, skeys = pv, [("pvg", sl), ("pvgh", sl)]
                    lo, k, ti = 0, 1, 0
                    while k < w:
                        dst = tbuf[ti]
                        tr.op(en, skeys, [tkey[ti]],
                              lambda e, dst=dst, src=src, lo=lo, k=k: e.tensor_tensor(
                                  out=dst[:, lo + k:528], in0=src[:, lo + k:528],
                                  in1=src[:, lo:528 - k], op=ALU.add))
                        src, skeys = dst, [tkey[ti]]
                        lo += k
                        k *= 2
                        ti ^= 1
                    psl = n % 2
                    pl = pooled[:, psl * TT:(psl + 1) * TT]
                    if g == 3:
                        oth, okey = tbuf[ti], tkey[ti]
                        tr.op(en, skeys, [okey], lambda e: e.tensor_scalar(
                            out=oth[:, 16:528], in0=src[:, 16:528], scalar1=1.0 / w, scalar2=0.0,
                            op0=ALU.mult, op1=ALU.add))
                        tr.op(en, [okey, ("pvg", sl)], [("pooled", psl)], lambda e: e.tensor_tensor(
                            out=pl, in0=oth[:, 16:528], in1=pv[:, 16:528], op=ALU.subtract))
                        if tt == 0:
                            tr.op(en, skeys, skeys, lambda e: e.tensor_tensor(
                                out=src[:, 0:w - 1], in0=src[:, 16:16 + w - 1],
                                in1=invc[:, 0:w - 1], op=ALU.mult))
                            tr.op(en, skeys + [("pvg", sl), ("pooled", psl)], [("pooled", psl)],
                                  lambda e: e.tensor_tensor(
                                      out=pl[:, 0:w - 1], in0=src[:, 0:w - 1],
                                      in1=pv[:, 16:16 + w - 1], op=ALU.subtract))
                        return
                    tr.op("dve", skeys + [("pvg", sl)], [("pooled", psl)],
                          lambda e: e.scalar_tensor_tensor(
                              out=pl, in0=src[:, 16:528], scalar=1.0 / w, in1=pv[:, 16:528],
                              op0=ALU.mult, op1=ALU.subtract))
                    if tt == 0:
                        tr.op("dve", skeys, ["fix"],
                              lambda e: e.tensor_tensor(
                                  out=fix[:, 0:w - 1], in0=src[:, 16:16 + w - 1],
                                  in1=invc[:, 0:w - 1], op=ALU.mult))
                        tr.op("dve", ["fix", ("pvg", sl), ("pooled", psl)], [("pooled", psl)],
                              lambda e: e.tensor_tensor(
                                  out=pl[:, 0:w - 1], in0=fix[:, 0:w - 1],
                                  in1=pv[:, 16:16 + w - 1], op=ALU.subtract))

                sqslot = {}

                def pool_stage_b(n):
                    tt, g = divmod(n, 4)
                    psl = n % 2
                    pl = pooled[:, psl * TT:(psl + 1) * TT]
                    mb = 4 + nxt("mb", 2)
                    tr.op("pe", [("pooled", psl)], [("ps", mb)],
                          lambda e: e.matmul(ps[mb][:], lhsT=poolw[:, g * 128:(g + 1) * 128], rhs=pl,
                                             start=True, stop=True))
                    mx = mixed[:, g * TT:(g + 1) * TT]
                    tr.op("act", [("ps", mb)], [("mixed", g)],
                          lambda e: e.activation(out=mx, in_=ps[mb][:], func=AF.Copy,
                                                 scale=pvec[:, 24 + g: 25 + g]))
                    sl2 = nxt("sq", 4)
                    sqslot[n] = sl2
                    sqs = sq_t[:, sl2 * TT:(sl2 + 1) * TT]
                    tr.op("act", [("ps", mb)], [("sq", sl2)],
                          lambda e: e.activation(out=sqs, in_=ps[mb][:], func=AF.Square,
                                                 scale=pvec[:, 24 + g: 25 + g]))

                def pool_stage_b2(n):
                    tt, g = divmod(n, 4)
                    sl2 = sqslot[n]
                    sqs = sq_t[:, sl2 * TT:(sl2 + 1) * TT]
                    tr.op("pe", [("sq", sl2)], [("ps", 6)],
                          lambda e: e.matmul(ps[6][:], lhsT=ones_bf[:], rhs=sqs,
                                             start=(g == 0), stop=(g == 3)))
                    if g != 3:
                        return
                    r = nxt("rstd", 2)
                    rs_ = rstd_t[:, r * TT:(r + 1) * TT]
                    tr.op("act", [("ps", 6)], [("rstd", r)],
                          lambda e: e.activation(out=rs_, in_=ps[6][:], func=AF.Ln, scale=1.0 / 512,
                                                 bias=eps_t[:, 0:1]))
                    tr.op("act", [("rstd", r)], [("rstd", r)],
                          lambda e: e.activation(out=rs_, in_=rs_, func=AF.Exp, scale=-0.5))
                    for g_ in range(4):
                        tr.op("dve", [("mixed", g_), ("rstd", r)], [("h", g_, tt)],
                              lambda e, g_=g_: e.scalar_tensor_tensor(
                                  out=hs(g_, tt), in0=mixed[:, g_ * TT:(g_ + 1) * TT],
                                  scalar=pvec[:, 28 + g_: 29 + g_], in1=rs_,
                                  op0=ALU.mult, op1=ALU.mult))

                for n in range(19):
                    if n < 16:
                        pool_stage_a(n)
                    if 0 <= n - 1 < 16:
                        pool_stage_a2(n - 1)
                    if 0 <= n - 3 < 16:
                        pool_stage_b2(n - 3)
                    if 0 <= n - 2 < 16:
                        pool_stage_b(n - 2)
                ringA.release()
                ringA.release()

                alias_fence([("pvg", 0), ("pvg", 1), ("pvgh", 0), ("pvgh", 1), ("tmp", 0), ("tmp", 1)],
                            [("on", c_, s_) for c_ in range(4) for s_ in range(2)])

                alias_fence([("pooled", 0), ("pooled", 1)],
                            [("PT", p_, s_) for p_ in range(3) for s_ in range(2)])
                alias_fence([("fz", k) for k in (0, 3, 4, 5, 6)] + ["tmpP0"],
                            [("rc", r_, s_) for r_ in range(2) for s_ in range(2)])
                tr.op("dve", ["tmpP1"], ["tmpP1", ("qz", 0, 0), ("qz", 0, 1), ("qz", 1, 0), ("qz", 1, 1)],
                      lambda e: e.memset(qz[:], 0.0))
                steps = []
                gcc = 0
                for g in range(4):
                    for c in range(4):
                        par = gcc % 2
                        gcc += 1
                        first_step = len(steps)
                        for j in range(4 * g + 4):
                            for hq in range(2):
                                i0 = 4 * g + 2 * hq
                                i1 = i0 + 1
                                if j > i1:
                                    continue
                                lo = max(i0, j)
                                steps.append(dict(g=g, c=c, par=par, j=j, i1=i1, lo=lo, hq=hq,
                                                  nb=i1 - lo + 1, qoff=(lo - 4 * g) * 128,
                                                  pre=False, post=False, gpost=False, first=False))
                        steps[first_step]["pre"] = True
                        steps[first_step]["first"] = True
                        steps[-1]["post"] = True
                        steps[-1]["gpost"] = (c == 3)

                def emit_pre(st):
                    g, c, par = st["g"], st["c"], st["par"]
                    for s_ in range(2):
                        p0 = s_ * 64
                        tr.op("dve", [("qn", c, g)], [("qz", par, s_)],
                              lambda e, p0=p0, s_=s_: e.tensor_copy(
                                  out=qz[p0:p0 + 64, par * 1024 + s_ * 512: par * 1024 + (s_ + 1) * 512],
                                  in_=qn[p0:p0 + 64, c * S + g * TT: c * S + (g + 1) * TT]))
                    for s_ in range(2):
                        h = 2 * c + s_
                        for hq in range(2):
                            i1 = 4 * g + 2 * hq + 1
                            bo = par * 64 + s_ * 32 + hq * 16
                            tr.op("pool", [("fz", 1), ("fz", 2)], [("BT", par, s_, hq)],
                                  lambda e, i1=i1, h=h, bo=bo: e.tensor_scalar(
                                      out=BT[:, bo: bo + i1 + 1],
                                      in0=negF3[:, 0:i1 + 1, h], scalar1=1.0,
                                      scalar2=Fref3[:, i1, h:h + 1],
                                      op0=ALU.mult, op1=ALU.add))

                def emit_qk(n, st):
                    c, par, j, qoff, ncol = st["c"], st["par"], st["j"], st["qoff"], st["nb"] * 128
                    sbk = n % 3
                    qz3 = qz[:, par * 1024:(par + 1) * 1024].rearrange("p (s t) -> p s t", s=2)
                    diag = (st["lo"] == j)

                    def emit_s(e):
                        ins = e.matmul(
                            ps[sbk][:, 0:2 * ncol],
                            lhsT=kn[:, c * S + j * 128: c * S + (j + 1) * 128],
                            rhs=qz3[:, :, qoff:qoff + ncol], start=True, stop=not diag,
                            skip_group_check=True)
                        if diag:
                            for s_ in range(2):
                                ins = e.matmul(
                                    ps[sbk][:, s_ * ncol: s_ * ncol + 128], lhsT=ident_bf[:],
                                    rhs=mneg_bf[:], start=False, stop=(s_ == 1),
                                    skip_group_check=True)
                        return ins
                    tr.op("pe", [("kn", c, j // 4), ("qz", par, 0), ("qz", par, 1)], [("ps", sbk)],
                          emit_s)

                def emit_exp(n, st):
                    g, par, j, lo, nb, hq = st["g"], st["par"], st["j"], st["lo"], st["nb"], st["hq"]
                    ncol = nb * 128
                    sbk = psl = n % 3
                    for s_ in range(2):
                        co = s_ * ncol
                        pts = PT[:, psl * TT + co: psl * TT + co + ncol]
                        bo = par * 64 + s_ * 32 + hq * 16 + j
                        tr.op("act", [("ps", sbk), ("BT", par, s_, hq)], [("PT", psl, s_)],
                              lambda e, pts=pts, co=co, bo=bo: e.activation(
                                  out=pts, in_=ps[sbk][:, co:co + ncol], func=AF.Exp,
                                  bias=BT[:, bo:bo + 1], scale=1.0))

                def emit_pv(n, st):
                    c, par, j, qoff, nb = st["c"], st["par"], st["j"], st["qoff"], st["nb"]
                    ncol = nb * 128
                    psl = n % 3
                    TA = 3 + 2 * par
                    for s_ in range(2):
                        Tb = TA + s_
                        tr.op("pe", [("PT", psl, s_), ("V", j)],
                              [("ps", Tb)],
                              lambda e, Tb=Tb, s_=s_: e.matmul(
                                  ps[Tb][:, qoff:qoff + ncol],
                                  lhsT=V5[:, j, c, s_:s_ + 2, :],
                                  rhs=PT[:, psl * TT + s_ * ncol: psl * TT + (s_ + 1) * ncol],
                                  start=st["first"], stop=(j == st["i1"]), skip_group_check=True))

                def emit_post(st):
                    g, c, par = st["g"], st["c"], st["par"]
                    assert not gpend, (g, c, gpend)
                    TA = 3 + 2 * par
                    TB = TA + 1
                    rcA = rc[0:64, par * TT:(par + 1) * TT]
                    rcB = rc[64:128, par * TT:(par + 1) * TT]
                    onc = ovl2[:, c * TT:(c + 1) * TT]
                    tr.op("dve", [("ps", TA)], [("rc", par, 0)],
                          lambda e: e.reciprocal(out=rcA, in_=ps[TA][64:128, :]))
                    tr.op("dve", [("ps", TB)], [("rc", par, 1)],
                          lambda e: e.reciprocal(out=rcB, in_=ps[TB][0:64, :]))
                    tr.op("dve", [("ps", TA), ("rc", par, 0)], [("on", c, 0)],
                          lambda e: e.tensor_tensor(
                              out=onc[0:64, :], in0=ps[TA][0:64, :], in1=rcA, op=ALU.mult))
                    tr.op("dve", [("ps", TB), ("rc", par, 1)], [("on", c, 1)],
                          lambda e: e.tensor_tensor(
                              out=onc[64:128, :], in0=ps[TB][64:128, :], in1=rcB, op=ALU.mult))
                    if st["gpost"]:
                        gpend.append(g)

                def emit_gpost(g):
                    for c_ in range(4):
                        onc_ = ovl2[:, c_ * TT:(c_ + 1) * TT]
                        sl = nxt("sq", 4)
                        sqs = sq_t[:, sl * TT:(sl + 1) * TT]
                        tr.op("pool", [("on", c_, 0), ("on", c_, 1)], [("sq", sl)],
                              lambda e, sqs=sqs, onc_=onc_: e.tensor_tensor(
                                  out=sqs, in0=onc_, in1=onc_, op=ALU.mult))
                        tr.op("pe", [("sq", sl)], [("ps", 7)],
                              lambda e, sqs=sqs, c_=c_: e.matmul(
                                  ps[7][:], lhsT=ones_bf[:], rhs=sqs, start=(c_ == 0), stop=(c_ == 3)))
                    r = nxt("rstd", 2)
                    rs_ = rstd_t[:, r * TT:(r + 1) * TT]
                    tr.op("act", [("ps", 7)], [("rstd", r)],
                          lambda e: e.activation(
                              out=rs_, in_=ps[7][:], func=AF.Ln, scale=1.0 / 512, bias=eps_t[:, 0:1]))
                    tr.op("act", [("rstd", r)], [("rstd", r)],
                          lambda e: e.activation(out=rs_, in_=rs_, func=AF.Exp, scale=-0.5))
                    for c_ in range(4):
                        onc_ = ovl2[:, c_ * TT:(c_ + 1) * TT]
                        tr.op("dve", [("on", c_, 0), ("on", c_, 1), ("rstd", r)], [("h", 4 + c_, g)],
                              lambda e, onc_=onc_, c_=c_: e.scalar_tensor_tensor(
                                  out=hs(4 + c_, g), in0=onc_, scalar=pvec[:, 34 + c_: 35 + c_],
                                  in1=rs_, op0=ALU.mult, op1=ALU.mult))

                NS = len(steps)
                gpend = []
                gdue = {}
                GDEFER = 12
                pres = [n for n in range(NS) if steps[n]["pre"]]
                emit_pre(steps[pres[0]])
                npre = 1
                for n in range(NS + 2):
                    if n < NS:
                        emit_qk(n, steps[n])
                    if 0 <= n - 1 < NS:
                        emit_exp(n - 1, steps[n - 1])
                    if 0 <= n - 2 < NS:
                        emit_pv(n - 2, steps[n - 2])
                        if steps[n - 2]["post"]:
                            emit_post(steps[n - 2])
                            for g_ in gpend:
                                gdue.setdefault(g_, n + GDEFER)
                    for g_ in list(gpend):
                        if n >= gdue[g_]:
                            emit_gpost(g_)
                            gpend.remove(g_)
                    if n < NS and steps[n]["pre"] and npre < len(pres):
                        emit_pre(steps[pres[npre]])
                        npre += 1

                for g_ in list(gpend):
                    emit_gpost(g_)
                def wo_step(dc, d2, tt, wap, wkey):
                    pb = nxt("pb", 3)

                    def emit_o(e):
                        ins = None
                        for kc in range(NKC):
                            ins = e.matmul(
                                ps[pb][:],
                                lhsT=wap[:, d2 * 1024 + kc * 128: d2 * 1024 + (kc + 1) * 128],
                                rhs=hs(kc, tt), start=(kc == 0), stop=(kc == NKC - 1))
                        return ins
                    tr.op("pe", [wkey] + [("h", kc, tt) for kc in range(NKC)], [("ps", pb)], emit_o)
                    tr.op("dve", [("ps", pb), ("x", dc, tt)], [("x", dc, tt)],
                          lambda e: e.tensor_tensor(
                              out=xs(dc, tt), in0=ps[pb][:], in1=xs(dc, tt), op=ALU.add))

                for it in range(3):
                    wap, wkey = ringA.acquire()
                    for d2 in range(2):
                        for tt in range(NTT):
                            wo_step(it * 2 + d2, d2, tt, wap, wkey)
                    ringA.release()
                wap, wkey = ringA.acquire()
                for tt in range(NTT):
                    for d2 in range(2):
                        wo_step(6 + d2, d2, tt, wap, wkey)
                    if tt >= 1:
                        rmsnorm_x(16, [tt - 1])
                ringA.release()
                rmsnorm_x(16, [NTT - 1])
                tr.barrier()

        def load_x(s, kc):
            tr.dma("sp", f"xld{kc}", [], [("x", kc, tt) for tt in range(NTT)],
                   lambda e: e.dma_start(out=x_sb[:, kc * S:(kc + 1) * S], in_=xT[s, kc]))

        for tt in range(NTT):
            for kc in range(NKC):
                tr.add_dma_sem(f"xi{kc}_{tt}")
                tr.dma("sp", f"xi{kc}_{tt}", [], [("x", kc, tt)],
                       lambda e, kc=kc, tt=tt: e.dma_start(
                           out=xs(kc, tt), in_=xT[0, kc][:, tt * TT:(tt + 1) * TT]))
        for s in range(NSEQ):
            def after_chunk(kc, s=s):
                tr.dma("sp", f"yst{kc}", [("x", kc, tt) for tt in range(NTT)], [("y", s, kc)],
                       lambda e: e.dma_start(out=yT[s, kc], in_=x_sb[:, kc * S:(kc + 1) * S]))
                if s + 1 < NSEQ:
                    load_x(s + 1, kc)
            ffn(0)
            mixer()
            ffn(16, after_chunk, prenormed=True)
        tr.barrier(["sp"], all_sems=True)
    return nc


def _prep_weights(inp):
    f = lambda a: np.ascontiguousarray(np.asarray(a, dtype=np.float32))

    def lay_gu(Wg, Wu):
        g = f(Wg).reshape(8, 128, 22, 128).transpose(2, 1, 0, 3)
        u = f(Wu).reshape(8, 128, 22, 128).transpose(2, 1, 0, 3)
        return f(np.stack([g, u], axis=2).reshape(22, 128, 2048))

    def lay_d(Wd):
        return f(f(Wd).reshape(2, FPH, 128, 8, 128).transpose(0, 3, 2, 1, 4).reshape(16, 128, FPH * 128))

    w_in = f(inp["w_in"])
    wi = w_in[:, 0:1536].reshape(8, 128, 12, 128).transpose(2, 1, 0, 3)
    order = [4, 5, 6, 7, 8, 9, 10, 11, 0, 1, 2, 3]
    wi = wi[order].reshape(6, 2, 128, 8, 128).transpose(0, 2, 1, 3, 4).reshape(6, 128, 2048)
    winv = w_in[:, 1536:2056].reshape(8, 128, 520).transpose(1, 0, 2)
    wo = f(inp["w_out"]).reshape(8, 128, 8, 128).transpose(2, 1, 0, 3)
    wo = wo.reshape(4, 2, 128, 8, 128).transpose(0, 2, 1, 3, 4).reshape(4, 128, 2048)
    poolw = f(inp["pool_w"]).transpose(1, 0, 2).reshape(128, 512)
    pvec = np.zeros((128, 40), np.float32)
    pvec[:, 0:8] = f(inp["ffn1_norm"]).reshape(8, 128).T
    pvec[:, 8:16] = f(inp["mix_norm"]).reshape(8, 128).T
    pvec[:, 16:24] = f(inp["ffn2_norm"]).reshape(8, 128).T
    pvec[:, 24:28] = f(inp["pool_scale"]).reshape(4, 128).T
    pvec[:, 28:32] = f(inp["out_norm_pool"]).reshape(4, 128).T
    pvec[:, 32] = np.tile(f(inp["q_norm"]), 2)
    pvec[:, 33] = np.tile(f(inp["k_norm"]), 2)
    pvec[:, 34:38] = f(inp["out_norm_attn"]).reshape(4, 128).T
    bfb = np.broadcast_to(np.tile(f(inp["b_forget"]), 16)[None, :], (128, 128))
    return {
        "wgu1": lay_gu(inp["ffn1_w_gate"], inp["ffn1_w_up"]),
        "wgu2": lay_gu(inp["ffn2_w_gate"], inp["ffn2_w_up"]),
        "wd1": lay_d(inp["ffn1_w_down"]), "wd2": lay_d(inp["ffn2_w_down"]),
        "win": f(wi), "winv": f(winv), "wout": f(wo), "poolw": f(poolw),
        "pvec": pvec, "bfb": f(bfb),
    }


_NC_CACHE = {}


def kernel(**inputs):
    x = np.asarray(inputs["x"], dtype=np.float32)
    w = _prep_weights(inputs)
    if "nc" not in _NC_CACHE:
        _NC_CACHE["nc"] = build_nc()
    nc = _NC_CACHE["nc"]
    in_maps = []
    for c in range(NCORES):
        xc = x[c * NSEQ:(c + 1) * NSEQ].transpose(0, 2, 1).reshape(NSEQ, NKC, 128, S)
        m = {"xT": np.ascontiguousarray(xc)}
        m.update(w)
        in_maps.append(m)
    res = run_bass_kernel_spmd(nc, in_maps, core_ids=list(range(NCORES)))
    out = np.empty((NCORES * NSEQ, S, D), np.float32)
    for c in range(NCORES):
        y = np.asarray(res.results[c]["yT"]).reshape(NSEQ, D, S)
        out[c * NSEQ:(c + 1) * NSEQ] = y.transpose(0, 2, 1)
    return out
```

```python
import contextlib
import numpy as np
import concourse.bass as bass
import concourse.mybir as mybir
from concourse.bass_utils import run_bass_kernel_spmd

F32 = mybir.dt.float32
BF16 = mybir.dt.bfloat16
ALU = mybir.AluOpType
AF = mybir.ActivationFunctionType

NCORES = 8
D = 1024
S = 2048
DFF = 2816
NKC = 8
TT = 512
NTT = 4
FPH = 11
EPS = 1e-6
NSEQ = 2


class Tracker:
    def __init__(self, nc, es):
        self.nc = nc
        self.eng = {"pe": nc.tensor, "act": nc.scalar, "dve": nc.vector,
                    "pool": nc.gpsimd, "sp": nc.sync}
        self.semobj = {}
        self.val = {}
        for k in self.eng:
            self.semobj["e_" + k] = es.enter_context(nc.semaphore("s_" + k))
            self.val["e_" + k] = 0
        self.es = es
        self.known = {}
        self.lastw = {}
        self.readers = {}

    def add_dma_sem(self, name):
        self.semobj[name] = self.es.enter_context(self.nc.semaphore("d_" + name))
        self.val[name] = 0

    def _deps(self, reads, writes):
        d = {}

        def add(tok):
            if tok is None:
                return
            s, v = tok
            if d.get(s, 0) < v:
                d[s] = v
        for r in reads:
            add(self.lastw.get(r))
        for w in writes:
            add(self.lastw.get(w))
            for s, v in self.readers.get(w, {}).items():
                add((s, v))
        return d

    def _wait(self, e, d):
        for s, v in d.items():
            if e == "pe" and s == "e_pe":
                continue
            if self.known.get((e, s), 0) >= v:
                continue
            self.eng[e].wait_ge(self.semobj[s], v)
            self.known[(e, s)] = v

    def _commit(self, tok, reads, writes):
        s, v = tok
        for r in reads:
            rd = self.readers.setdefault(r, {})
            if rd.get(s, 0) < v:
                rd[s] = v
        for w in writes:
            self.lastw[w] = tok
            self.readers[w] = {}

    def op(self, e, reads, writes, emit):
        self._wait(e, self._deps(reads, writes))
        ins = emit(self.eng[e])
        s = "e_" + e
        self.val[s] += 1
        ins.then_inc(self.semobj[s], 1)
        self._commit((s, self.val[s]), reads, writes)

    def dma(self, e, semname, reads, writes, emit):
        self._wait(e, self._deps(reads, writes))
        ins = emit(self.eng[e])
        self.val[semname] += 16
        ins.then_inc(self.semobj[semname], 16)
        self._commit((semname, self.val[semname]), reads, writes)

    def barrier(self, engines=None, all_sems=False):
        d = {s: v for s, v in self.val.items()
             if v > 0 and (all_sems or s.startswith("e_") or s in ("winv", "cstp", "cst0", "cst2"))}
        for e in (engines or self.eng):
            self._wait(e, d)


class Ring:
    def __init__(self, tr, name, tile, nslots, slotw, items):
        self.tr, self.name, self.tile = tr, name, tile
        self.nslots, self.slotw, self.items = nslots, slotw, items
        self.n_issued = self.n_acq = self.n_rel = 0
        for s in range(nslots):
            tr.add_dma_sem(f"{name}{s}")

    def prefetch(self):
        while self.n_issued < len(self.items) and self.n_issued < self.n_rel + self.nslots:
            i = self.n_issued
            slot = i % self.nslots
            src = self.items[i]
            w = src.shape[-1]
            dst = self.tile[:, slot * self.slotw: slot * self.slotw + w]
            self.tr.dma("pool", f"{self.name}{slot}", [], [(self.name, slot)],
                        lambda g, dst=dst, src=src: g.dma_start(out=dst, in_=src))
            self.n_issued += 1

    def acquire(self):
        self.prefetch()
        i = self.n_acq
        assert self.n_issued > i, (self.name, i, self.n_issued, self.n_rel)
        self.n_acq += 1
        slot = i % self.nslots
        return self.tile[:, slot * self.slotw:(slot + 1) * self.slotw], (self.name, slot)

    def release(self):
        self.n_rel += 1
        self.prefetch()


def build_nc():
    nc = bass.Bass("TRN2", target_bir_lowering=False)

    def din(name, shape):
        return nc.dram_tensor(name, shape, F32, kind="ExternalInput").ap()

    xT = din("xT", [NSEQ, NKC, 128, S])
    wgu = [din("wgu1", [22, 128, 2048]), din("wgu2", [22, 128, 2048])]
    wd = [din("wd1", [16, 128, FPH * 128]), din("wd2", [16, 128, FPH * 128])]
    win = din("win", [6, 128, 2048])
    winv_d = din("winv", [128, NKC, 520])
    wout = din("wout", [4, 128, 2048])
    poolw_d = din("poolw", [128, 512])
    pvec_d = din("pvec", [128, 40])
    bfb_d = din("bfb", [128, 128])
    yT = nc.dram_tensor("yT", [NSEQ, NKC, 128, S], F32, kind="ExternalOutput").ap()

    es = contextlib.ExitStack()
    with es:
        def sb(name, shape, dt):
            return es.enter_context(nc.sbuf_tensor("sb_" + name, shape, dt))
        uid = [0]

        tr = Tracker(nc, es)
        for nm in [f"xld{k}" for k in range(8)] + [f"yst{k}" for k in range(8)] + ["cst0", "cst2", "cstp", "winv"]:
            tr.add_dma_sem(nm)

        x_sb = sb("x_sb", [128, NKC * S], F32)
        h_sb = sb("h_sb", [128, NKC * S], BF16)
        rA_t = sb("rA", [128, 2 * 2048], BF16)
        rB_t = sb("rB", [128, 2 * FPH * 128], BF16)
        sq_t = sb("sq", [128, 4 * TT], BF16)
        rstd_t = sb("rstd", [128, 2 * TT], F32)
        pvec = sb("pvec", [128, 40], F32)
        ones_bf = sb("ones_bf", [128, 128], BF16)
        bd_bf = sb("bd_bf", [128, 128], BF16)
        ident_bf = sb("ident_bf", [128, 128], BF16)
        mneg_bf = sb("mneg_bf", [128, 128], BF16)
        zero_bf = sb("zero_bf", [128, 128], BF16)
        U_f = sb("U_f", [128, 128], F32)
        ones_f = sb("ones_f", [128, 128], F32)
        H_f = sb("H_f", [128, 128], F32)
        bfb = sb("bfb", [128, 128], F32)
        poolw = sb("poolw", [128, 512], BF16)
        invc = sb("invc", [128, 16], F32)
        gq8 = sb("gq8", [128, 1], F32)

        ps = [es.enter_context(nc.psum_tensor(f"ps{b}", [128, 512], F32)) for b in range(8)]

        def xs(kc, tt):
            return x_sb[:, kc * S + tt * TT: kc * S + (tt + 1) * TT]

        def hs(kc, tt):
            return h_sb[:, kc * S + tt * TT: kc * S + (tt + 1) * TT]

        def hs_tok(kc, t0, n):
            return h_sb[:, kc * S + t0: kc * S + t0 + n]

        V = nc.vector
        G = nc.gpsimd
        tr.dma("sp", "cst0", [], ["c0"], lambda e: e.dma_start(out=pvec[:], in_=pvec_d))
        tr.dma("sp", "cst2", [], ["c2"], lambda e: e.dma_start(out=bfb[:], in_=bfb_d))
        tr.dma("pool", "cstp", [], ["c3"], lambda e: e.dma_start(out=poolw[:], in_=poolw_d))
        tr.op("dve", [], ["k0"], lambda e: e.memset(ones_bf[:], 1.0))
        tr.op("dve", [], ["k1"], lambda e: e.memset(ones_f[:], 1.0))
        tr.op("dve", [], ["k2"], lambda e: e.memset(bd_bf[:], 0.0))
        tr.op("dve", ["k2"], ["k2"], lambda e: e.memset(bd_bf[0:64, 0:64], 1.0))
        tr.op("dve", ["k2"], ["k2"], lambda e: e.memset(bd_bf[64:128, 64:128], 1.0))
        tr.op("dve", [], ["k3"], lambda e: e.memset(H_f[:], 0.0))
        tr.op("dve", ["k3"], ["k3"], lambda e: e.memset(H_f[0:64, :], 1.0))
        tr.op("dve", [], ["k11"], lambda e: e.memset(zero_bf[:], 0.0))
        tr.op("pool", ["k0"], ["k4"], lambda e: e.affine_select(
            out=ident_bf[:], in_=ones_bf[:], pattern=[[-1, 128]], compare_op=ALU.is_equal,
            fill=0.0, base=0, channel_multiplier=1))
        tr.op("pool", ["k11"], ["k5"], lambda e: e.affine_select(
            out=mneg_bf[:], in_=zero_bf[:], pattern=[[1, 128]], compare_op=ALU.is_ge,
            fill=-30000.0, base=0, channel_multiplier=-1))
        tr.op("pool", ["k1"], ["k6"], lambda e: e.affine_select(
            out=U_f[:], in_=ones_f[:], pattern=[[1, 128]], compare_op=ALU.is_ge,
            fill=0.0, base=0, channel_multiplier=-1))
        tr.op("pool", [], ["k7"], lambda e: e.iota(
            invc[:], pattern=[[1, 16]], base=1, channel_multiplier=0,
            allow_small_or_imprecise_dtypes=True))
        tr.op("dve", ["k7"], ["k7"], lambda e: e.reciprocal(out=invc[:], in_=invc[:]))
        tr.op("dve", ["c0"], ["k8"], lambda e: e.tensor_scalar(
            out=gq8[:], in0=pvec[:, 32:33], scalar1=0.125, scalar2=None, op0=ALU.mult))
        tr.barrier()

        itemsA, itemsB = [], []
        for s in range(NSEQ):
            for f in range(2):
                if f == 1:
                    for i in range(6):
                        itemsA.append(win[i])
                    for i in range(4):
                        itemsA.append(wout[i])
                for ffc in range(22):
                    itemsA.append(wgu[f][ffc])
                for i in range(16):
                    itemsB.append(wd[f][i])
        ringA = Ring(tr, "rA", rA_t, 2, 2048, itemsA)
        ringB = Ring(tr, "rB", rB_t, 2, FPH * 128, itemsB)

        cnt = {"sq": 0, "rstd": 0, "ab": 0, "bb": 0, "sg": 0, "pb": 0, "mb": 0, "yb": 0, "pl": 0}

        def nxt(k, mod):
            v = cnt[k] % mod
            cnt[k] += 1
            return v

        def rmsnorm_x(gcol, tiles=range(NTT)):
            for tt in tiles:
                for kc in range(NKC):
                    sl = nxt("sq", 4)
                    sqs = sq_t[:, sl * TT:(sl + 1) * TT]
                    tr.op("act", [("x", kc, tt)], [("sq", sl)],
                          lambda e, sqs=sqs, kc=kc, tt=tt: e.activation(
                              out=sqs, in_=xs(kc, tt), func=AF.Square))
                    tr.op("pe", [("sq", sl)], [("ps", 6)],
                          lambda e, sqs=sqs, kc=kc: e.matmul(
                              ps[6][:], lhsT=ones_bf[:], rhs=sqs,
                              start=(kc == 0), stop=(kc == NKC - 1)))
                r = nxt("rstd", 2)
                rs_ = rstd_t[:, r * TT:(r + 1) * TT]
                tr.op("act", [("ps", 6)], [("rstd", r)],
                      lambda e, rs_=rs_: e.activation(out=rs_, in_=ps[6][:], func=AF.Ln,
                                                      scale=1.0 / D, bias=eps_t[:, 0:1]))
                tr.op("act", [("rstd", r)], [("rstd", r)],
                      lambda e, rs_=rs_: e.activation(out=rs_, in_=rs_, func=AF.Exp, scale=-0.5))
                for kc in range(NKC):
                    tr.op("dve", [("x", kc, tt), ("rstd", r)], [("h", kc, tt)],
                          lambda e, kc=kc, tt=tt, rs_=rs_: e.scalar_tensor_tensor(
                              out=hs(kc, tt), in0=xs(kc, tt),
                              scalar=pvec[:, gcol + kc: gcol + kc + 1], in1=rs_,
                              op0=ALU.mult, op1=ALU.mult))

        eps_t = sb("eps_t", [128, 1], F32)
        tr.op("dve", [], ["k9"], lambda e: e.memset(eps_t[:], EPS))
        one_t = sb("one_t", [128, 1], F32)
        tr.op("dve", [], ["k10"], lambda e: e.memset(one_t[:], 1.0))
        tr.barrier()

        def ffn(gcol, after_chunk=None, prenormed=False):
            uid[0] += 1
            with nc.sbuf_tensor(f"act_t{uid[0]}", [128, FPH * S], BF16) as act_t, \
                    nc.sbuf_tensor(f"sg_t{uid[0]}", [128, 2 * TT], F32) as sg_t:
                def acts(fl, tt):
                    return act_t[:, fl * S + tt * TT: fl * S + (tt + 1) * TT]

                ringB.prefetch()
                def stage_a(fl, tt, wap, wkey):
                    pb = nxt("ab", 2) * 2
                    hreads = [("h", kc, tt) for kc in range(NKC)]

                    def emit_mm(e, off, bank):
                        ins = None
                        for kc in range(NKC):
                            ins = e.matmul(
                                ps[bank][:], lhsT=wap[:, off + kc * 128: off + (kc + 1) * 128],
                                rhs=hs(kc, tt), start=(kc == 0), stop=(kc == NKC - 1))
                        return ins
                    tr.op("pe", [wkey] + hreads, [("ps", pb)], lambda e: emit_mm(e, 0, pb))
                    tr.op("pe", [wkey] + hreads, [("ps", pb + 1)], lambda e: emit_mm(e, 1024, pb + 1))
                    s_ = nxt("sg", 2)
                    sgs = sg_t[:, s_ * TT:(s_ + 1) * TT]
                    tr.op("act", [("ps", pb)], [("sg", s_)],
                          lambda e: e.activation(out=sgs, in_=ps[pb][:], func=AF.Silu))
                    tr.op("dve", [("sg", s_), ("ps", pb + 1)], [("act", fl, tt)],
                          lambda e: e.tensor_tensor(
                              out=acts(fl, tt), in0=sgs, in1=ps[pb + 1][:], op=ALU.mult))

                if not prenormed:
                    rmsnorm_x(gcol)
                for half in range(2):
                    for fl in range(FPH):
                        wap, wkey = ringA.acquire()
                        for tt in range(NTT):
                            stage_a(fl, tt, wap, wkey)
                        ringA.release()
                    for dc in range(NKC):
                        wdap, wdkey = ringB.acquire()
                        for tt in range(NTT):
                            pb = 4 + nxt("bb", 2)

                            def emit_d(e, pb=pb, tt=tt, dc=dc, wdap=wdap):
                                ins = None
                                for fl in range(FPH):
                                    ins = e.matmul(
                                        ps[pb][:], lhsT=wdap[:, fl * 128:(fl + 1) * 128],
                                        rhs=acts(fl, tt), start=(fl == 0), stop=(fl == FPH - 1))
                                return ins
                            tr.op("pe", [wdkey] + [("act", fl, tt) for fl in range(FPH)],
                                  [("ps", pb)], emit_d)
                            tr.op("dve", [("ps", pb), ("x", dc, tt)], [("x", dc, tt)],
                                  lambda e, pb=pb, dc=dc, tt=tt: e.scalar_tensor_tensor(
                                      out=xs(dc, tt), in0=ps[pb][:], scalar=0.5, in1=xs(dc, tt),
                                      op0=ALU.mult, op1=ALU.add))
                        ringB.release()
                        if half == 1 and after_chunk is not None:
                            after_chunk(dc)
                tr.barrier()

        def mixer():
            mes = contextlib.ExitStack()
            uid[0] += 1
            with mes:
                def msb(name, shape, dt):
                    return mes.enter_context(nc.sbuf_tensor(f"m{uid[0]}_{name}", shape, dt))
                qn = msb("qn", [128, 4 * S], BF16)
                kn = msb("kn", [128, 4 * S], BF16)
                Vt = msb("Vt", [128, 16 * 768], BF16)
                ovl = msb("ovl", [128, 2080], F32)
                ovl2 = msb("ovl2", [128, 2112], F32)
                BT = msb("BT", [128, 2 * 128], F32)
                PT = msb("PT", [128, 3 * TT], BF16)
                pooled = PT
                fzp = msb("fzp", [128, 2 * 128], F32)
                rc = msb("rc", [128, 2 * TT], F32)
                halo = msb("halo", [128, 64], F32)
                qz = msb("qz", [128, 2 * 1024], BF16)
                fix = msb("fix", [128, 16], F32)

                fence_t = msb("fence_t", [128, 2], F32)

                def alias_fence(old_keys, new_keys):
                    tr.op("dve", list(old_keys) + ["fence"], list(old_keys) + list(new_keys) + ["fence"],
                          lambda e: e.memset(fence_t[:, 0:1], 0.0))

                winv = ovl[:, :].bitcast(BF16)
                mixed = ovl
                pvg = ovl2
                V5 = Vt[:, :].rearrange("p (j c s e) -> p j c s e", j=16, c=4, s=3, e=64)

                def og(il):
                    return ovl2[:, il * 512:(il + 1) * 512]

                for kc in range(NKC):
                    tr.dma("pool", "winv", [], ["winv"],
                           lambda g, kc=kc: g.dma_start(out=winv[:, kc * 520:(kc + 1) * 520],
                                                        in_=winv_d[:, kc, :]))
                tr.op("dve", [], ["Vones"], lambda e: e.memset(V5[:, :, :, 1, :], 1.0))

                rmsnorm_x(8)

                def qk_stage_b(pb, sl, dst_t, dkey, gap, c, tt):
                    sqs = sq_t[:, sl * TT:(sl + 1) * TT]
                    mb = 4 + nxt("mb", 2)
                    tr.op("pe", [("sq", sl)], [("ps", mb)],
                          lambda e: e.matmul(ps[mb][:], lhsT=bd_bf[:], rhs=sqs, start=True, stop=True))
                    r = nxt("rstd", 2)
                    rs_ = rstd_t[:, r * TT:(r + 1) * TT]
                    tr.op("act", [("ps", mb)], [("rstd", r)],
                          lambda e: e.activation(out=rs_, in_=ps[mb][:], func=AF.Ln, scale=1.0 / 64,
                                                 bias=eps_t[:, 0:1]))
                    tr.op("act", [("rstd", r)], [("rstd", r)],
                          lambda e: e.activation(out=rs_, in_=rs_, func=AF.Exp, scale=-0.5))
                    tr.op("dve", [("ps", pb), ("rstd", r)], [(dkey, c, tt)],
                          lambda e: e.scalar_tensor_tensor(
                              out=dst_t[:, c * S + tt * TT: c * S + (tt + 1) * TT],
                              in0=ps[pb][:], scalar=gap, in1=rs_, op0=ALU.mult, op1=ALU.mult))

                pend = None
                for it in range(4):
                    wap, wkey = ringA.acquire()
                    for c2 in range(2):
                        c = (it % 2) * 2 + c2
                        isq = it < 2
                        dst_t = qn if isq else kn
                        dkey = "qn" if isq else "kn"
                        gap = gq8[:, 0:1] if isq else pvec[:, 33:34]
                        for tt in range(NTT):
                            pb = nxt("pb", 4)

                            def emit_p(e, pb=pb, tt=tt, c2=c2, wap=wap):
                                ins = None
                                for kc in range(NKC):
                                    ins = e.matmul(
                                        ps[pb][:],
                                        lhsT=wap[:, c2 * 1024 + kc * 128: c2 * 1024 + (kc + 1) * 128],
                                        rhs=hs(kc, tt), start=(kc == 0), stop=(kc == NKC - 1))
                                return ins
                            tr.op("pe", [wkey] + [("h", kc, tt) for kc in range(NKC)],
                                  [("ps", pb)], emit_p)
                            sl = nxt("sq", 4)
                            sqs = sq_t[:, sl * TT:(sl + 1) * TT]
                            tr.op("act", [("ps", pb)], [("sq", sl)],
                                  lambda e, sqs=sqs, pb=pb: e.activation(
                                      out=sqs, in_=ps[pb][:], func=AF.Square))
                            if pend is not None:
                                qk_stage_b(*pend)
                            pend = (pb, sl, dst_t, dkey, gap, c, tt)
                    ringA.release()
                qk_stage_b(*pend)

                for j in range(16):
                    pb = nxt("pb", 4)

                    def emit_v(e, pb=pb, j=j):
                        ins = None
                        for kc in range(NKC):
                            ins = e.matmul(ps[pb][:], lhsT=hs_tok(kc, j * 128, 128),
                                           rhs=winv[:, kc * 520: kc * 520 + 512],
                                           start=(kc == 0), stop=(kc == NKC - 1))
                        return ins
                    hr = [("h", kc, j // 4) for kc in range(NKC)]
                    tr.op("pe", ["winv"] + hr, [("ps", pb)], emit_v)

                    def emit_f(e, j=j):
                        ins = None
                        for kc in range(NKC):
                            ins = e.matmul(ps[6][:, j * 8:(j + 1) * 8],
                                           lhsT=hs_tok(kc, j * 128, 128),
                                           rhs=winv[:, kc * 520 + 512: kc * 520 + 520],
                                           start=(kc == 0), stop=(kc == NKC - 1))
                        return ins
                    tr.op("pe", ["winv"] + hr, [("ps", 6)], emit_f)
                    for s_ in range(2):
                        tr.op("dve", [("ps", pb), "Vones"], [("V", j)],
                              lambda e, pb=pb, j=j, s_=s_: e.tensor_copy(
                                  out=V5[:, j, :, 2 * s_, :],
                                  in_=ps[pb][:].rearrange("p (c s e) -> p c s e", c=4, s=2)[:, :, s_, :]))

                def fzs(k):
                    if k in (1, 2):
                        return fzp[:, (k - 1) * 128: k * 128]
                    kk = {0: 0, 3: 1, 4: 2, 5: 3, 6: 4}[k]
                    return rc[:, kk * 128:(kk + 1) * 128]
                tr.op("dve", [("ps", 6)], [("fz", 0)], lambda e: e.tensor_tensor(
                    out=fzs(0), in0=ps[6][:, 0:128], in1=bfb[:], op=ALU.add))
                tr.op("dve", [("fz", 0)], [("fz", 1)], lambda e: e.tensor_scalar(
                    out=fzs(1), in0=fzs(0), scalar1=-1.0, scalar2=None, op0=ALU.mult))
                tr.op("dve", [("fz", 0), ("fz", 1)], [("fz", 1)], lambda e: e.tensor_tensor(
                    out=fzs(1), in0=fzs(1), in1=fzs(0), op=ALU.max))
                tr.op("act", [("fz", 1)], [("fz", 2)], lambda e: e.activation(
                    out=fzs(2), in_=fzs(1), func=AF.Exp, scale=-1.0))
                tr.op("act", [("fz", 2)], [("fz", 2)], lambda e: e.activation(
                    out=fzs(2), in_=fzs(2), func=AF.Ln, bias=one_t[:, 0:1], scale=1.0))
                tr.op("dve", [("fz", 0)], [("fz", 3)], lambda e: e.tensor_scalar_min(
                    out=fzs(3), in0=fzs(0), scalar1=0.0))
                tr.op("dve", [("fz", 3), ("fz", 2)], [("fz", 3)], lambda e: e.tensor_sub(
                    out=fzs(3), in0=fzs(3), in1=fzs(2)))
                tr.op("pe", [("fz", 3)], [("ps", 6)], lambda e: e.matmul(
                    ps[6][:, 0:128], lhsT=U_f[:], rhs=fzs(3), start=True, stop=True))
                tr.op("pe", [("fz", 3)], [("ps", 6)], lambda e: e.matmul(
                    ps[6][:, 128:256], lhsT=ones_f[:], rhs=fzs(3), start=True, stop=True))
                tr.op("pe", [("fz", 3)], [("ps", 6)], lambda e: e.matmul(
                    ps[6][:, 256:384], lhsT=H_f[:], rhs=fzs(3), start=True, stop=True))
                tr.op("dve", [("ps", 6)], [("fz", 4), ("fz", 5), ("fz", 6)],
                      lambda e: e.tensor_copy(out=rc[:, 256:640], in_=ps[6][:, 0:384]))
                tr.op("dve", [("fz", 0)], [("fz", 0)], lambda e: e.memset(rc[:, 0:8], 0.0))
                for j in range(1, 16):
                    tr.op("dve", [("fz", 0), ("fz", 5)], [("fz", 0)],
                          lambda e, j=j: e.tensor_tensor(
                              out=rc[:, j * 8:(j + 1) * 8], in0=rc[:, (j - 1) * 8: j * 8],
                              in1=rc[:, 384 + (j - 1) * 8: 384 + j * 8], op=ALU.add))
                tr.op("dve", [("fz", 4), ("fz", 0)], [("fz", 1)],
                      lambda e: e.scalar_tensor_tensor(
                          out=fzs(1), in0=fzs(4), scalar=-1.0, in1=fzs(0),
                          op0=ALU.mult, op1=ALU.subtract))
                tr.op("dve", [("fz", 0)], [("fz", 2)], lambda e: e.tensor_copy(
                    out=fzs(2), in_=fzs(0)))
                negF3 = fzs(1).rearrange("p (j h) -> p j h", h=8)
                Fref3 = fzs(2).rearrange("p (j h) -> p j h", h=8)

                wp0, wk0 = ringA.acquire()
                wp1, wk1 = ringA.acquire()
                alias_fence([("fz", k) for k in (0, 3, 4, 5, 6)], ["tmpP0"])
                tr.op("dve", ["winv", "fence"], ["winv", "fence"] + [("mixed", g) for g in range(4)],
                      lambda e: e.memset(fence_t[:, 0:1], 0.0))
                SCAN_ENG = {0: "dve", 1: "dve", 2: "pool", 3: "pool"}

                def pool_stage_a(n):
                    tt, g = divmod(n, 4)
                    w = 2 << g
                    wap, wkey = (wp0, wk0) if g < 2 else (wp1, wk1)
                    off = (g % 2) * 1024
                    pb = nxt("pb", 4)

                    def emit_pv(e):
                        ins = None
                        for kc in range(NKC):
                            ins = e.matmul(
                                ps[pb][:], lhsT=wap[:, off + kc * 128: off + (kc + 1) * 128],
                                rhs=hs(kc, tt), start=(kc == 0), stop=(kc == NKC - 1))
                        return ins
                    tr.op("pe", [wkey] + [("h", kc, tt) for kc in range(NKC)], [("ps", pb)], emit_pv)
                    sl = n % 2
                    pv = pvg[:, sl * 528:(sl + 1) * 528]
                    tr.op("act", [("ps", pb)], [("pvg", sl)],
                          lambda e: e.copy(out=pv[:, 16:528], in_=ps[pb][:]))

                qzf = qz[:, :].bitcast(F32)

                def pool_stage_a2(n):
                    tt, g = divmod(n, 4)
                    w = 2 << g
                    sl = n % 2
                    pv = pvg[:, sl * 528:(sl + 1) * 528]
                    en = "pool" if g == 3 else "dve"
                    if g == 3:
                        tbuf = [rc[:, 0:528], qzf[:, 0:528]]
                        tkey = ["tmpP0", "tmpP1"]
                    else:
                        tbuf = [ovl2[:, 1056:1584], ovl2[:, 1584:2112]]
                        tkey = [("tmp", 0), ("tmp", 1)]
                    if tt == 0:
                        tr.op(en, [], [("pvgh", sl)], lambda e: e.memset(pv[:, 0:16], 0.0))
                    else:
                        tr.op(en, [("halo", g)], [("pvgh", sl)],
                              lambda e: e.tensor_copy(out=pv[:, 0:16], in_=halo[:, g * 16:(g + 1) * 16]))
                    tr.op(en, [("pvg", sl)], [("halo", g)],
                          lambda e: e.tensor_copy(out=halo[:, g * 16:(g + 1) * 16], in_=pv[:, 512:528]))
                    src, skeys = pv, [("pvg", sl), ("pvgh", sl)]
                    lo, k, ti = 0, 1, 0
                    while k < w:
                        dst = tbuf[ti]
                        tr.op(en, skeys, [tkey[ti]],
                              lambda e, dst=dst, src=src, lo=lo, k=k: e.tensor_tensor(
                                  out=dst[:, lo + k:528], in0=src[:, lo + k:528],
                                  in1=src[:, lo:528 - k], op=ALU.add))
                        src, skeys = dst, [tkey[ti]]
                        lo += k
                        k *= 2
                        ti ^= 1
                    psl = n % 2
                    pl = pooled[:, psl * TT:(psl + 1) * TT]
                    if g == 3:
                        oth, okey = tbuf[ti], tkey[ti]
                        tr.op(en, skeys, [okey], lambda e: e.tensor_scalar(
                            out=oth[:, 16:528], in0=src[:, 16:528], scalar1=1.0 / w, scalar2=0.0,
                            op0=ALU.mult, op1=ALU.add))
                        tr.op(en, [okey, ("pvg", sl)], [("pooled", psl)], lambda e: e.tensor_tensor(
                            out=pl, in0=oth[:, 16:528], in1=pv[:, 16:528], op=ALU.subtract))
                        if tt == 0:
                            tr.op(en, skeys, skeys, lambda e: e.tensor_tensor(
                                out=src[:, 0:w - 1], in0=src[:, 16:16 + w - 1],
                                in1=invc[:, 0:w - 1], op=ALU.mult))
                            tr.op(en, skeys + [("pvg", sl), ("pooled", psl)], [("pooled", psl)],
                                  lambda e: e.tensor_tensor(
                                      out=pl[:, 0:w - 1], in0=src[:, 0:w - 1],
                                      in1=pv[:, 16:16 + w - 1], op=ALU.subtract))
                        return
                    tr.op("dve", skeys + [("pvg", sl)], [("pooled", psl)],
                          lambda e: e.scalar_tensor_tensor(
                              out=pl, in0=src[:, 16:528], scalar=1.0 / w, in1=pv[:, 16:528],
                              op0=ALU.mult, op1=ALU.subtract))
                    if tt == 0:
                        tr.op("dve", skeys, ["fix"],
                              lambda e: e.tensor_tensor(
                                  out=fix[:, 0:w - 1], in0=src[:, 16:16 + w - 1],
                                  in1=invc[:, 0:w - 1], op=ALU.mult))
                        tr.op("dve", ["fix", ("pvg", sl), ("pooled", psl)], [("pooled", psl)],
                              lambda e: e.tensor_tensor(
                                  out=pl[:, 0:w - 1], in0=fix[:, 0:w - 1],
                                  in1=pv[:, 16:16 + w - 1], op=ALU.subtract))

                sqslot = {}

                def pool_stage_b(n):
                    tt, g = divmod(n, 4)
                    psl = n % 2
                    pl = pooled[:, psl * TT:(psl + 1) * TT]
                    mb = 4 + nxt("mb", 2)
                    tr.op("pe", [("pooled", psl)], [("ps", mb)],
                          lambda e: e.matmul(ps[mb][:], lhsT=poolw[:, g * 128:(g + 1) * 128], rhs=pl,
                                             start=True, stop=True))
                    mx = mixed[:, g * TT:(g + 1) * TT]
                    tr.op("act", [("ps", mb)], [("mixed", g)],
                          lambda e: e.activation(out=mx, in_=ps[mb][:], func=AF.Copy,
                                                 scale=pvec[:, 24 + g: 25 + g]))
                    sl2 = nxt("sq", 4)
                    sqslot[n] = sl2
                    sqs = sq_t[:, sl2 * TT:(sl2 + 1) * TT]
                    tr.op("act", [("ps", mb)], [("sq", sl2)],
                          lambda e: e.activation(out=sqs, in_=ps[mb][:], func=AF.Square,
                                                 scale=pvec[:, 24 + g: 25 + g]))

                def pool_stage_b2(n):
                    tt, g = divmod(n, 4)
                    sl2 = sqslot[n]
                    sqs = sq_t[:, sl2 * TT:(sl2 + 1) * TT]
                    tr.op("pe", [("sq", sl2)], [("ps", 6)],
                          lambda e: e.matmul(ps[6][:], lhsT=ones_bf[:], rhs=sqs,
                                             start=(g == 0), stop=(g == 3)))
                    if g != 3:
                        return
                    r = nxt("rstd", 2)
                    rs_ = rstd_t[:, r * TT:(r + 1) * TT]
                    tr.op("act", [("ps", 6)], [("rstd", r)],
                          lambda e: e.activation(out=rs_, in_=ps[6][:], func=AF.Ln, scale=1.0 / 512,
                                                 bias=eps_t[:, 0:1]))
                    tr.op("act", [("rstd", r)], [("rstd", r)],
                          lambda e: e.activation(out=rs_, in_=rs_, func=AF.Exp, scale=-0.5))
                    for g_ in range(4):
                        tr.op("dve", [("mixed", g_), ("rstd", r)], [("h", g_, tt)],
                              lambda e, g_=g_: e.scalar_tensor_tensor(
                                  out=hs(g_, tt), in0=mixed[:, g_ * TT:(g_ + 1) * TT],
                                  scalar=pvec[:, 28 + g_: 29 + g_], in1=rs_,
                                  op0=ALU.mult, op1=ALU.mult))

                for n in range(19):
                    if n < 16:
                        pool_stage_a(n)
                    if 0 <= n - 1 < 16:
                        pool_stage_a2(n - 1)
                    if 0 <= n - 3 < 16:
                        pool_stage_b2(n - 3)
                    if 0 <= n - 2 < 16:
                        pool_stage_b(n - 2)
                ringA.release()
                ringA.release()

                alias_fence([("pvg", 0), ("pvg", 1), ("pvgh", 0), ("pvgh", 1), ("tmp", 0), ("tmp", 1)],
                            [("on", c_, s_) for c_ in range(4) for s_ in range(2)])

                alias_fence([("pooled", 0), ("pooled", 1)],
                            [("PT", p_, s_) for p_ in range(3) for s_ in range(2)])
                alias_fence([("fz", k) for k in (0, 3, 4, 5, 6)] + ["tmpP0"],
                            [("rc", r_, s_) for r_ in range(2) for s_ in range(2)])
                tr.op("dve", ["tmpP1"], ["tmpP1", ("qz", 0, 0), ("qz", 0, 1), ("qz", 1, 0), ("qz", 1, 1)],
                      lambda e: e.memset(qz[:], 0.0))
                steps = []
                gcc = 0
                for g in range(4):
                    for c in range(4):
                        par = gcc % 2
                        gcc += 1
                        first_step = len(steps)
                        for j in range(4 * g + 4):
                            for hq in range(2):
                                i0 = 4 * g + 2 * hq
                                i1 = i0 + 1
                                if j > i1:
                                    continue
                                lo = max(i0, j)
                                steps.append(dict(g=g, c=c, par=par, j=j, i1=i1, lo=lo, hq=hq,
                                                  nb=i1 - lo + 1, qoff=(lo - 4 * g) * 128,
                                                  pre=False, post=False, gpost=False, first=False))
                        steps[first_step]["pre"] = True
                        steps[first_step]["first"] = True
                        steps[-1]["post"] = True
                        steps[-1]["gpost"] = (c == 3)

                def emit_pre(st):
                    g, c, par = st["g"], st["c"], st["par"]
                    for s_ in range(2):
                        p0 = s_ * 64
                        tr.op("pool" if g < 2 else "dve", [("qn", c, g)], [("qz", par, s_)],
                              lambda e, p0=p0, s_=s_: e.tensor_copy(
                                  out=qz[p0:p0 + 64, par * 1024 + s_ * 512: par * 1024 + (s_ + 1) * 512],
                                  in_=qn[p0:p0 + 64, c * S + g * TT: c * S + (g + 1) * TT]))
                    for s_ in range(2):
                        h = 2 * c + s_
                        for hq in range(2):
                            i1 = 4 * g + 2 * hq + 1
                            bo = par * 64 + s_ * 32 + hq * 16
                            tr.op("pool", [("fz", 1), ("fz", 2)], [("BT", par, s_, hq)],
                                  lambda e, i1=i1, h=h, bo=bo: e.tensor_scalar(
                                      out=BT[:, bo: bo + i1 + 1],
                                      in0=negF3[:, 0:i1 + 1, h], scalar1=1.0,
                                      scalar2=Fref3[:, i1, h:h + 1],
                                      op0=ALU.mult, op1=ALU.add))

                def emit_qk(n, st):
                    c, par, j, qoff, ncol = st["c"], st["par"], st["j"], st["qoff"], st["nb"] * 128
                    sbk = n % 3
                    qz3 = qz[:, par * 1024:(par + 1) * 1024].rearrange("p (s t) -> p s t", s=2)
                    diag = (st["lo"] == j)

                    def emit_s(e):
                        ins = e.matmul(
                            ps[sbk][:, 0:2 * ncol],
                            lhsT=kn[:, c * S + j * 128: c * S + (j + 1) * 128],
                            rhs=qz3[:, :, qoff:qoff + ncol], start=True, stop=not diag,
                            skip_group_check=True)
                        if diag:
                            for s_ in range(2):
                                ins = e.matmul(
                                    ps[sbk][:, s_ * ncol: s_ * ncol + 128], lhsT=ident_bf[:],
                                    rhs=mneg_bf[:], start=False, stop=(s_ == 1),
                                    skip_group_check=True)
                        return ins
                    tr.op("pe", [("kn", c, j // 4), ("qz", par, 0), ("qz", par, 1)], [("ps", sbk)],
                          emit_s)

                def emit_exp(n, st):
                    g, par, j, lo, nb, hq = st["g"], st["par"], st["j"], st["lo"], st["nb"], st["hq"]
                    ncol = nb * 128
                    sbk = psl = n % 3
                    for s_ in range(2):
                        co = s_ * ncol
                        pts = PT[:, psl * TT + co: psl * TT + co + ncol]
                        bo = par * 64 + s_ * 32 + hq * 16 + j
                        tr.op("act", [("ps", sbk), ("BT", par, s_, hq)], [("PT", psl, s_)],
                              lambda e, pts=pts, co=co, bo=bo: e.activation(
                                  out=pts, in_=ps[sbk][:, co:co + ncol], func=AF.Exp,
                                  bias=BT[:, bo:bo + 1], scale=1.0))

                def emit_pv(n, st):
                    c, par, j, qoff, nb = st["c"], st["par"], st["j"], st["qoff"], st["nb"]
                    ncol = nb * 128
                    psl = n % 3
                    TA = 3 + 2 * par
                    for s_ in range(2):
                        Tb = TA + s_
                        tr.op("pe", [("PT", psl, s_), ("V", j)],
                              [("ps", Tb)],
                              lambda e, Tb=Tb, s_=s_: e.matmul(
                                  ps[Tb][:, qoff:qoff + ncol],
                                  lhsT=V5[:, j, c, s_:s_ + 2, :],
                                  rhs=PT[:, psl * TT + s_ * ncol: psl * TT + (s_ + 1) * ncol],
                                  start=st["first"], stop=(j == st["i1"]), skip_group_check=True))

                def emit_post(st):
                    g, c, par = st["g"], st["c"], st["par"]
                    assert not gpend, (g, c, gpend)
                    TA = 3 + 2 * par
                    TB = TA + 1
                    rcA = rc[0:64, par * TT:(par + 1) * TT]
                    rcB = rc[64:128, par * TT:(par + 1) * TT]
                    onc = ovl2[:, c * TT:(c + 1) * TT]
                    tr.op("dve", [("ps", TA)], [("rc", par, 0)],
                          lambda e: e.reciprocal(out=rcA, in_=ps[TA][64:128, :]))
                    tr.op("dve", [("ps", TB)], [("rc", par, 1)],
                          lambda e: e.reciprocal(out=rcB, in_=ps[TB][0:64, :]))
                    tr.op("dve", [("ps", TA), ("rc", par, 0)], [("on", c, 0)],
                          lambda e: e.tensor_tensor(
                              out=onc[0:64, :], in0=ps[TA][0:64, :], in1=rcA, op=ALU.mult))
                    tr.op("dve", [("ps", TB), ("rc", par, 1)], [("on", c, 1)],
                          lambda e: e.tensor_tensor(
                              out=onc[64:128, :], in0=ps[TB][64:128, :], in1=rcB, op=ALU.mult))
                    if st["gpost"]:
                        gpend.append(g)

                def emit_gpost(g):
                    for c_ in range(4):
                        onc_ = ovl2[:, c_ * TT:(c_ + 1) * TT]
                        sl = nxt("sq", 4)
                        sqs = sq_t[:, sl * TT:(sl + 1) * TT]
                        tr.op("pool", [("on", c_, 0), ("on", c_, 1)], [("sq", sl)],
                              lambda e, sqs=sqs, onc_=onc_: e.tensor_tensor(
                                  out=sqs, in0=onc_, in1=onc_, op=ALU.mult))
                        tr.op("pe", [("sq", sl)], [("ps", 7)],
                              lambda e, sqs=sqs, c_=c_: e.matmul(
                                  ps[7][:], lhsT=ones_bf[:], rhs=sqs, start=(c_ == 0), stop=(c_ == 3)))
                    r = nxt("rstd", 2)
                    rs_ = rstd_t[:, r * TT:(r + 1) * TT]
                    tr.op("act", [("ps", 7)], [("rstd", r)],
                          lambda e: e.activation(
                              out=rs_, in_=ps[7][:], func=AF.Ln, scale=1.0 / 512, bias=eps_t[:, 0:1]))
                    tr.op("act", [("rstd", r)], [("rstd", r)],
                          lambda e: e.activation(out=rs_, in_=rs_, func=AF.Exp, scale=-0.5))
                    for c_ in range(4):
                        onc_ = ovl2[:, c_ * TT:(c_ + 1) * TT]
                        tr.op("dve", [("on", c_, 0), ("on", c_, 1), ("rstd", r)], [("h", 4 + c_, g)],
                              lambda e, onc_=onc_, c_=c_: e.scalar_tensor_tensor(
                                  out=hs(4 + c_, g), in0=onc_, scalar=pvec[:, 34 + c_: 35 + c_],
                                  in1=rs_, op0=ALU.mult, op1=ALU.mult))

                NS = len(steps)
                gpend = []
                gdue = {}
                GDEFER = 12
                pres = [n for n in range(NS) if steps[n]["pre"]]
                emit_pre(steps[pres[0]])
                npre = 1
                for n in range(NS + 2):
                    if n < NS:
                        emit_qk(n, steps[n])
                    if 0 <= n - 1 < NS:
                        emit_exp(n - 1, steps[n - 1])
                    if 0 <= n - 2 < NS:
                        emit_pv(n - 2, steps[n - 2])
                        if steps[n - 2]["post"]:
                            emit_post(steps[n - 2])
                            for g_ in gpend:
                                gdue.setdefault(g_, n + GDEFER)
                    for g_ in list(gpend):
                        if n >= gdue[g_]:
                            emit_gpost(g_)
                            gpend.remove(g_)
                    if n < NS and steps[n]["pre"] and npre < len(pres):
                        emit_pre(steps[pres[npre]])
                        npre += 1

                for g_ in list(gpend):
                    emit_gpost(g_)
                def wo_step(dc, d2, tt, wap, wkey):
                    pb = nxt("pb", 3)

                    def emit_o(e):
                        ins = None
                        for kc in range(NKC):
                            ins = e.matmul(
                                ps[pb][:],
                                lhsT=wap[:, d2 * 1024 + kc * 128: d2 * 1024 + (kc + 1) * 128],
                                rhs=hs(kc, tt), start=(kc == 0), stop=(kc == NKC - 1))
                        return ins
                    tr.op("pe", [wkey] + [("h", kc, tt) for kc in range(NKC)], [("ps", pb)], emit_o)
                    tr.op("dve", [("ps", pb), ("x", dc, tt)], [("x", dc, tt)],
                          lambda e: e.tensor_tensor(
                              out=xs(dc, tt), in0=ps[pb][:], in1=xs(dc, tt), op=ALU.add))

                for it in range(3):
                    wap, wkey = ringA.acquire()
                    for d2 in range(2):
                        for tt in range(NTT):
                            wo_step(it * 2 + d2, d2, tt, wap, wkey)
                    ringA.release()
                wap, wkey = ringA.acquire()
                for tt in range(NTT):
                    for d2 in range(2):
                        wo_step(6 + d2, d2, tt, wap, wkey)
                    if tt >= 1:
                        rmsnorm_x(16, [tt - 1])
                ringA.release()
                rmsnorm_x(16, [NTT - 1])
                tr.barrier()

        def load_x(s, kc):
            tr.dma("sp", f"xld{kc}", [], [("x", kc, tt) for tt in range(NTT)],
                   lambda e: e.dma_start(out=x_sb[:, kc * S:(kc + 1) * S], in_=xT[s, kc]))

        for tt in range(NTT):
            for kc in range(NKC):
                tr.add_dma_sem(f"xi{kc}_{tt}")
                tr.dma("sp", f"xi{kc}_{tt}", [], [("x", kc, tt)],
                       lambda e, kc=kc, tt=tt: e.dma_start(
                           out=xs(kc, tt), in_=xT[0, kc][:, tt * TT:(tt + 1) * TT]))
        for s in range(NSEQ):
            def after_chunk(kc, s=s):
                tr.dma("sp", f"yst{kc}", [("x", kc, tt) for tt in range(NTT)], [("y", s, kc)],
                       lambda e: e.dma_start(out=yT[s, kc], in_=x_sb[:, kc * S:(kc + 1) * S]))
                if s + 1 < NSEQ:
                    load_x(s + 1, kc)
            ffn(0)
            mixer()
            ffn(16, after_chunk, prenormed=True)
        tr.barrier(["sp"], all_sems=True)
    return nc


def _prep_weights(inp):
    f = lambda a: np.ascontiguousarray(np.asarray(a, dtype=np.float32))

    def lay_gu(Wg, Wu):
        g = f(Wg).reshape(8, 128, 22, 128).transpose(2, 1, 0, 3)
        u = f(Wu).reshape(8, 128, 22, 128).transpose(2, 1, 0, 3)
        return f(np.stack([g, u], axis=2).reshape(22, 128, 2048))

    def lay_d(Wd):
        return f(f(Wd).reshape(2, FPH, 128, 8, 128).transpose(0, 3, 2, 1, 4).reshape(16, 128, FPH * 128))

    w_in = f(inp["w_in"])
    wi = w_in[:, 0:1536].reshape(8, 128, 12, 128).transpose(2, 1, 0, 3)
    order = [4, 5, 6, 7, 8, 9, 10, 11, 0, 1, 2, 3]
    wi = wi[order].reshape(6, 2, 128, 8, 128).transpose(0, 2, 1, 3, 4).reshape(6, 128, 2048)
    winv = w_in[:, 1536:2056].reshape(8, 128, 520).transpose(1, 0, 2)
    wo = f(inp["w_out"]).reshape(8, 128, 8, 128).transpose(2, 1, 0, 3)
    wo = wo.reshape(4, 2, 128, 8, 128).transpose(0, 2, 1, 3, 4).reshape(4, 128, 2048)
    poolw = f(inp["pool_w"]).transpose(1, 0, 2).reshape(128, 512)
    pvec = np.zeros((128, 40), np.float32)
    pvec[:, 0:8] = f(inp["ffn1_norm"]).reshape(8, 128).T
    pvec[:, 8:16] = f(inp["mix_norm"]).reshape(8, 128).T
    pvec[:, 16:24] = f(inp["ffn2_norm"]).reshape(8, 128).T
    pvec[:, 24:28] = f(inp["pool_scale"]).reshape(4, 128).T
    pvec[:, 28:32] = f(inp["out_norm_pool"]).reshape(4, 128).T
    pvec[:, 32] = np.tile(f(inp["q_norm"]), 2)
    pvec[:, 33] = np.tile(f(inp["k_norm"]), 2)
    pvec[:, 34:38] = f(inp["out_norm_attn"]).reshape(4, 128).T
    bfb = np.broadcast_to(np.tile(f(inp["b_forget"]), 16)[None, :], (128, 128))
    return {
        "wgu1": lay_gu(inp["ffn1_w_gate"], inp["ffn1_w_up"]),
        "wgu2": lay_gu(inp["ffn2_w_gate"], inp["ffn2_w_up"]),
        "wd1": lay_d(inp["ffn1_w_down"]), "wd2": lay_d(inp["ffn2_w_down"]),
        "win": f(wi), "winv": f(winv), "wout": f(wo), "poolw": f(poolw),
        "pvec": pvec, "bfb": f(bfb),
    }


_NC_CACHE = {}


def kernel(**inputs):
    x = np.asarray(inputs["x"], dtype=np.float32)
    w = _prep_weights(inputs)
    if "nc" not in _NC_CACHE:
        _NC_CACHE["nc"] = build_nc()
    nc = _NC_CACHE["nc"]
    in_maps = []
    for c in range(NCORES):
        xc = x[c * NSEQ:(c + 1) * NSEQ].transpose(0, 2, 1).reshape(NSEQ, NKC, 128, S)
        m = {"xT": np.ascontiguousarray(xc)}
        m.update(w)
        in_maps.append(m)
    res = run_bass_kernel_spmd(nc, in_maps, core_ids=list(range(NCORES)))
    out = np.empty((NCORES * NSEQ, S, D), np.float32)
    for c in range(NCORES):
        y = np.asarray(res.results[c]["yT"]).reshape(NSEQ, D, S)
        out[c * NSEQ:(c + 1) * NSEQ] = y.transpose(0, 2, 1)
    return out
```

```python
import contextlib
import numpy as np
import concourse.bass as bass
import concourse.mybir as mybir
from concourse.bass_utils import run_bass_kernel_spmd

F32 = mybir.dt.float32
BF16 = mybir.dt.bfloat16
ALU = mybir.AluOpType
AF = mybir.ActivationFunctionType

NCORES = 8
D = 1024
S = 2048
DFF = 2816
NKC = 8
TT = 512
NTT = 4
FPH = 11
EPS = 1e-6
NSEQ = 2


class Tracker:
    def __init__(self, nc, es):
        self.nc = nc
        self.eng = {"pe": nc.tensor, "act": nc.scalar, "dve": nc.vector,
                    "pool": nc.gpsimd, "sp": nc.sync}
        self.semobj = {}
        self.val = {}
        for k in self.eng:
            self.semobj["e_" + k] = es.enter_context(nc.semaphore("s_" + k))
            self.val["e_" + k] = 0
        self.es = es
        self.known = {}
        self.lastw = {}
        self.readers = {}

    def add_dma_sem(self, name):
        self.semobj[name] = self.es.enter_context(self.nc.semaphore("d_" + name))
        self.val[name] = 0

    def _deps(self, reads, writes):
        d = {}

        def add(tok):
            if tok is None:
                return
            s, v = tok
            if d.get(s, 0) < v:
                d[s] = v
        for r in reads:
            add(self.lastw.get(r))
        for w in writes:
            add(self.lastw.get(w))
            for s, v in self.readers.get(w, {}).items():
                add((s, v))
        return d

    def _wait(self, e, d):
        for s, v in d.items():
            if e == "pe" and s == "e_pe":
                continue
            if self.known.get((e, s), 0) >= v:
                continue
            self.eng[e].wait_ge(self.semobj[s], v)
            self.known[(e, s)] = v

    def _commit(self, tok, reads, writes):
        s, v = tok
        for r in reads:
            rd = self.readers.setdefault(r, {})
            if rd.get(s, 0) < v:
                rd[s] = v
        for w in writes:
            self.lastw[w] = tok
            self.readers[w] = {}

    def op(self, e, reads, writes, emit):
        self._wait(e, self._deps(reads, writes))
        ins = emit(self.eng[e])
        s = "e_" + e
        self.val[s] += 1
        ins.then_inc(self.semobj[s], 1)
        self._commit((s, self.val[s]), reads, writes)

    def dma(self, e, semname, reads, writes, emit):
        self._wait(e, self._deps(reads, writes))
        ins = emit(self.eng[e])
        self.val[semname] += 16
        ins.then_inc(self.semobj[semname], 16)
        self._commit((semname, self.val[semname]), reads, writes)

    def barrier(self, engines=None, all_sems=False):
        d = {s: v for s, v in self.val.items()
             if v > 0 and (all_sems or s.startswith("e_") or s in ("winv", "cstp", "cst0", "cst2"))}
        for e in (engines or self.eng):
            self._wait(e, d)


class Ring:
    def __init__(self, tr, name, tile, nslots, slotw, items):
        self.tr, self.name, self.tile = tr, name, tile
        self.nslots, self.slotw, self.items = nslots, slotw, items
        self.n_issued = self.n_acq = self.n_rel = 0
        for s in range(nslots):
            tr.add_dma_sem(f"{name}{s}")

    def prefetch(self):
        while self.n_issued < len(self.items) and self.n_issued < self.n_rel + self.nslots:
            i = self.n_issued
            slot = i % self.nslots
            src = self.items[i]
            w = src.shape[-1]
            dst = self.tile[:, slot * self.slotw: slot * self.slotw + w]
            self.tr.dma("pool", f"{self.name}{slot}", [], [(self.name, slot)],
                        lambda g, dst=dst, src=src: g.dma_start(out=dst, in_=src))
            self.n_issued += 1

    def acquire(self):
        self.prefetch()
        i = self.n_acq
        assert self.n_issued > i, (self.name, i, self.n_issued, self.n_rel)
        self.n_acq += 1
        slot = i % self.nslots
        return self.tile[:, slot * self.slotw:(slot + 1) * self.slotw], (self.name, slot)

    def release(self):
        self.n_rel += 1
        self.prefetch()


def build_nc():
    nc = bass.Bass("TRN2", target_bir_lowering=False)

    def din(name, shape):
        return nc.dram_tensor(name, shape, F32, kind="ExternalInput").ap()

    xT = din("xT", [NSEQ, NKC, 128, S])
    wgu = [din("wgu1", [22, 128, 2048]), din("wgu2", [22, 128, 2048])]
    wd = [din("wd1", [16, 128, FPH * 128]), din("wd2", [16, 128, FPH * 128])]
    win = din("win", [6, 128, 2048])
    winv_d = din("winv", [128, NKC, 520])
    wout = din("wout", [4, 128, 2048])
    poolw_d = din("poolw", [128, 512])
    pvec_d = din("pvec", [128, 40])
    bfb_d = din("bfb", [128, 128])
    yT = nc.dram_tensor("yT", [NSEQ, NKC, 128, S], F32, kind="ExternalOutput").ap()

    es = contextlib.ExitStack()
    with es:
        def sb(name, shape, dt):
            return es.enter_context(nc.sbuf_tensor("sb_" + name, shape, dt))
        uid = [0]

        tr = Tracker(nc, es)
        for nm in [f"xld{k}" for k in range(8)] + [f"yst{k}" for k in range(8)] + ["cst0", "cst2", "cstp", "winv"]:
            tr.add_dma_sem(nm)

        x_sb = sb("x_sb", [128, NKC * S], F32)
        h_sb = sb("h_sb", [128, NKC * S], BF16)
        rA_t = sb("rA", [128, 2 * 2048], BF16)
        rB_t = sb("rB", [128, 2 * FPH * 128], BF16)
        sq_t = sb("sq", [128, 4 * TT], BF16)
        rstd_t = sb("rstd", [128, 2 * TT], F32)
        pvec = sb("pvec", [128, 40], F32)
        ones_bf = sb("ones_bf", [128, 128], BF16)
        bd_bf = sb("bd_bf", [128, 128], BF16)
        ident_bf = sb("ident_bf", [128, 128], BF16)
        mneg_bf = sb("mneg_bf", [128, 128], BF16)
        zero_bf = sb("zero_bf", [128, 128], BF16)
        U_f = sb("U_f", [128, 128], F32)
        ones_f = sb("ones_f", [128, 128], F32)
        H_f = sb("H_f", [128, 128], F32)
        bfb = sb("bfb", [128, 128], F32)
        poolw = sb("poolw", [128, 512], BF16)
        invc = sb("invc", [128, 16], F32)
        gq8 = sb("gq8", [128, 1], F32)

        ps = [es.enter_context(nc.psum_tensor(f"ps{b}", [128, 512], F32)) for b in range(8)]

        def xs(kc, tt):
            return x_sb[:, kc * S + tt * TT: kc * S + (tt + 1) * TT]

        def hs(kc, tt):
            return h_sb[:, kc * S + tt * TT: kc * S + (tt + 1) * TT]

        def hs_tok(kc, t0, n):
            return h_sb[:, kc * S + t0: kc * S + t0 + n]

        V = nc.vector
        G = nc.gpsimd
        tr.dma("sp", "cst0", [], ["c0"], lambda e: e.dma_start(out=pvec[:], in_=pvec_d))
        tr.dma("sp", "cst2", [], ["c2"], lambda e: e.dma_start(out=bfb[:], in_=bfb_d))
        tr.dma("pool", "cstp", [], ["c3"], lambda e: e.dma_start(out=poolw[:], in_=poolw_d))
        tr.op("dve", [], ["k0"], lambda e: e.memset(ones_bf[:], 1.0))
        tr.op("dve", [], ["k1"], lambda e: e.memset(ones_f[:], 1.0))
        tr.op("dve", [], ["k2"], lambda e: e.memset(bd_bf[:], 0.0))
        tr.op("dve", ["k2"], ["k2"], lambda e: e.memset(bd_bf[0:64, 0:64], 1.0))
        tr.op("dve", ["k2"], ["k2"], lambda e: e.memset(bd_bf[64:128, 64:128], 1.0))
        tr.op("dve", [], ["k3"], lambda e: e.memset(H_f[:], 0.0))
        tr.op("dve", ["k3"], ["k3"], lambda e: e.memset(H_f[0:64, :], 1.0))
        tr.op("dve", [], ["k11"], lambda e: e.memset(zero_bf[:], 0.0))
        tr.op("pool", ["k0"], ["k4"], lambda e: e.affine_select(
            out=ident_bf[:], in_=ones_bf[:], pattern=[[-1, 128]], compare_op=ALU.is_equal,
            fill=0.0, base=0, channel_multiplier=1))
        tr.op("pool", ["k11"], ["k5"], lambda e: e.affine_select(
            out=mneg_bf[:], in_=zero_bf[:], pattern=[[1, 128]], compare_op=ALU.is_ge,
            fill=-30000.0, base=0, channel_multiplier=-1))
        tr.op("pool", ["k1"], ["k6"], lambda e: e.affine_select(
            out=U_f[:], in_=ones_f[:], pattern=[[1, 128]], compare_op=ALU.is_ge,
            fill=0.0, base=0, channel_multiplier=-1))
        tr.op("pool", [], ["k7"], lambda e: e.iota(
            invc[:], pattern=[[1, 16]], base=1, channel_multiplier=0,
            allow_small_or_imprecise_dtypes=True))
        tr.op("dve", ["k7"], ["k7"], lambda e: e.reciprocal(out=invc[:], in_=invc[:]))
        tr.op("dve", ["c0"], ["k8"], lambda e: e.tensor_scalar(
            out=gq8[:], in0=pvec[:, 32:33], scalar1=0.125, scalar2=None, op0=ALU.mult))
        tr.barrier()

        itemsA, itemsB = [], []
        for s in range(NSEQ):
            for f in range(2):
                if f == 1:
                    for i in range(6):
                        itemsA.append(win[i])
                    for i in range(4):
                        itemsA.append(wout[i])
                for ffc in range(22):
                    itemsA.append(wgu[f][ffc])
                for i in range(16):
                    itemsB.append(wd[f][i])
        ringA = Ring(tr, "rA", rA_t, 2, 2048, itemsA)
        ringB = Ring(tr, "rB", rB_t, 2, FPH * 128, itemsB)

        cnt = {"sq": 0, "rstd": 0, "ab": 0, "bb": 0, "sg": 0, "pb": 0, "mb": 0, "yb": 0, "pl": 0}

        def nxt(k, mod):
            v = cnt[k] % mod
            cnt[k] += 1
            return v

        def rmsnorm_x(gcol, tiles=range(NTT)):
            for tt in tiles:
                for kc in range(NKC):
                    sl = nxt("sq", 4)
                    sqs = sq_t[:, sl * TT:(sl + 1) * TT]
                    tr.op("act", [("x", kc, tt)], [("sq", sl)],
                          lambda e, sqs=sqs, kc=kc, tt=tt: e.activation(
                              out=sqs, in_=xs(kc, tt), func=AF.Square))
                    tr.op("pe", [("sq", sl)], [("ps", 6)],
                          lambda e, sqs=sqs, kc=kc: e.matmul(
                              ps[6][:], lhsT=ones_bf[:], rhs=sqs,
                              start=(kc == 0), stop=(kc == NKC - 1)))
                r = nxt("rstd", 2)
                rs_ = rstd_t[:, r * TT:(r + 1) * TT]
                tr.op("act", [("ps", 6)], [("rstd", r)],
                      lambda e, rs_=rs_: e.activation(out=rs_, in_=ps[6][:], func=AF.Ln,
                                                      scale=1.0 / D, bias=eps_t[:, 0:1]))
                tr.op("act", [("rstd", r)], [("rstd", r)],
                      lambda e, rs_=rs_: e.activation(out=rs_, in_=rs_, func=AF.Exp, scale=-0.5))
                for kc in range(NKC):
                    tr.op("dve", [("x", kc, tt), ("rstd", r)], [("h", kc, tt)],
                          lambda e, kc=kc, tt=tt, rs_=rs_: e.scalar_tensor_tensor(
                              out=hs(kc, tt), in0=xs(kc, tt),
                              scalar=pvec[:, gcol + kc: gcol + kc + 1], in1=rs_,
                              op0=ALU.mult, op1=ALU.mult))

        eps_t = sb("eps_t", [128, 1], F32)
        tr.op("dve", [], ["k9"], lambda e: e.memset(eps_t[:], EPS))
        one_t = sb("one_t", [128, 1], F32)
        tr.op("dve", [], ["k10"], lambda e: e.memset(one_t[:], 1.0))
        tr.barrier()

        def ffn(gcol, after_chunk=None, prenormed=False):
            uid[0] += 1
            with nc.sbuf_tensor(f"act_t{uid[0]}", [128, FPH * S], BF16) as act_t, \
                    nc.sbuf_tensor(f"sg_t{uid[0]}", [128, 2 * TT], F32) as sg_t:
                def acts(fl, tt):
                    return act_t[:, fl * S + tt * TT: fl * S + (tt + 1) * TT]

                ringB.prefetch()
                def stage_a(fl, tt, wap, wkey):
                    pb = nxt("ab", 2) * 2
                    hreads = [("h", kc, tt) for kc in range(NKC)]

                    def emit_mm(e, off, bank):
                        ins = None
                        for kc in range(NKC):
                            ins = e.matmul(
                                ps[bank][:], lhsT=wap[:, off + kc * 128: off + (kc + 1) * 128],
                                rhs=hs(kc, tt), start=(kc == 0), stop=(kc == NKC - 1))
                        return ins
                    tr.op("pe", [wkey] + hreads, [("ps", pb)], lambda e: emit_mm(e, 0, pb))
                    tr.op("pe", [wkey] + hreads, [("ps", pb + 1)], lambda e: emit_mm(e, 1024, pb + 1))
                    s_ = nxt("sg", 2)
                    sgs = sg_t[:, s_ * TT:(s_ + 1) * TT]
                    tr.op("act", [("ps", pb)], [("sg", s_)],
                          lambda e: e.activation(out=sgs, in_=ps[pb][:], func=AF.Silu))
                    tr.op("dve", [("sg", s_), ("ps", pb + 1)], [("act", fl, tt)],
                          lambda e: e.tensor_tensor(
                              out=acts(fl, tt), in0=sgs, in1=ps[pb + 1][:], op=ALU.mult))

                if not prenormed:
                    rmsnorm_x(gcol)
                for half in range(2):
                    for fl in range(FPH):
                        wap, wkey = ringA.acquire()
                        for tt in range(NTT):
                            stage_a(fl, tt, wap, wkey)
                        ringA.release()
                    for dc in range(NKC):
                        wdap, wdkey = ringB.acquire()
                        for tt in range(NTT):
                            pb = 4 + nxt("bb", 2)

                            def emit_d(e, pb=pb, tt=tt, dc=dc, wdap=wdap):
                                ins = None
                                for fl in range(FPH):
                                    ins = e.matmul(
                                        ps[pb][:], lhsT=wdap[:, fl * 128:(fl + 1) * 128],
                                        rhs=acts(fl, tt), start=(fl == 0), stop=(fl == FPH - 1))
                                return ins
                            tr.op("pe", [wdkey] + [("act", fl, tt) for fl in range(FPH)],
                                  [("ps", pb)], emit_d)
                            tr.op("dve", [("ps", pb), ("x", dc, tt)], [("x", dc, tt)],
                                  lambda e, pb=pb, dc=dc, tt=tt: e.scalar_tensor_tensor(
                                      out=xs(dc, tt), in0=ps[pb][:], scalar=0.5, in1=xs(dc, tt),
                                      op0=ALU.mult, op1=ALU.add))
                        ringB.release()
                        if half == 1 and after_chunk is not None:
                            after_chunk(dc)
                tr.barrier()

        def mixer():
            mes = contextlib.ExitStack()
            uid[0] += 1
            with mes:
                def msb(name, shape, dt):
                    return mes.enter_context(nc.sbuf_tensor(f"m{uid[0]}_{name}", shape, dt))
                qn = msb("qn", [128, 4 * S], BF16)
                kn = msb("kn", [128, 4 * S], BF16)
                Vt = msb("Vt", [128, 16 * 768], BF16)
                ovl = msb("ovl", [128, 2080], F32)
                ovl2 = msb("ovl2", [128, 2112], F32)
                BT = msb("BT", [128, 2 * 128], F32)
                PT = msb("PT", [128, 3 * TT], BF16)
                pooled = PT
                fzp = msb("fzp", [128, 2 * 128], F32)
                rc = msb("rc", [128, 2 * TT], F32)
                halo = msb("halo", [128, 64], F32)
                qz = msb("qz", [128, 2 * 1024], BF16)
                fix = msb("fix", [128, 16], F32)

                fence_t = msb("fence_t", [128, 2], F32)

                def alias_fence(old_keys, new_keys):
                    tr.op("dve", list(old_keys) + ["fence"], list(old_keys) + list(new_keys) + ["fence"],
                          lambda e: e.memset(fence_t[:, 0:1], 0.0))

                winv = ovl[:, :].bitcast(BF16)
                mixed = ovl
                pvg = ovl2
                V5 = Vt[:, :].rearrange("p (j c s e) -> p j c s e", j=16, c=4, s=3, e=64)

                def og(il):
                    return ovl2[:, il * 512:(il + 1) * 512]

                for kc in range(NKC):
                    tr.dma("pool", "winv", [], ["winv"],
                           lambda g, kc=kc: g.dma_start(out=winv[:, kc * 520:(kc + 1) * 520],
                                                        in_=winv_d[:, kc, :]))
                tr.op("dve", [], ["Vones"], lambda e: e.memset(V5[:, :, :, 1, :], 1.0))

                rmsnorm_x(8)

                def qk_stage_b(pb, sl, dst_t, dkey, gap, c, tt):
                    sqs = sq_t[:, sl * TT:(sl + 1) * TT]
                    mb = 4 + nxt("mb", 2)
                    tr.op("pe", [("sq", sl)], [("ps", mb)],
                          lambda e: e.matmul(ps[mb][:], lhsT=bd_bf[:], rhs=sqs, start=True, stop=True))
                    r = nxt("rstd", 2)
                    rs_ = rstd_t[:, r * TT:(r + 1) * TT]
                    tr.op("act", [("ps", mb)], [("rstd", r)],
                          lambda e: e.activation(out=rs_, in_=ps[mb][:], func=AF.Ln, scale=1.0 / 64,
                                                 bias=eps_t[:, 0:1]))
                    tr.op("act", [("rstd", r)], [("rstd", r)],
                          lambda e: e.activation(out=rs_, in_=rs_, func=AF.Exp, scale=-0.5))
                    tr.op("dve", [("ps", pb), ("rstd", r)], [(dkey, c, tt)],
                          lambda e: e.scalar_tensor_tensor(
                              out=dst_t[:, c * S + tt * TT: c * S + (tt + 1) * TT],
                              in0=ps[pb][:], scalar=gap, in1=rs_, op0=ALU.mult, op1=ALU.mult))

                pend = None
                for it in range(4):
                    wap, wkey = ringA.acquire()
                    for c2 in range(2):
                        c = (it % 2) * 2 + c2
                        isq = it < 2
                        dst_t = qn if isq else kn
                        dkey = "qn" if isq else "kn"
                        gap = gq8[:, 0:1] if isq else pvec[:, 33:34]
                        for tt in range(NTT):
                            pb = nxt("pb", 4)

                            def emit_p(e, pb=pb, tt=tt, c2=c2, wap=wap):
                                ins = None
                                for kc in range(NKC):
                                    ins = e.matmul(
                                        ps[pb][:],
                                        lhsT=wap[:, c2 * 1024 + kc * 128: c2 * 1024 + (kc + 1) * 128],
                                        rhs=hs(kc, tt), start=(kc == 0), stop=(kc == NKC - 1))
                                return ins
                            tr.op("pe", [wkey] + [("h", kc, tt) for kc in range(NKC)],
                                  [("ps", pb)], emit_p)
                            sl = nxt("sq", 4)
                            sqs = sq_t[:, sl * TT:(sl + 1) * TT]
                            tr.op("act", [("ps", pb)], [("sq", sl)],
                                  lambda e, sqs=sqs, pb=pb: e.activation(
                                      out=sqs, in_=ps[pb][:], func=AF.Square))
                            if pend is not None:
                                qk_stage_b(*pend)
                            pend = (pb, sl, dst_t, dkey, gap, c, tt)
                    ringA.release()
                qk_stage_b(*pend)

                for j in range(16):
                    pb = nxt("pb", 4)

                    def emit_v(e, pb=pb, j=j):
                        ins = None
                        for kc in range(NKC):
                            ins = e.matmul(ps[pb][:], lhsT=hs_tok(kc, j * 128, 128),
                                           rhs=winv[:, kc * 520: kc * 520 + 512],
                                           start=(kc == 0), stop=(kc == NKC - 1))
                        return ins
                    hr = [("h", kc, j // 4) for kc in range(NKC)]
                    tr.op("pe", ["winv"] + hr, [("ps", pb)], emit_v)

                    def emit_f(e, j=j):
                        ins = None
                        for kc in range(NKC):
                            ins = e.matmul(ps[6][:, j * 8:(j + 1) * 8],
                                           lhsT=hs_tok(kc, j * 128, 128),
                                           rhs=winv[:, kc * 520 + 512: kc * 520 + 520],
                                           start=(kc == 0), stop=(kc == NKC - 1))
                        return ins
                    tr.op("pe", ["winv"] + hr, [("ps", 6)], emit_f)
                    for s_ in range(2):
                        tr.op("dve", [("ps", pb), "Vones"], [("V", j)],
                              lambda e, pb=pb, j=j, s_=s_: e.tensor_copy(
                                  out=V5[:, j, :, 2 * s_, :],
                                  in_=ps[pb][:].rearrange("p (c s e) -> p c s e", c=4, s=2)[:, :, s_, :]))

                def fzs(k):
                    if k in (1, 2):
                        return fzp[:, (k - 1) * 128: k * 128]
                    kk = {0: 0, 3: 1, 4: 2, 5: 3, 6: 4}[k]
                    return rc[:, kk * 128:(kk + 1) * 128]
                tr.op("dve", [("ps", 6)], [("fz", 0)], lambda e: e.tensor_tensor(
                    out=fzs(0), in0=ps[6][:, 0:128], in1=bfb[:], op=ALU.add))
                tr.op("dve", [("fz", 0)], [("fz", 1)], lambda e: e.tensor_scalar(
                    out=fzs(1), in0=fzs(0), scalar1=-1.0, scalar2=None, op0=ALU.mult))
                tr.op("dve", [("fz", 0), ("fz", 1)], [("fz", 1)], lambda e: e.tensor_tensor(
                    out=fzs(1), in0=fzs(1), in1=fzs(0), op=ALU.max))
                tr.op("act", [("fz", 1)], [("fz", 2)], lambda e: e.activation(
                    out=fzs(2), in_=fzs(1), func=AF.Exp, scale=-1.0))
                tr.op("act", [("fz", 2)], [("fz", 2)], lambda e: e.activation(
                    out=fzs(2), in_=fzs(2), func=AF.Ln, bias=one_t[:, 0:1], scale=1.0))
                tr.op("dve", [("fz", 0)], [("fz", 3)], lambda e: e.tensor_scalar_min(
                    out=fzs(3), in0=fzs(0), scalar1=0.0))
                tr.op("dve", [("fz", 3), ("fz", 2)], [("fz", 3)], lambda e: e.tensor_sub(
                    out=fzs(3), in0=fzs(3), in1=fzs(2)))
                tr.op("pe", [("fz", 3)], [("ps", 6)], lambda e: e.matmul(
                    ps[6][:, 0:128], lhsT=U_f[:], rhs=fzs(3), start=True, stop=True))
                tr.op("pe", [("fz", 3)], [("ps", 6)], lambda e: e.matmul(
                    ps[6][:, 128:256], lhsT=ones_f[:], rhs=fzs(3), start=True, stop=True))
                tr.op("pe", [("fz", 3)], [("ps", 6)], lambda e: e.matmul(
                    ps[6][:, 256:384], lhsT=H_f[:], rhs=fzs(3), start=True, stop=True))
                tr.op("dve", [("ps", 6)], [("fz", 4), ("fz", 5), ("fz", 6)],
                      lambda e: e.tensor_copy(out=rc[:, 256:640], in_=ps[6][:, 0:384]))
                tr.op("dve", [("fz", 0)], [("fz", 0)], lambda e: e.memset(rc[:, 0:8], 0.0))
                for j in range(1, 16):
                    tr.op("dve", [("fz", 0), ("fz", 5)], [("fz", 0)],
                          lambda e, j=j: e.tensor_tensor(
                              out=rc[:, j * 8:(j + 1) * 8], in0=rc[:, (j - 1) * 8: j * 8],
                              in1=rc[:, 384 + (j - 1) * 8: 384 + j * 8], op=ALU.add))
                tr.op("dve", [("fz", 4), ("fz", 0)], [("fz", 1)],
                      lambda e: e.scalar_tensor_tensor(
                          out=fzs(1), in0=fzs(4), scalar=-1.0, in1=fzs(0),
                          op0=ALU.mult, op1=ALU.subtract))
                tr.op("dve", [("fz", 0)], [("fz", 2)], lambda e: e.tensor_copy(
                    out=fzs(2), in_=fzs(0)))
                negF3 = fzs(1).rearrange("p (j h) -> p j h", h=8)
                Fref3 = fzs(2).rearrange("p (j h) -> p j h", h=8)

                wp0, wk0 = ringA.acquire()
                wp1, wk1 = ringA.acquire()
                alias_fence([("fz", k) for k in (0, 3, 4, 5, 6)], ["tmpP0"])
                tr.op("dve", ["winv", "fence"], ["winv", "fence"] + [("mixed", g) for g in range(4)],
                      lambda e: e.memset(fence_t[:, 0:1], 0.0))
                SCAN_ENG = {0: "dve", 1: "dve", 2: "pool", 3: "pool"}

                def pool_stage_a(n):
                    tt, g = divmod(n, 4)
                    w = 2 << g
                    wap, wkey = (wp0, wk0) if g < 2 else (wp1, wk1)
                    off = (g % 2) * 1024
                    pb = nxt("pb", 4)

                    def emit_pv(e):
                        ins = None
                        for kc in range(NKC):
                            ins = e.matmul(
                                ps[pb][:], lhsT=wap[:, off + kc * 128: off + (kc + 1) * 128],
                                rhs=hs(kc, tt), start=(kc == 0), stop=(kc == NKC - 1))
                        return ins
                    tr.op("pe", [wkey] + [("h", kc, tt) for kc in range(NKC)], [("ps", pb)], emit_pv)
                    sl = n % 2
                    pv = pvg[:, sl * 528:(sl + 1) * 528]
                    tr.op("act", [("ps", pb)], [("pvg", sl)],
                          lambda e: e.copy(out=pv[:, 16:528], in_=ps[pb][:]))

                qzf = qz[:, :].bitcast(F32)

                def pool_stage_a2(n):
                    tt, g = divmod(n, 4)
                    w = 2 << g
                    sl = n % 2
                    pv = pvg[:, sl * 528:(sl + 1) * 528]
                    en = "pool" if g == 3 else "dve"
                    if g == 3:
                        tbuf = [rc[:, 0:528], qzf[:, 0:528]]
                        tkey = ["tmpP0", "tmpP1"]
                    else:
                        tbuf = [ovl2[:, 1056:1584], ovl2[:, 1584:2112]]
                        tkey = [("tmp", 0), ("tmp", 1)]
                    if tt == 0:
                        tr.op(en, [], [("pvgh", sl)], lambda e: e.memset(pv[:, 0:16], 0.0))
                    else:
                        tr.op(en, [("halo", g)], [("pvgh", sl)],
                              lambda e: e.tensor_copy(out=pv[:, 0:16], in_=halo[:, g * 16:(g + 1) * 16]))
                    tr.op("pool", [("pvg", sl)], [("halo", g)],
                          lambda e: e.tensor_copy(out=halo[:, g * 16:(g + 1) * 16], in_=pv[:, 512:528]))
                    src, skeys = pv, [("pvg", sl), ("pvgh", sl)]
                    lo, k, ti = 0, 1, 0
                    while k < w:
                        dst = tbuf[ti]
                        tr.op(en, skeys, [tkey[ti]],
                              lambda e, dst=dst, src=src, lo=lo, k=k: e.tensor_tensor(
                                  out=dst[:, lo + k:528], in0=src[:, lo + k:528],
                                  in1=src[:, lo:528 - k], op=ALU.add))
                        src, skeys = dst, [tkey[ti]]
                        lo += k
                        k *= 2
                        ti ^= 1
                    psl = n % 2
                    pl = pooled[:, psl * TT:(psl + 1) * TT]
                    if g == 3:
                        oth, okey = tbuf[ti], tkey[ti]
                        tr.op(en, skeys, [okey], lambda e: e.tensor_scalar(
                            out=oth[:, 16:528], in0=src[:, 16:528], scalar1=1.0 / w, scalar2=0.0,
                            op0=ALU.mult, op1=ALU.add))
                        tr.op(en, [okey, ("pvg", sl)], [("pooled", psl)], lambda e: e.tensor_tensor(
                            out=pl, in0=oth[:, 16:528], in1=pv[:, 16:528], op=ALU.subtract))
                        if tt == 0:
                            tr.op(en, skeys, skeys, lambda e: e.tensor_tensor(
                                out=src[:, 0:w - 1], in0=src[:, 16:16 + w - 1],
                                in1=invc[:, 0:w - 1], op=ALU.mult))
                            tr.op(en, skeys + [("pvg", sl), ("pooled", psl)], [("pooled", psl)],
                                  lambda e: e.tensor_tensor(
                                      out=pl[:, 0:w - 1], in0=src[:, 0:w - 1],
                                      in1=pv[:, 16:16 + w - 1], op=ALU.subtract))
                        return
                    tr.op("dve", skeys + [("pvg", sl)], [("pooled", psl)],
                          lambda e: e.scalar_tensor_tensor(
                              out=pl, in0=src[:, 16:528], scalar=1.0 / w, in1=pv[:, 16:528],
                              op0=ALU.mult, op1=ALU.subtract))
                    if tt == 0:
                        tr.op("dve", skeys, ["fix"],
                              lambda e: e.tensor_tensor(
                                  out=fix[:, 0:w - 1], in0=src[:, 16:16 + w - 1],
                                  in1=invc[:, 0:w - 1], op=ALU.mult))
                        tr.op("dve", ["fix", ("pvg", sl), ("pooled", psl)], [("pooled", psl)],
                              lambda e: e.tensor_tensor(
                                  out=pl[:, 0:w - 1], in0=fix[:, 0:w - 1],
                                  in1=pv[:, 16:16 + w - 1], op=ALU.subtract))

                sqslot = {}

                def pool_stage_b(n):
                    tt, g = divmod(n, 4)
                    psl = n % 2
                    pl = pooled[:, psl * TT:(psl + 1) * TT]
                    mb = 4 + nxt("mb", 2)
                    tr.op("pe", [("pooled", psl)], [("ps", mb)],
                          lambda e: e.matmul(ps[mb][:], lhsT=poolw[:, g * 128:(g + 1) * 128], rhs=pl,
                                             start=True, stop=True))
                    mx = mixed[:, g * TT:(g + 1) * TT]
                    tr.op("act", [("ps", mb)], [("mixed", g)],
                          lambda e: e.activation(out=mx, in_=ps[mb][:], func=AF.Copy,
                                                 scale=pvec[:, 24 + g: 25 + g]))
                    sl2 = nxt("sq", 4)
                    sqslot[n] = sl2
                    sqs = sq_t[:, sl2 * TT:(sl2 + 1) * TT]
                    tr.op("act", [("ps", mb)], [("sq", sl2)],
                          lambda e: e.activation(out=sqs, in_=ps[mb][:], func=AF.Square,
                                                 scale=pvec[:, 24 + g: 25 + g]))

                def pool_stage_b2(n):
                    tt, g = divmod(n, 4)
                    sl2 = sqslot[n]
                    sqs = sq_t[:, sl2 * TT:(sl2 + 1) * TT]
                    tr.op("pe", [("sq", sl2)], [("ps", 6)],
                          lambda e: e.matmul(ps[6][:], lhsT=ones_bf[:], rhs=sqs,
                                             start=(g == 0), stop=(g == 3)))
                    if g != 3:
                        return
                    r = nxt("rstd", 2)
                    rs_ = rstd_t[:, r * TT:(r + 1) * TT]
                    tr.op("act", [("ps", 6)], [("rstd", r)],
                          lambda e: e.activation(out=rs_, in_=ps[6][:], func=AF.Ln, scale=1.0 / 512,
                                                 bias=eps_t[:, 0:1]))
                    tr.op("act", [("rstd", r)], [("rstd", r)],
                          lambda e: e.activation(out=rs_, in_=rs_, func=AF.Exp, scale=-0.5))
                    for g_ in range(4):
                        tr.op("dve", [("mixed", g_), ("rstd", r)], [("h", g_, tt)],
                              lambda e, g_=g_: e.scalar_tensor_tensor(
                                  out=hs(g_, tt), in0=mixed[:, g_ * TT:(g_ + 1) * TT],
                                  scalar=pvec[:, 28 + g_: 29 + g_], in1=rs_,
                                  op0=ALU.mult, op1=ALU.mult))

                for n in range(19):
                    if n < 16:
                        pool_stage_a(n)
                    if 0 <= n - 1 < 16:
                        pool_stage_a2(n - 1)
                    if 0 <= n - 3 < 16:
                        pool_stage_b2(n - 3)
                    if 0 <= n - 2 < 16:
                        pool_stage_b(n - 2)
                ringA.release()
                ringA.release()

                alias_fence([("pvg", 0), ("pvg", 1), ("pvgh", 0), ("pvgh", 1), ("tmp", 0), ("tmp", 1)],
                            [("on", c_, s_) for c_ in range(4) for s_ in range(2)])

                alias_fence([("pooled", 0), ("pooled", 1)],
                            [("PT", p_, s_) for p_ in range(3) for s_ in range(2)])
                alias_fence([("fz", k) for k in (0, 3, 4, 5, 6)] + ["tmpP0"],
                            [("rc", r_, s_) for r_ in range(2) for s_ in range(2)])
                tr.op("pool", ["tmpP1"], ["tmpP1", ("qz", 0, 0), ("qz", 0, 1), ("qz", 1, 0), ("qz", 1, 1)],
                      lambda e: e.memset(qz[:], 0.0))
                steps = []
                gcc = 0
                for g in range(4):
                    for c in range(4):
                        par = gcc % 2
                        gcc += 1
                        first_step = len(steps)
                        for j in range(4 * g + 4):
                            for hq in range(2):
                                i0 = 4 * g + 2 * hq
                                i1 = i0 + 1
                                if j > i1:
                                    continue
                                lo = max(i0, j)
                                steps.append(dict(g=g, c=c, par=par, j=j, i1=i1, lo=lo, hq=hq,
                                                  nb=i1 - lo + 1, qoff=(lo - 4 * g) * 128,
                                                  pre=False, post=False, gpost=False, first=False))
                        steps[first_step]["pre"] = True
                        steps[first_step]["first"] = True
                        steps[-1]["post"] = True
                        steps[-1]["gpost"] = (c == 3)

                def emit_pre(st):
                    g, c, par = st["g"], st["c"], st["par"]
                    for s_ in range(2):
                        p0 = s_ * 64
                        tr.op("pool", [("qn", c, g)], [("qz", par, s_)],
                              lambda e, p0=p0, s_=s_: e.tensor_copy(
                                  out=qz[p0:p0 + 64, par * 1024 + s_ * 512: par * 1024 + (s_ + 1) * 512],
                                  in_=qn[p0:p0 + 64, c * S + g * TT: c * S + (g + 1) * TT]))
                    for s_ in range(2):
                        h = 2 * c + s_
                        for hq in range(2):
                            i1 = 4 * g + 2 * hq + 1
                            bo = par * 64 + s_ * 32 + hq * 16
                            tr.op("pool", [("fz", 1), ("fz", 2)], [("BT", par, s_, hq)],
                                  lambda e, i1=i1, h=h, bo=bo: e.tensor_scalar(
                                      out=BT[:, bo: bo + i1 + 1],
                                      in0=negF3[:, 0:i1 + 1, h], scalar1=1.0,
                                      scalar2=Fref3[:, i1, h:h + 1],
                                      op0=ALU.mult, op1=ALU.add))

                def emit_qk(n, st):
                    c, par, j, qoff, ncol = st["c"], st["par"], st["j"], st["qoff"], st["nb"] * 128
                    sbk = n % 3
                    qz3 = qz[:, par * 1024:(par + 1) * 1024].rearrange("p (s t) -> p s t", s=2)
                    diag = (st["lo"] == j)

                    def emit_s(e):
                        ins = e.matmul(
                            ps[sbk][:, 0:2 * ncol],
                            lhsT=kn[:, c * S + j * 128: c * S + (j + 1) * 128],
                            rhs=qz3[:, :, qoff:qoff + ncol], start=True, stop=not diag,
                            skip_group_check=True)
                        if diag:
                            for s_ in range(2):
                                ins = e.matmul(
                                    ps[sbk][:, s_ * ncol: s_ * ncol + 128], lhsT=ident_bf[:],
                                    rhs=mneg_bf[:], start=False, stop=(s_ == 1),
                                    skip_group_check=True)
                        return ins
                    tr.op("pe", [("kn", c, j // 4), ("qz", par, 0), ("qz", par, 1)], [("ps", sbk)],
                          emit_s)

                def emit_exp(n, st):
                    g, par, j, lo, nb, hq = st["g"], st["par"], st["j"], st["lo"], st["nb"], st["hq"]
                    ncol = nb * 128
                    sbk = psl = n % 3
                    for s_ in range(2):
                        co = s_ * ncol
                        pts = PT[:, psl * TT + co: psl * TT + co + ncol]
                        bo = par * 64 + s_ * 32 + hq * 16 + j
                        tr.op("act", [("ps", sbk), ("BT", par, s_, hq)], [("PT", psl, s_)],
                              lambda e, pts=pts, co=co, bo=bo: e.activation(
                                  out=pts, in_=ps[sbk][:, co:co + ncol], func=AF.Exp,
                                  bias=BT[:, bo:bo + 1], scale=1.0))

                def emit_pv(n, st):
                    c, par, j, qoff, nb = st["c"], st["par"], st["j"], st["qoff"], st["nb"]
                    ncol = nb * 128
                    psl = n % 3
                    TA = 3 + 2 * par
                    for s_ in range(2):
                        Tb = TA + s_
                        tr.op("pe", [("PT", psl, s_), ("V", j)],
                              [("ps", Tb)],
                              lambda e, Tb=Tb, s_=s_: e.matmul(
                                  ps[Tb][:, qoff:qoff + ncol],
                                  lhsT=V5[:, j, c, s_:s_ + 2, :],
                                  rhs=PT[:, psl * TT + s_ * ncol: psl * TT + (s_ + 1) * ncol],
                                  start=st["first"], stop=(j == st["i1"]), skip_group_check=True))

                def emit_post(st):
                    g, c, par = st["g"], st["c"], st["par"]
                    assert not gpend, (g, c, gpend)
                    TA = 3 + 2 * par
                    TB = TA + 1
                    rcA = rc[0:64, par * TT:(par + 1) * TT]
                    rcB = rc[64:128, par * TT:(par + 1) * TT]
                    onc = ovl2[:, c * TT:(c + 1) * TT]
                    tr.op("dve", [("ps", TA)], [("rc", par, 0)],
                          lambda e: e.reciprocal(out=rcA, in_=ps[TA][64:128, :]))
                    tr.op("dve", [("ps", TB)], [("rc", par, 1)],
                          lambda e: e.reciprocal(out=rcB, in_=ps[TB][0:64, :]))
                    tr.op("dve", [("ps", TA), ("rc", par, 0)], [("on", c, 0)],
                          lambda e: e.tensor_tensor(
                              out=onc[0:64, :], in0=ps[TA][0:64, :], in1=rcA, op=ALU.mult))
                    tr.op("dve", [("ps", TB), ("rc", par, 1)], [("on", c, 1)],
                          lambda e: e.tensor_tensor(
                              out=onc[64:128, :], in0=ps[TB][64:128, :], in1=rcB, op=ALU.mult))
                    if st["gpost"]:
                        gpend.append(g)

                def emit_gpost(g):
                    for c_ in range(4):
                        onc_ = ovl2[:, c_ * TT:(c_ + 1) * TT]
                        sl = nxt("sq", 4)
                        sqs = sq_t[:, sl * TT:(sl + 1) * TT]
                        tr.op("pool", [("on", c_, 0), ("on", c_, 1)], [("sq", sl)],
                              lambda e, sqs=sqs, onc_=onc_: e.tensor_tensor(
                                  out=sqs, in0=onc_, in1=onc_, op=ALU.mult))
                        tr.op("pe", [("sq", sl)], [("ps", 7)],
                              lambda e, sqs=sqs, c_=c_: e.matmul(
                                  ps[7][:], lhsT=ones_bf[:], rhs=sqs, start=(c_ == 0), stop=(c_ == 3)))
                    r = nxt("rstd", 2)
                    rs_ = rstd_t[:, r * TT:(r + 1) * TT]
                    tr.op("act", [("ps", 7)], [("rstd", r)],
                          lambda e: e.activation(
                              out=rs_, in_=ps[7][:], func=AF.Ln, scale=1.0 / 512, bias=eps_t[:, 0:1]))
                    tr.op("act", [("rstd", r)], [("rstd", r)],
                          lambda e: e.activation(out=rs_, in_=rs_, func=AF.Exp, scale=-0.5))
                    for c_ in range(4):
                        onc_ = ovl2[:, c_ * TT:(c_ + 1) * TT]
                        tr.op("dve", [("on", c_, 0), ("on", c_, 1), ("rstd", r)], [("h", 4 + c_, g)],
                              lambda e, onc_=onc_, c_=c_: e.scalar_tensor_tensor(
                                  out=hs(4 + c_, g), in0=onc_, scalar=pvec[:, 34 + c_: 35 + c_],
                                  in1=rs_, op0=ALU.mult, op1=ALU.mult))

                NS = len(steps)
                gpend = []
                gdue = {}
                GDEFER = 12
                pres = [n for n in range(NS) if steps[n]["pre"]]
                emit_pre(steps[pres[0]])
                npre = 1
                for n in range(NS + 2):
                    if n < NS:
                        emit_qk(n, steps[n])
                    if 0 <= n - 1 < NS:
                        emit_exp(n - 1, steps[n - 1])
                    if 0 <= n - 2 < NS:
                        emit_pv(n - 2, steps[n - 2])
                        if steps[n - 2]["post"]:
                            emit_post(steps[n - 2])
                            for g_ in gpend:
                                gdue.setdefault(g_, n + GDEFER)
                    for g_ in list(gpend):
                        if n >= gdue[g_]:
                            emit_gpost(g_)
                            gpend.remove(g_)
                    if n < NS and steps[n]["pre"] and npre < len(pres):
                        emit_pre(steps[pres[npre]])
                        npre += 1

                for g_ in list(gpend):
                    emit_gpost(g_)
                def wo_step(dc, d2, tt, wap, wkey):
                    pb = nxt("pb", 3)

                    def emit_o(e):
                        ins = None
                        for kc in range(NKC):
                            ins = e.matmul(
                                ps[pb][:],
                                lhsT=wap[:, d2 * 1024 + kc * 128: d2 * 1024 + (kc + 1) * 128],
                                rhs=hs(kc, tt), start=(kc == 0), stop=(kc == NKC - 1))
                        return ins
                    tr.op("pe", [wkey] + [("h", kc, tt) for kc in range(NKC)], [("ps", pb)], emit_o)
                    tr.op("dve", [("ps", pb), ("x", dc, tt)], [("x", dc, tt)],
                          lambda e: e.tensor_tensor(
                              out=xs(dc, tt), in0=ps[pb][:], in1=xs(dc, tt), op=ALU.add))

                for it in range(3):
                    wap, wkey = ringA.acquire()
                    for d2 in range(2):
                        for tt in range(NTT):
                            wo_step(it * 2 + d2, d2, tt, wap, wkey)
                    ringA.release()
                wap, wkey = ringA.acquire()
                for tt in range(NTT):
                    for d2 in range(2):
                        wo_step(6 + d2, d2, tt, wap, wkey)
                    if tt >= 1:
                        rmsnorm_x(16, [tt - 1])
                ringA.release()
                rmsnorm_x(16, [NTT - 1])
                tr.barrier()

        def load_x(s, kc):
            tr.dma("sp", f"xld{kc}", [], [("x", kc, tt) for tt in range(NTT)],
                   lambda e: e.dma_start(out=x_sb[:, kc * S:(kc + 1) * S], in_=xT[s, kc]))

        for tt in range(NTT):
            for kc in range(NKC):
                tr.add_dma_sem(f"xi{kc}_{tt}")
                tr.dma("sp", f"xi{kc}_{tt}", [], [("x", kc, tt)],
                       lambda e, kc=kc, tt=tt: e.dma_start(
                           out=xs(kc, tt), in_=xT[0, kc][:, tt * TT:(tt + 1) * TT]))
        for s in range(NSEQ):
            def after_chunk(kc, s=s):
                tr.dma("sp", f"yst{kc}", [("x", kc, tt) for tt in range(NTT)], [("y", s, kc)],
                       lambda e: e.dma_start(out=yT[s, kc], in_=x_sb[:, kc * S:(kc + 1) * S]))
                if s + 1 < NSEQ:
                    load_x(s + 1, kc)
            ffn(0)
            mixer()
            ffn(16, after_chunk, prenormed=True)
        tr.barrier(["sp"], all_sems=True)
    return nc


def _prep_weights(inp):
    f = lambda a: np.ascontiguousarray(np.asarray(a, dtype=np.float32))

    def lay_gu(Wg, Wu):
        g = f(Wg).reshape(8, 128, 22, 128).transpose(2, 1, 0, 3)
        u = f(Wu).reshape(8, 128, 22, 128).transpose(2, 1, 0, 3)
        return f(np.stack([g, u], axis=2).reshape(22, 128, 2048))

    def lay_d(Wd):
        return f(f(Wd).reshape(2, FPH, 128, 8, 128).transpose(0, 3, 2, 1, 4).reshape(16, 128, FPH * 128))

    w_in = f(inp["w_in"])
    wi = w_in[:, 0:1536].reshape(8, 128, 12, 128).transpose(2, 1, 0, 3)
    order = [4, 5, 6, 7, 8, 9, 10, 11, 0, 1, 2, 3]
    wi = wi[order].reshape(6, 2, 128, 8, 128).transpose(0, 2, 1, 3, 4).reshape(6, 128, 2048)
    winv = w_in[:, 1536:2056].reshape(8, 128, 520).transpose(1, 0, 2)
    wo = f(inp["w_out"]).reshape(8, 128, 8, 128).transpose(2, 1, 0, 3)
    wo = wo.reshape(4, 2, 128, 8, 128).transpose(0, 2, 1, 3, 4).reshape(4, 128, 2048)
    poolw = f(inp["pool_w"]).transpose(1, 0, 2).reshape(128, 512)
    pvec = np.zeros((128, 40), np.float32)
    pvec[:, 0:8] = f(inp["ffn1_norm"]).reshape(8, 128).T
    pvec[:, 8:16] = f(inp["mix_norm"]).reshape(8, 128).T
    pvec[:, 16:24] = f(inp["ffn2_norm"]).reshape(8, 128).T
    pvec[:, 24:28] = f(inp["pool_scale"]).reshape(4, 128).T
    pvec[:, 28:32] = f(inp["out_norm_pool"]).reshape(4, 128).T
    pvec[:, 32] = np.tile(f(inp["q_norm"]), 2)
    pvec[:, 33] = np.tile(f(inp["k_norm"]), 2)
    pvec[:, 34:38] = f(inp["out_norm_attn"]).reshape(4, 128).T
    bfb = np.broadcast_to(np.tile(f(inp["b_forget"]), 16)[None, :], (128, 128))
    return {
        "wgu1": lay_gu(inp["ffn1_w_gate"], inp["ffn1_w_up"]),
        "wgu2": lay_gu(inp["ffn2_w_gate"], inp["ffn2_w_up"]),
        "wd1": lay_d(inp["ffn1_w_down"]), "wd2": lay_d(inp["ffn2_w_down"]),
        "win": f(wi), "winv": f(winv), "wout": f(wo), "poolw": f(poolw),
        "pvec": pvec, "bfb": f(bfb),
    }


_NC_CACHE = {}


def kernel(**inputs):
    x = np.asarray(inputs["x"], dtype=np.float32)
    w = _prep_weights(inputs)
    if "nc" not in _NC_CACHE:
        _NC_CACHE["nc"] = build_nc()
    nc = _NC_CACHE["nc"]
    in_maps = []
    for c in range(NCORES):
        xc = x[c * NSEQ:(c + 1) * NSEQ].transpose(0, 2, 1).reshape(NSEQ, NKC, 128, S)
        m = {"xT": np.ascontiguousarray(xc)}
        m.update(w)
        in_maps.append(m)
    res = run_bass_kernel_spmd(nc, in_maps, core_ids=list(range(NCORES)))
    out = np.empty((NCORES * NSEQ, S, D), np.float32)
    for c in range(NCORES):
        y = np.asarray(res.results[c]["yT"]).reshape(NSEQ, D, S)
        out[c * NSEQ:(c + 1) * NSEQ] = y.transpose(0, 2, 1)
    return out
```
